# Optimizing a Trainium2 kernel written in Bass

```python
import math
import jax, jax.numpy as jnp
from jax import lax
import numpy as np

D_MODEL = 2048
BATCH = 2
SEQ = 16384
DEPTH = 1
DEC_BATCH = 8
DEC_SEQ = 32
PAST_LEN = 2048

CHUNK = 64
Q_BLOCK = 128
NORM_EPS = 1e-5
N_ATT_HEADS = 8
ATT_HEAD_DIM = 128
D_ATT = N_ATT_HEADS * 2 * ATT_HEAD_DIM
N_REL_BUCKETS = 32
REL_MAX_DIST = 128
D_SSM = 2048
SSM_HEAD_DIM = 64
N_SSM_HEADS = D_SSM // SSM_HEAD_DIM
N_SSM_GROUPS = 4
SSM_STATE = 128
CONV_WIDTH = 4
D_CONV = D_SSM + 2 * N_SSM_GROUPS * SSM_STATE
SSD_CHUNK = CHUNK
D_MIX = D_ATT + D_SSM
D_IN_PROJ = 4 * D_ATT + D_SSM + D_CONV + N_SSM_HEADS
SPLITS = [D_ATT, 2 * D_ATT, 3 * D_ATT, 4 * D_ATT, 4 * D_ATT + D_SSM, 4 * D_ATT + D_SSM + D_CONV]

kernel_name = 'hymba_diffattn_ssd_streaming_step'


def rms_norm(x, w):
    xf = x.astype(jnp.float32)
    y = xf * lax.rsqrt(jnp.mean(xf * xf, axis=-1, keepdims=True) + NORM_EPS)
    return (y * w.astype(jnp.float32)).astype(x.dtype)


def lambda_init(layer):
    return 0.8 - 0.6 * math.exp(-0.3 * layer)


def rel_bucket(rel):
    half = N_REL_BUCKETS // 2
    max_exact = half // 2
    ret = jnp.where(rel > 0, half, 0)
    n = jnp.abs(rel)
    nf = jnp.maximum(n, 1).astype(jnp.float32)
    large = max_exact + (jnp.log(nf / max_exact) / math.log(REL_MAX_DIST / max_exact)
                         * (half - max_exact)).astype(jnp.int32)
    large = jnp.minimum(large, half - 1)
    return ret + jnp.where(n < max_exact, n, large)


def diff_attn_block(q, k, v, q_pos, k_pos, rel_table, lam):
    logits = jnp.einsum('bqhmd,bkhmd->bhmqk', q, k,
                        preferred_element_type=jnp.float32) * (ATT_HEAD_DIM ** -0.5)
    bias = rel_table.astype(jnp.float32)[rel_bucket(k_pos[None, :] - q_pos[:, None])]
    bias = jnp.transpose(bias, (2, 0, 1))
    visible = (k_pos[None, :] // CHUNK) <= (q_pos[:, None] // CHUNK)
    logits = jnp.where(visible, logits + bias[None, :, None], -jnp.inf)
    p = jax.nn.softmax(logits, axis=-1)
    w = p[:, :, 0] - lam * p[:, :, 1]
    return jnp.einsum('bhqk,bkhe->bqhe', w.astype(v.dtype), v)


def diff_attention(q, k, v, q_pos, k_pos, rel_table, lam):
    b, s = q.shape[:2]
    if s > Q_BLOCK and s % Q_BLOCK == 0:
        nb = s // Q_BLOCK
        qb = jnp.moveaxis(q.reshape(b, nb, Q_BLOCK, *q.shape[2:]), 1, 0)
        pb = q_pos.reshape(nb, Q_BLOCK)
        out = lax.map(lambda a: diff_attn_block(a[0], k, v, a[1], k_pos, rel_table, lam), (qb, pb))
        return jnp.moveaxis(out, 0, 1).reshape(b, s, *out.shape[3:])
    return diff_attn_block(q, k, v, q_pos, k_pos, rel_table, lam)


def causal_conv(xbc, hist, w, bias):
    xp = jnp.concatenate([hist.astype(xbc.dtype), xbc], axis=1)
    out = lax.conv_general_dilated(xp, w[:, None, :].astype(xbc.dtype), window_strides=(1,),
                                   padding='VALID', dimension_numbers=('NWC', 'WIO', 'NWC'),
                                   feature_group_count=xbc.shape[-1])
    return out + bias.astype(xbc.dtype), xp[:, -(CONV_WIDTH - 1):]


def ssd_scan(x, dt, A, B, C, h0):
    f32 = jnp.float32
    b, s = x.shape[:2]
    L = min(SSD_CHUNK, s)
    nc = s // L
    G, R, P, N = N_SSM_GROUPS, N_SSM_HEADS // N_SSM_GROUPS, SSM_HEAD_DIM, SSM_STATE
    xdt = (x.astype(f32) * dt[..., None]).reshape(b, nc, L, G, R, P)
    a = (dt * A).reshape(b, nc, L, G, R)
    Bc = B.astype(f32).reshape(b, nc, L, G, N)
    Cc = C.astype(f32).reshape(b, nc, L, G, N)
    tri = jnp.tril(jnp.ones((L, L), dtype=bool))[None, :, :, None, None]

    def step(h, inp):
        xdt_c, a_c, B_c, C_c = inp
        a_cs = jnp.cumsum(a_c, axis=1)
        seg = a_cs[:, :, None] - a_cs[:, None, :]
        decay = jnp.exp(jnp.where(tri, seg, -jnp.inf))
        cb = jnp.einsum('btgn,bsgn->btsg', C_c, B_c)
        y = jnp.einsum('btsgr,bsgrp->btgrp', cb[..., None] * decay, xdt_c)
        y = y + jnp.einsum('btgn,bgrpn->btgrp', C_c, h) * jnp.exp(a_cs)[..., None]
        to_end = jnp.exp(a_cs[:, -1:] - a_cs)
        h = h * jnp.exp(a_cs[:, -1])[..., None, None] + jnp.einsum(
            'bsgn,bsgr,bsgrp->bgrpn', B_c, to_end, xdt_c)
        return h, y

    sw = lambda t: jnp.moveaxis(t, 1, 0)
    hN, ys = lax.scan(step, h0.astype(f32).reshape(b, G, R, P, N), (sw(xdt), sw(a), sw(Bc), sw(Cc)))
    y = jnp.moveaxis(ys, 0, 1).reshape(b, s, N_SSM_HEADS, P)
    return y, hN.reshape(b, N_SSM_HEADS, P, N)


def hybrid_layer(h, q_pos, k_past, v_past, conv_past, ssm_past, layer, rel_table,
                 norm_w, w_in, lam_q1, lam_k1, lam_q2, lam_k2, subln_w,
                 conv_w, conv_b, dt_bias, A_log, D_skip, ssm_norm_w, w_out):
    b, s, _ = h.shape
    f32 = jnp.float32
    u = rms_norm(h, norm_w)
    proj = u @ w_in.astype(u.dtype)
    q, k, v, g, z, xbc, dt = jnp.split(proj, SPLITS, axis=-1)
    q = q.reshape(b, s, N_ATT_HEADS, 2, ATT_HEAD_DIM)
    k = k.reshape(b, s, N_ATT_HEADS, 2, ATT_HEAD_DIM)
    v = v.reshape(b, s, N_ATT_HEADS, 2 * ATT_HEAD_DIM)
    if k_past is None:
        k_all, v_all, k_pos = k, v, q_pos
    else:
        k_all = jnp.concatenate([k_past.astype(k.dtype), k], axis=1)
        v_all = jnp.concatenate([v_past.astype(v.dtype), v], axis=1)
        k_pos = jnp.concatenate([jnp.arange(k_past.shape[1], dtype=jnp.int32), q_pos])
    lam0 = lambda_init(layer)
    lam = (jnp.exp(jnp.sum(lam_q1.astype(f32) * lam_k1.astype(f32)))
           - jnp.exp(jnp.sum(lam_q2.astype(f32) * lam_k2.astype(f32))) + lam0)
    att = diff_attention(q, k_all, v_all, q_pos, k_pos, rel_table, lam)
    att = rms_norm(att, subln_w) * (1.0 - lam0)
    att = att.reshape(b, s, D_ATT) * jax.nn.silu(g)
    xbc_c, conv_new = causal_conv(xbc, conv_past, conv_w, conv_b)
    xbc_c = jax.nn.silu(xbc_c)
    xs, Bm, Cm = jnp.split(xbc_c, [D_SSM, D_SSM + N_SSM_GROUPS * SSM_STATE], axis=-1)
    xs = xs.reshape(b, s, N_SSM_HEADS, SSM_HEAD_DIM)
    dt = jax.nn.softplus(dt.astype(f32) + dt_bias.astype(f32))
    A = -jnp.exp(A_log.astype(f32))
    y, ssm_new = ssd_scan(xs, dt, A, Bm.reshape(b, s, N_SSM_GROUPS, SSM_STATE),
                          Cm.reshape(b, s, N_SSM_GROUPS, SSM_STATE), ssm_past)
    y = y + D_skip.astype(f32)[:, None] * xs.astype(f32)
    y = y.reshape(b, s, D_SSM).astype(h.dtype) * jax.nn.silu(z)
    gs = D_SSM // N_SSM_GROUPS
    y = rms_norm(y.reshape(b, s, N_SSM_GROUPS, gs),
                 ssm_norm_w.reshape(N_SSM_GROUPS, gs)).reshape(b, s, D_SSM)
    out = jnp.concatenate([att, y], axis=-1) @ w_out.astype(h.dtype)
    return h + out, k, v, conv_new, ssm_new.astype(ssm_past.dtype)


def setup_inputs(seed: int = 0) -> dict:
    key = jax.random.key(seed)
    ks = jax.random.split(key, 24)
    nrm = jax.random.normal
    Hs = N_SSM_HEADS
    dt0 = jnp.exp(jax.random.uniform(ks[13], (DEPTH, Hs)) * (math.log(0.1) - math.log(0.001))
                  + math.log(0.001))
    return {
        'x_prompt': nrm(ks[0], (BATCH, SEQ, D_MODEL), jnp.float32),
        'x_sample': nrm(ks[1], (DEC_BATCH, DEC_SEQ, D_MODEL), jnp.float32),
        'cache_k': nrm(ks[2], (DEPTH, DEC_BATCH, PAST_LEN, N_ATT_HEADS, 2, ATT_HEAD_DIM), jnp.float32),
        'cache_v': nrm(ks[3], (DEPTH, DEC_BATCH, PAST_LEN, N_ATT_HEADS, 2 * ATT_HEAD_DIM), jnp.float32),
        'cache_conv': nrm(ks[4], (DEPTH, DEC_BATCH, CONV_WIDTH - 1, D_CONV), jnp.float32),
        'state_ssm': 0.1 * nrm(ks[5], (DEPTH, DEC_BATCH, Hs, SSM_HEAD_DIM, SSM_STATE), jnp.float32),
        'rel_bias': 0.5 * nrm(ks[6], (N_REL_BUCKETS, N_ATT_HEADS), jnp.float32),
        'norm_w': 1.0 + 0.01 * nrm(ks[7], (DEPTH, D_MODEL), jnp.float32),
        'w_in': nrm(ks[8], (DEPTH, D_MODEL, D_IN_PROJ), jnp.float32) * D_MODEL ** -0.5,
        'lambda_q1': 0.1 * nrm(ks[9], (DEPTH, ATT_HEAD_DIM), jnp.float32),
        'lambda_k1': 0.1 * nrm(ks[10], (DEPTH, ATT_HEAD_DIM), jnp.float32),
        'lambda_q2': 0.1 * nrm(ks[11], (DEPTH, ATT_HEAD_DIM), jnp.float32),
        'lambda_k2': 0.1 * nrm(ks[12], (DEPTH, ATT_HEAD_DIM), jnp.float32),
        'subln_w': 1.0 + 0.01 * nrm(ks[14], (DEPTH, 2 * ATT_HEAD_DIM), jnp.float32),
        'conv_w': 0.5 * nrm(ks[15], (DEPTH, CONV_WIDTH, D_CONV), jnp.float32),
        'conv_b': 0.01 * nrm(ks[16], (DEPTH, D_CONV), jnp.float32),
        'dt_bias': dt0 + jnp.log(-jnp.expm1(-dt0)),
        'A_log': jnp.log(jax.random.uniform(ks[17], (DEPTH, Hs), minval=1.0, maxval=16.0)),
        'D_skip': 1.0 + 0.01 * nrm(ks[18], (DEPTH, Hs), jnp.float32),
        'ssm_norm_w': 1.0 + 0.01 * nrm(ks[19], (DEPTH, D_SSM), jnp.float32),
        'w_out': nrm(ks[20], (DEPTH, D_MIX, D_MODEL), jnp.float32) * D_MIX ** -0.5,
        'final_norm_w': 1.0 + 0.01 * nrm(ks[21], (D_MODEL,), jnp.float32),
    }


def reference(x_prompt, x_sample, cache_k, cache_v, cache_conv, state_ssm, rel_bias, norm_w, w_in,
              lambda_q1, lambda_k1, lambda_q2, lambda_k2, subln_w, conv_w, conv_b, dt_bias, A_log,
              D_skip, ssm_norm_w, w_out, final_norm_w):
    bp, sp = x_prompt.shape[:2]
    pos_p = jnp.arange(sp, dtype=jnp.int32)
    conv0 = jnp.zeros((bp, CONV_WIDTH - 1, D_CONV), x_prompt.dtype)
    ssm0 = jnp.zeros((bp, N_SSM_HEADS, SSM_HEAD_DIM, SSM_STATE), state_ssm.dtype)
    past_len = cache_k.shape[2]
    pos_s = past_len + jnp.arange(x_sample.shape[1], dtype=jnp.int32)
    hp, hs = x_prompt, x_sample
    kp, vp, cp, sp_l, ksl, vsl, csl, ssl = [], [], [], [], [], [], [], []
    for l in range(DEPTH):
        params = (norm_w[l], w_in[l], lambda_q1[l], lambda_k1[l], lambda_q2[l], lambda_k2[l],
                  subln_w[l], conv_w[l], conv_b[l], dt_bias[l], A_log[l], D_skip[l],
                  ssm_norm_w[l], w_out[l])
        hp, k1, v1, c1, s1 = hybrid_layer(hp, pos_p, None, None, conv0, ssm0, l, rel_bias, *params)
        hs, k2, v2, c2, s2 = hybrid_layer(hs, pos_s, cache_k[l], cache_v[l], cache_conv[l],
                                          state_ssm[l], l, rel_bias, *params)
        kp.append(k1); vp.append(v1); cp.append(c1); sp_l.append(s1)
        ksl.append(k2); vsl.append(v2); csl.append(c2); ssl.append(s2)
    y_prompt = rms_norm(hp, final_norm_w)
    y_sample = rms_norm(hs, final_norm_w)
    return (y_prompt, y_sample, jnp.stack(kp), jnp.stack(vp), jnp.stack(cp), jnp.stack(sp_l),
            jnp.stack(ksl), jnp.stack(vsl), jnp.stack(csl), jnp.stack(ssl))
```

```python
import math
import numpy as np
from contextlib import ExitStack
import concourse.bass as bass
import concourse.mybir as mybir
from concourse.bass_utils import run_bass_kernel_spmd

F32 = mybir.dt.float32
BF16 = mybir.dt.bfloat16
ALU = mybir.AluOpType
AF = mybir.ActivationFunctionType
AX = mybir.AxisListType

D = 2048
KC = 16
NCOL = 13344
OQ, OK_, OV, OG, OZ, OX, ODT = 0, 2048, 4096, 6144, 8192, 10240, 13312
EPS = 1e-5
LAM0 = 0.8 - 0.6 * math.exp(-0.3 * 0)
SCALE = 128 ** -0.5
PAST = 2048
NEG = -30000.0


class Buf:
    __slots__ = ("t", "w", "r", "name")

    def __init__(self, t=None, name=""):
        self.t = t
        self.w = None
        self.r = {}
        self.name = name

    def __getitem__(self, k):
        return self.t[k]


class FW:
    def __init__(self, nc, es, ndma=16):
        self.nc = nc
        self.eng = {"pe": nc.tensor, "act": nc.scalar, "dve": nc.vector, "pool": nc.gpsimd, "sp": nc.sync}
        self.sem = {}
        self.cnt = {}
        self.semobj = {}
        for k in self.eng:
            self.sem[k] = es.enter_context(nc.semaphore("s_" + k))
            self.cnt[k] = 0
            self.semobj[k] = self.sem[k]
        self.waited = {}
        self.dring = {}
        self.dpos = {}
        for q in ("sp", "pool"):
            self.dring[q] = [[es.enter_context(nc.semaphore("d_%s%d" % (q, i))), 0] for i in range(ndma)]
            self.dpos[q] = 0
            for i, s in enumerate(self.dring[q]):
                self.semobj[("d", q, i)] = s[0]
        self.ninstr = 0

    def _wait(self, e, key, val):
        if val <= 0:
            return
        if e == "pe" and key == "pe":
            return
        if self.waited.get((e, key), 0) >= val:
            return
        self.eng[e].wait_ge(self.semobj[key], val)
        self.waited[(e, key)] = val
        self.ninstr += 1

    def _deps(self, e, reads, writes):
        deps = {}
        for b in reads:
            if b.w is not None:
                k, v = b.w
                if deps.get(k, 0) < v:
                    deps[k] = v
        for b in writes:
            if b.w is not None:
                k, v = b.w
                if deps.get(k, 0) < v:
                    deps[k] = v
            for k, v in b.r.items():
                if deps.get(k, 0) < v:
                    deps[k] = v
        for k, v in deps.items():
            self._wait(e, k, v)

    def _mark(self, tok, reads, writes):
        k, v = tok
        for b in reads:
            if b.r.get(k, 0) < v:
                b.r[k] = v
        for b in writes:
            b.w = tok
            b.r = {}

    def op(self, e, fn, reads=(), writes=()):
        return self.group(e, [fn], reads, writes)

    def group(self, e, fns, reads=(), writes=()):
        self._deps(e, reads, writes)
        ins = None
        for fn in fns:
            ins = fn(self.eng[e])
            self.ninstr += 1
        self.cnt[e] += 1
        ins.then_inc(self.sem[e], 1)
        tok = (e, self.cnt[e])
        self._mark(tok, reads, writes)
        return tok

    def dma(self, q, out, in_, reads=(), writes=(), **kw):
        self._deps(q, reads, writes)
        ring = self.dring[q]
        i = self.dpos[q]
        self.dpos[q] = (i + 1) % len(ring)
        slot = ring[i]
        key = ("d", q, i)
        self._wait(q, key, slot[1])
        ins = self.eng[q].dma_start(out=out, in_=in_, **kw)
        slot[1] += 16
        ins.then_inc(slot[0], 16)
        self.ninstr += 1
        tok = (key, slot[1])
        self._mark(tok, reads, writes)
        return tok

    def alias(self, dst, srcs):
        for s in srcs:
            if s.w is not None:
                k, v = s.w
                if dst.r.get(k, 0) < v:
                    dst.r[k] = v
            for k, v in s.r.items():
                if dst.r.get(k, 0) < v:
                    dst.r[k] = v

    def barrier(self):
        for e in self.eng:
            for k in self.eng:
                if k != e:
                    self._wait(e, k, self.cnt[k])
            for q in self.dring:
                for i, slot in enumerate(self.dring[q]):
                    self._wait(e, ("d", q, i), slot[1])


def rel_bucket_np(rel):
    half, max_exact = 16, 8
    ret = np.where(rel > 0, half, 0)
    n = np.abs(rel)
    nf = np.maximum(n, 1).astype(np.float32)
    large = max_exact + (np.log(nf / np.float32(max_exact)) / np.float32(math.log(128 / max_exact))
                         * np.float32(half - max_exact)).astype(np.int32)
    large = np.minimum(large, half - 1)
    return ret + np.where(n < max_exact, n, large)


def build(NBLK=32, sample=True, stage=99):
    nc = bass.Bass("TRN2", target_bir_lowering=False)
    SEQ = NBLK * 512

    def din(name, shape, dt=F32):
        return nc.dram_tensor(name, list(shape), dt, kind="ExternalInput").ap()

    def dout(name, shape, dt=F32):
        return nc.dram_tensor(name, list(shape), dt, kind="ExternalOutput").ap()

    def dscr(name, shape, dt):
        return nc.dram_tensor(name, list(shape), dt).ap()

    xp = din("xp", [SEQ, D])
    NM = NBLK // 4
    xown = din("xown", [NM * 512, D])
    selw = din("selw", [128, 17 * 7])
    wsel = din("wsel", [128, 4])
    xsm = din("xsm", [32, D])
    ck = din("ck", [PAST, D])
    cv = din("cv", [PAST, D])
    cconv = din("cconv", [128, 24, 3])
    sst = din("sst", [2048, 128])
    relb = din("relb", [32, 8])
    nw = din("nw", [128, 16])
    w_in = din("w_in", [D, NCOL])
    w_out = din("w_out", [4096, D])
    lamv = din("lamv", [4, 128])
    sublnT = din("sublnT", [128, 2])
    convw = din("convw", [128, 24, 4])
    convb = din("convb", [128, 24])
    dtb = din("dtb", [1, 32])
    alog = din("alog", [1, 32])
    dskip = din("dskip", [1, 32])
    ssmnwT = din("ssmnwT", [128, 16])
    fnw = din("fnw", [1, D])
    c_ident = din("c_ident", [128, 128])
    c_triu = din("c_triu", [128, 128])
    c_anti = din("c_anti", [128, 128])
    c_oh = din("c_oh", [32, 1152])
    c_mask = din("c_mask", [128, 5, 512])

    y_p = dout("y_p", [NM * 512, D])
    k_p = dout("k_p", [SEQ, D])
    v_p = dout("v_p", [SEQ, D])
    conv_p = dout("conv_p", [3, 3072])
    ssm_p = dout("ssm_p", [2048, 128])
    y_s = dout("y_s", [32, D])
    k_s = dout("k_s", [32, D])
    v_s = dout("v_s", [32, D])
    conv_s = dout("conv_s", [3, 3072])
    ssm_s = dout("ssm_s", [2048, 128])

    Wb = dscr("Wb", [D, NCOL], BF16)
    Wob = dscr("Wob", [4096, D], BF16)
    KTd = dscr("KTd", [16, 128, SEQ], BF16)
    Vd = dscr("Vd", [SEQ, D], BF16)
    KTsd = dscr("KTsd", [16, 128, PAST + 128], BF16)
    Vsd = dscr("Vsd", [PAST + 128, D], BF16)
    bvs = dscr("bvs", [8, 1152], F32)
    bmd = dscr("bmd", [8, 128, 5 * 512], F32)
    bm2 = dscr("bm2", [8, 17, 128, 512], F32)

    es = ExitStack()
    with es:
        fw = FW(nc, es)
        ARENA = 207 * 1024
        arena = es.enter_context(nc.sbuf_tensor("arena", [128, ARENA // 4], F32))
        apos = [0]

        def alloc(nbytes):
            o = apos[0]
            apos[0] = o + ((nbytes + 31) // 32) * 32
            assert apos[0] <= ARENA, ("SBUF arena overflow", apos[0])
            return o

        def view(off, shape, dt, name=""):
            n = int(np.prod(shape[1:]))
            if dt == F32:
                ap = arena[0:shape[0], off // 4: off // 4 + n]
            else:
                ap = arena[0:shape[0], off // 4: off // 4 + (n + 1) // 2].bitcast(BF16)[:, 0:n]
            if len(shape) == 3:
                ap = ap.rearrange("p (a b) -> p a b", a=shape[1])
            elif len(shape) == 4:
                ap = ap.rearrange("p (a b c) -> p a b c", a=shape[1], b=shape[2])
            return Buf(ap, name)

        def sb(name, shape, dt):
            n = int(np.prod(shape[1:])) * (4 if dt == F32 else 2)
            return view(alloc(n), shape, dt, name)

        PS = [Buf(es.enter_context(nc.psum_tensor("ps%d" % i, [128, 512], F32)), "ps%d" % i) for i in range(8)]
        PSb = [p.t[:].bitcast(BF16) for p in PS]

        idf = sb("idf", [128, 128], F32)
        idb = sb("idb", [128, 128], BF16)
        triu = sb("triu", [128, 128], F32)
        onesf = sb("onesf", [128, 128], F32)
        nwt = sb("nwt", [128, 16], F32)
        cw = sb("cw", [128, 24, 4], F32)
        cb = sb("cb", [128, 24], F32)
        dtb_bc = sb("dtb_bc", [128, 32], F32)
        A_bc = sb("A_bc", [128, 32], F32)
        D_bc = sb("D_bc", [128, 32], F32)
        fnw_bc = sb("fnw_bc", [128, D], F32)
        cb15 = sb("cb15", [128, 8], F32)
        nlam = sb("nlam", [128, 1], F32)
        hT = sb("hT", [128, 2048], F32)
        hTb = sb("hTb", [128, 2048], BF16)
        tails = sb("tails", [128, 24, 3], F32)
        small = sb("small", [128, 64], F32)
        selt = sb("selt", [128, 17 * 7], F32)
        wselt = sb("wselt", [128, 4], F32)
        MAIN0 = apos[0]

        fw.dma("sp", idf[:], c_ident, writes=[idf])
        fw.dma("sp", triu[:], c_triu, writes=[triu])
        fw.dma("sp", nwt[:], nw, writes=[nwt])
        fw.dma("sp", cw[:], convw, writes=[cw])
        fw.dma("sp", cb[:], convb, writes=[cb])
        fw.dma("sp", dtb_bc[:], dtb.partition_broadcast(128), writes=[dtb_bc])
        fw.dma("sp", A_bc[:], alog.partition_broadcast(128), writes=[A_bc])
        fw.dma("sp", D_bc[:], dskip.partition_broadcast(128), writes=[D_bc])
        fw.dma("sp", fnw_bc[:], fnw.partition_broadcast(128), writes=[fnw_bc])
        fw.dma("sp", cb15[:], relb[15:16, :].partition_broadcast(128), writes=[cb15])
        fw.dma("sp", selt[:], selw, writes=[selt])
        fw.dma("sp", wselt[:], wsel, writes=[wselt])
        fw.op("dve", lambda e: e.tensor_copy(idb[:], idf[:]), reads=[idf], writes=[idb])
        fw.op("pool", lambda e: e.memset(onesf[:], 1.0), writes=[onesf])
        fw.op("pool", lambda e: e.memset(hT[:], 0.0), writes=[hT])
        fw.op("pool", lambda e: e.memset(hTb[:], 0.0), writes=[hTb])
        fw.op("pool", lambda e: e.memset(tails[:], 0.0), writes=[tails])
        fw.op("act", lambda e: e.activation(A_bc[:], A_bc[:], AF.Exp), reads=[A_bc], writes=[A_bc])
        fw.op("dve", lambda e: e.tensor_scalar_mul(A_bc[:], A_bc[:], -1.0), reads=[A_bc], writes=[A_bc])

        apos[0] = MAIN0
        lv = sb("lv", [128, 4, 128], F32)
        lp = sb("lp", [128, 2, 128], F32)
        ls = sb("ls", [128, 2], F32)
        for i in range(4):
            fw.dma("sp", lv[:, i, :], lamv[i:i + 1, :].partition_broadcast(128), writes=[lv])
        fw.op("dve", lambda e: e.tensor_tensor(lp[:, 0, :], lv[:, 0, :], lv[:, 1, :], ALU.mult), reads=[lv], writes=[lp])
        fw.op("dve", lambda e: e.tensor_tensor(lp[:, 1, :], lv[:, 2, :], lv[:, 3, :], ALU.mult), reads=[lv, lp], writes=[lp])
        fw.op("dve", lambda e: e.reduce_sum(ls[:], lp[:], AX.X), reads=[lp], writes=[ls])
        fw.op("act", lambda e: e.activation(ls[:], ls[:], AF.Exp), reads=[ls], writes=[ls])
        fw.op("dve", lambda e: e.tensor_tensor(nlam[:], ls[:, 1:2], ls[:, 0:1], ALU.subtract), reads=[ls], writes=[nlam])
        fw.op("dve", lambda e: e.tensor_scalar_add(nlam[:], nlam[:], -LAM0), reads=[nlam], writes=[nlam])

        oh = sb("oh", [32, 1152], F32)
        tab = sb("tab", [32, 8], F32)
        bvsb = sb("bvsb", [8, 1152], F32)
        anti = sb("anti", [128, 128], F32)
        msk = sb("msk", [128, 5, 512], F32)
        bvd = Buf(None, "bvd")
        fw.dma("sp", oh[:], c_oh, writes=[oh])
        fw.dma("sp", tab[:], relb, writes=[tab])
        fw.dma("sp", anti[:], c_anti, writes=[anti])
        fw.dma("sp", msk[:], c_mask, writes=[msk])
        for j in range(3):
            fw.group("pe", [lambda e, j=j: e.matmul(PS[j][0:8, 0:384], lhsT=tab[:, :], rhs=oh[:, j * 384:(j + 1) * 384],
                                                   start=True, stop=True)], reads=[tab, oh], writes=[PS[j]])
            fw.op("dve", lambda e, j=j: e.tensor_copy(bvsb[:, j * 384:(j + 1) * 384], PS[j][0:8, 0:384]),
                  reads=[PS[j]], writes=[bvsb])
        fw.dma("sp", bvs, bvsb[:], reads=[bvsb], writes=[bvd])
        trev = [sb("trev%d" % i, [128, 512], F32) for i in range(2)]
        tfl = [sb("tfl%d" % i, [128, 512], F32) for i in range(2)]
        bmbuf = Buf(None, "bmd")
        it = 0
        for h in range(8):
            for d in range(5):
                delta = 128 * (d - 1)
                tr = trev[it % 2]
                tf = tfl[it % 2]
                pp = PS[3 + it % 2]
                src = bass.AP(tensor=bvs.tensor, offset=h * 1152 + 384 - delta, ap=[[1, 128], [1, 512]])
                fw.dma("sp", tr[:], src, reads=[bvd], writes=[tr])
                fw.group("pe", [lambda e, tr=tr, pp=pp: e.matmul(pp[:, :], lhsT=anti[:, :], rhs=tr[:, :], start=True, stop=True)],
                         reads=[anti, tr], writes=[pp])
                fw.op("dve", lambda e, tf=tf, pp=pp, d=d: e.tensor_tensor(tf[:], pp[:, :], msk[:, d, :], ALU.add),
                      reads=[pp, msk], writes=[tf])
                fw.dma("pool", bmd[h, :, d * 512:(d + 1) * 512], tf[:], reads=[tf], writes=[bmbuf])
                it += 1

        Tt = sb("Tt", [128, 5, 512], F32)
        s56 = sb("s56", [128, 2], F32)
        bm2buf = Buf(None, "bm2")
        it = 0
        for h in range(8):
            fw.dma("sp", Tt[:], bmd[h].rearrange("p (d q) -> p d q", d=5), reads=[bmbuf], writes=[Tt])
            for t_ in range(17):
                o = tfl[it % 2]
                c0 = t_ * 7
                fw.op("dve", lambda e: e.tensor_scalar_mul(o[:], Tt[:, 0, :], selt[:, c0:c0 + 1]), reads=[Tt, selt], writes=[o])
                for d in range(1, 5):
                    fw.op("dve", lambda e, d=d: e.scalar_tensor_tensor(o[:], Tt[:, d, :], selt[:, c0 + d:c0 + d + 1], o[:], ALU.mult, ALU.add),
                          reads=[Tt, selt, o], writes=[o])
                fw.op("dve", lambda e: e.tensor_tensor(s56[:, 0:1], selt[:, c0 + 5:c0 + 6], cb15[:, h:h + 1], ALU.mult),
                      reads=[selt, cb15], writes=[s56])
                fw.op("dve", lambda e: e.scalar_tensor_tensor(s56[:, 1:2], selt[:, c0 + 6:c0 + 7], NEG, s56[:, 0:1], ALU.mult, ALU.add),
                      reads=[selt, s56], writes=[s56])
                fw.op("dve", lambda e: e.tensor_scalar_add(o[:], o[:], s56[:, 1:2]), reads=[o, s56], writes=[o])
                fw.dma("pool", bm2[h, t_], o[:], reads=[o], writes=[bm2buf])
                it += 1

        wf = [sb("wf%d" % i, [128, 1668], F32) for i in range(3)]
        wb_ = [sb("wb%d" % i, [128, 1668], BF16) for i in range(3)]
        sT = sb("sT", [128, 2], F32)
        snT = sb("snT", [128, 16], F32)
        fw.dma("sp", sT[:], sublnT, writes=[sT])
        fw.dma("sp", snT[:], ssmnwT, writes=[snT])
        fw.op("dve", lambda e: e.tensor_scalar_mul(sT[:], sT[:], 1.0 - LAM0), reads=[sT], writes=[sT])
        Wbuf = Buf(None, "Wb")
        Wobuf = Buf(None, "Wob")
        it = 0
        for kc in range(16):
            for pc in range(8):
                a, b = wf[it % 3], wb_[it % 3]
                c0 = pc * 1668
                fw.dma("sp", a[:], w_in[kc * 128:(kc + 1) * 128, c0:c0 + 1668], writes=[a])
                if it % 2 == 0:
                    fw.op("act", lambda e, a=a, b=b, kc=kc: e.activation(b[:], a[:], AF.Copy, scale=nwt[:, kc:kc + 1]),
                          reads=[a, nwt], writes=[b])
                else:
                    fw.op("dve", lambda e, a=a, b=b, kc=kc: e.tensor_scalar_mul(b[:], a[:], nwt[:, kc:kc + 1]),
                          reads=[a, nwt], writes=[b])
                fw.dma("pool", Wb[kc * 128:(kc + 1) * 128, c0:c0 + 1668], b[:], reads=[b], writes=[Wbuf])
                it += 1
        for kc in range(32):
            for pc in range(2):
                a, b = wf[it % 3], wb_[it % 3]
                c0 = pc * 1024
                sc = sT[:, kc % 2:kc % 2 + 1] if kc < 16 else snT[:, kc - 16:kc - 15]
                fw.dma("sp", a[:, 0:1024], w_out[kc * 128:(kc + 1) * 128, c0:c0 + 1024], writes=[a])
                if it % 2 == 0:
                    fw.op("act", lambda e, a=a, b=b, sc=sc: e.activation(b[:, 0:1024], a[:, 0:1024], AF.Copy, scale=sc),
                          reads=[a, sT, snT], writes=[b])
                else:
                    fw.op("dve", lambda e, a=a, b=b, sc=sc: e.tensor_scalar_mul(b[:, 0:1024], a[:, 0:1024], sc),
                          reads=[a, sT, snT], writes=[b])
                fw.dma("pool", Wob[kc * 128:(kc + 1) * 128, c0:c0 + 1024], b[:, 0:1024], reads=[b], writes=[Wobuf])
                it += 1
        fw.barrier()

        apos[0] = MAIN0
        xT = sb("xT", [128, 16, 512], BF16)
        xld = [sb("xld%d" % i, [128, D], F32) for i in range(2)]
        xn = sb("xn", [128, D], BF16)
        wst = [sb("wst%d" % i, [128, 16, 256], BF16) for i in range(3)]
        kf = [sb("kf%d" % i, [128, 256], F32) for i in range(2)]
        kb = [sb("kb%d" % i, [128, 256], BF16) for i in range(2)]
        KTst = [sb("KTst%d" % i, [128, 2, 512], BF16) for i in range(2)]
        o_sz = alloc(16384)
        o_xs = alloc(16384)
        sz = view(o_sz, [128, 4, 2048], BF16, "sz")
        xs_tok = view(o_xs, [128, 4, 2048], BF16, "xs_tok")
        hres = view(o_sz, [128, 4, 2048], F32, "hres")
        dts = sb("dts", [128, 4, 32], F32)
        av = sb("av", [128, 4, 32], F32)
        dtmp = [sb("dtmp%d" % i, [128, 32], F32) for i in range(3)]
        acst = sb("acst", [128, 32], F32)
        tot = sb("tot", [128, 32], F32)
        ea = sb("ea", [128, 32], F32)
        te = sb("te", [128, 32], F32)
        dec = sb("dec", [128, 32], F32)
        mixT = sb("mixT", [128, 32, 512], BF16)
        U0 = apos[0]
        BT = sb("BT", [128, 4, 512], BF16)
        CT = sb("CT", [128, 4, 512], BF16)
        B_tok = sb("B_tok", [128, 4, 512], BF16)
        raw = [sb("raw%d" % i, [128, 516], F32) for i in range(2)]
        cva = [sb("cva%d" % i, [128, 512], F32) for i in range(1)] * 2
        csb = [sb("csb%d" % i, [128, 512], BF16) for i in range(2)]
        xdt = sb("xdt", [128, 2048], BF16)
        xdte = sb("xdte", [128, 2048], BF16)
        Rb = [sb("R%d" % i, [128, 512], F32) for i in range(2)]
        segs = [sb("segs%d" % i, [128, 512], F32) for i in range(2)]
        MT = [sb("MT%d" % i, [128, 512], BF16) for i in range(2)]
        cbm = sb("cbm", [128, 4, 128], F32)
        t1 = [sb("t1_%d" % i, [128, 512], F32) for i in range(2)]
        t2 = [sb("t2_%d" % i, [128, 512], F32) for i in range(1)] * 2
        ym = [sb("ym%d" % i, [128, 512], BF16) for i in range(2)]
        U1 = apos[0]
        ssd_bufs = [BT, CT, B_tok, xdt, xdte, cbm] + raw + cva + csb + Rb + segs + MT + t1 + t2 + ym
        apos[0] = U0
        kst = [sb("kst%d" % i, [128, 2048], BF16) for i in range(2)]
        vst = [sb("vst%d" % i, [128, 16, 257], BF16) for i in range(2)]
        QT = sb("QT", [128, 2, 512], BF16)
        sg = sb("sg", [128, 4, 256], BF16)
        o_bmt0 = apos[0]
        bmt = sb("bmt", [128, 5, 512], F32)
        bms = [view(o_bmt0 + i * 2048, [128, 512], F32, "bms%d" % i) for i in range(5)]
        PT = [sb("PT%d" % i, [128, 512], BF16) for i in range(3)]
        dgt = [sb("dgt%d" % i, [128, 512], F32) for i in range(1)] * 2
        att = sb("att", [128, 4, 256], F32)
        ma = [sb("ma%d" % i, [128, 256], BF16) for i in range(1)] * 2
        o_bmt = apos[0] - 0
        att_bufs = kst + vst + [QT, sg, bmt, att] + PT + dgt + ma + bms
        apos[0] = max(U1, apos[0])

        def enter_ssd_phase():
            for b_ in ssd_bufs:
                fw.alias(b_, att_bufs)

        def enter_att_phase():
            for b_ in att_bufs:
                fw.alias(b_, ssd_bufs)
        print("SBUF bytes/partition used:", apos[0])


        rr = {"ps": 0, "w": 0, "tr": 0, "sm": 0}

        def next_ps():
            rr["ps"] = (rr["ps"] + 1) % 4
            return PS[rr["ps"]]

        def next_tr():
            rr["tr"] = (rr["tr"] + 1) % 2
            return 4 + rr["tr"]

        def next_w():
            rr["w"] = (rr["w"] + 1) % 3
            return wst[rr["w"]]

        def smallcol(n=1):
            o = rr["sm"]
            if o + n > 64:
                o = 0
            rr["sm"] = o + n
            return o

        def load_w(col0, ncols):
            w = next_w()
            fw.dma("sp", w[:, :, 0:ncols], Wb[:, col0:col0 + ncols].rearrange("(kc p) c -> p kc c", p=128),
                   reads=[Wbuf], writes=[w])
            return w

        def evac_alt(idx, out_ap, in_ap, reads, writes):
            if idx % 2 == 0:
                fw.op("dve", lambda e: e.tensor_copy(out_ap, in_ap), reads=reads, writes=writes)
            else:
                fw.op("act", lambda e: e.copy(out_ap, in_ap), reads=reads, writes=writes)

        def transposes_to(dst_buf, dst_ap_fn, srcs, src_bufs, tp_in, n_out):
            bi = next_tr()
            pv = PSb[bi]
            n = len(srcs)
            fw.group("pe", [lambda e, j=j, s=s: e.transpose(pv[0:n_out, j * 128:j * 128 + tp_in], s, idb[0:tp_in, 0:tp_in])
                            for j, s in enumerate(srcs)], reads=list(src_bufs) + [idb], writes=[PS[bi]])
            pview = pv[0:n_out, 0:n * 128].rearrange("p (a b) -> p a b", a=n)[:, :, 0:tp_in]
            dst_ap_fn(pview, PS[bi])

        def rms_rstd(ss_ap, ss_buf, n, tp):
            fw.op("act", lambda e: e.activation(ss_ap, ss_ap, AF.Sqrt, bias=EPS, scale=1.0 / n), reads=[ss_buf], writes=[ss_buf])
            fw.op("dve", lambda e: e.reciprocal(ss_ap, ss_ap), reads=[ss_buf], writes=[ss_buf])

        def process_block(cfg):
            tp, ntt = cfg["tp"], cfg["ntt"]
            NQ = tp * ntt
            xsrc = cfg["xsrc"]
            phases = cfg.get("phases", "ABCDEF")
            for tt in range(ntt):
                xl = xld[tt % 2]
                fw.dma("sp", xl[0:tp, :], xsrc[tt * tp:(tt + 1) * tp, :], writes=[xl])
                c = smallcol()
                ssap = small[0:tp, c:c + 1]
                fw.op("pool", lambda e: e.memset(ssap, 0.0), writes=[small])
                fw.op("act", lambda e: e.activation(xn[0:tp, :], xl[0:tp, :], AF.Square, accum_out=ssap),
                      reads=[xl], writes=[xn, small])
                rms_rstd(ssap, small, D, tp)
                fw.op("act", lambda e: e.activation(xn[0:tp, :], xl[0:tp, :], AF.Copy, scale=ssap),
                      reads=[xl, small], writes=[xn])
                for half in range(2):
                    def cp(pview, pbuf, half=half, tt=tt):
                        evac_alt(half, xT[:, half * 8:(half + 1) * 8, tt * tp:(tt + 1) * tp], pview, [pbuf], [xT])
                    transposes_to(xT, cp, [xn[0:tp, (half * 8 + j) * 128:(half * 8 + j + 1) * 128] for j in range(8)],
                                  [xn], tp, 128)

            def inproj_tok(col0, ncols, consume):
                w = load_w(col0, ncols)
                for tt in range(ntt):
                    ps = next_ps()
                    fw.group("pe", [lambda e, kc=kc: e.matmul(ps[0:tp, 0:ncols], lhsT=xT[:, kc, tt * tp:(tt + 1) * tp],
                                                              rhs=w[:, kc, 0:ncols], start=(kc == 0), stop=(kc == 15))
                                    for kc in range(16)], reads=[xT, w], writes=[ps])
                    consume(tt, ps)

            def inproj_feat(w, j, consume):
                ps = next_ps()
                fw.group("pe", [lambda e, kc=kc: e.matmul(ps[:, 0:NQ], lhsT=w[:, kc, j * 128:(j + 1) * 128],
                                                          rhs=xT[:, kc, 0:NQ], start=(kc == 0), stop=(kc == 15))
                                for kc in range(16)], reads=[xT, w], writes=[ps])
                consume(ps)

            if stage <= 1:
                return
            if "B" in phases:
                ktd, vd_, kout, vout = cfg["KT"], cfg["V"], cfg["kout"], cfg["vout"]
                tok0 = cfg["tok0"]
                kvb = cfg["kvbuf"]
                for cc in range(16):
                    isk = cc < 8
                    col0 = (OK_ if isk else OV) + (cc % 8) * 256
                    kts = KTst[cc % 2]

                    def cons(tt, ps, cc=cc, isk=isk, kts=kts):
                        f = kf[(cc * ntt + tt) % 2]
                        b = kb[(cc * ntt + tt) % 2]
                        if stage <= 1.2:
                            return
                        fw.op("act", lambda e: e.copy(f[0:tp, :], ps[0:tp, 0:256]), reads=[ps], writes=[f])
                        fw.op("dve", lambda e: e.tensor_copy(b[0:tp, :], f[0:tp, :]), reads=[f], writes=[b])
                        if stage <= 1.4:
                            return
                        dst = kout if isk else vout
                        fw.dma("pool", dst[tt * tp:(tt + 1) * tp, (cc % 8) * 256:(cc % 8 + 1) * 256], f[0:tp, :], reads=[f])
                        if stage <= 1.6:
                            return
                        if isk:
                            def cp(pview, pbuf):
                                fw.op("dve", lambda e: e.tensor_copy(kts[:, :, tt * tp:(tt + 1) * tp], pview), reads=[pbuf], writes=[kts])
                            transposes_to(kts, cp, [b[0:tp, j * 128:(j + 1) * 128] for j in range(2)], [b], tp, 128)
                        else:
                            fw.dma("pool", vd_[tok0 + tt * tp: tok0 + (tt + 1) * tp, (cc % 8) * 256:(cc % 8 + 1) * 256], b[0:tp, :],
                                   reads=[b], writes=[kvb])
                    inproj_tok(col0, 256, cons)
                    if isk and stage > 1.8:
                        hm0 = (cc % 8) * 2
                        fw.dma("pool", ktd[hm0:hm0 + 2, :, tok0:tok0 + NQ].rearrange("a d t -> d a t"), kts[:, :, 0:NQ],
                               reads=[kts], writes=[kvb])

                if stage <= 2:
                    return
                enter_ssd_phase()
                fw.alias(sz, [hres])
                fw.alias(xs_tok, [hres])
                for cc in range(8):
                    def consz(tt, ps, cc=cc):
                        fw.op("act", lambda e: e.activation(sz[0:tp, tt, cc * 256:(cc + 1) * 256], ps[0:tp, 0:256], AF.Silu),
                              reads=[ps], writes=[sz])
                    inproj_tok(OZ + cc * 256, 256, consz)
                for cc in range(12):
                    w = load_w(OX + cc * 256, 256)
                    for j in range(2):
                        ct = cc * 2 + j
                        rw = raw[ct % 2]
                        ca = cva[ct % 2]

                        def consx(ps, ct=ct, rw=rw, ca=ca):
                            fw.op("pool", lambda e: e.tensor_copy(rw[:, 0:3], tails[:, ct, :]), reads=[tails], writes=[rw])
                            fw.op("act", lambda e: e.copy(rw[:, 3:3 + NQ], ps[:, 0:NQ]), reads=[ps], writes=[rw])
                            fw.op("pool", lambda e: e.tensor_copy(tails[:, ct, :], rw[:, NQ:NQ + 3]), reads=[rw], writes=[tails])
                            fw.op("dve", lambda e: e.tensor_scalar(ca[:, 0:NQ], rw[:, 3:3 + NQ], cw[:, ct, 3:4], cb[:, ct:ct + 1],
                                                                   ALU.mult, ALU.add), reads=[rw, cw, cb], writes=[ca])
                            for jj in range(3):
                                fw.op("dve", lambda e, jj=jj: e.scalar_tensor_tensor(ca[:, 0:NQ], rw[:, jj:jj + NQ], cw[:, ct, jj:jj + 1],
                                                                                   ca[:, 0:NQ], ALU.mult, ALU.add),
                                      reads=[rw, cw, ca], writes=[ca])
                            if ct < 16:
                                cs = csb[ct % 2]
                                fw.op("act", lambda e: e.activation(cs[:, 0:NQ], ca[:, 0:NQ], AF.Silu), reads=[ca], writes=[cs])

                                def cp(pview, pbuf):
                                    evac_alt(ct, xs_tok[0:tp, 0:ntt, ct * 128:(ct + 1) * 128], pview, [pbuf], [xs_tok])
                                transposes_to(xs_tok, cp, [cs[:, tt * tp:(tt + 1) * tp] for tt in range(ntt)], [cs], 128, tp)
                            elif ct < 20:
                                g = ct - 16
                                fw.op("act", lambda e: e.activation(BT[:, g, 0:NQ], ca[:, 0:NQ], AF.Silu), reads=[ca], writes=[BT])

                                def cp(pview, pbuf):
                                    evac_alt(ct, B_tok[0:tp, 0:ntt, g * 128:(g + 1) * 128], pview, [pbuf], [B_tok])
                                transposes_to(B_tok, cp, [BT[:, g, tt * tp:(tt + 1) * tp] for tt in range(ntt)], [BT], 128, tp)
                            else:
                                g = ct - 20
                                fw.op("act", lambda e: e.activation(CT[:, g, 0:NQ], ca[:, 0:NQ], AF.Silu), reads=[ca], writes=[CT])
                        inproj_feat(w, j, consx)

                def consdt(tt, ps):
                    d0, d1, d2 = dtmp
                    fw.op("dve", lambda e: e.tensor_tensor(d0[0:tp, :], ps[0:tp, 0:32], dtb_bc[0:tp, :], ALU.add),
                          reads=[ps, dtb_bc], writes=[d0])
                    fw.op("dve", lambda e: e.scalar_tensor_tensor(d1[0:tp, :], d0[0:tp, :], -1.0, d0[0:tp, :], ALU.mult, ALU.max),
                          reads=[d0], writes=[d1])
                    fw.op("act", lambda e: e.activation(d1[0:tp, :], d1[0:tp, :], AF.Exp, scale=-1.0), reads=[d1], writes=[d1])
                    fw.op("act", lambda e: e.activation(d1[0:tp, :], d1[0:tp, :], AF.Ln, bias=1.0), reads=[d1], writes=[d1])
                    fw.op("dve", lambda e: e.scalar_tensor_tensor(dts[0:tp, tt, :], d0[0:tp, :], 0.0, d1[0:tp, :], ALU.max, ALU.add),
                          reads=[d0, d1], writes=[dts])
                    fw.op("dve", lambda e: e.tensor_tensor(av[0:tp, tt, :], dts[0:tp, tt, :], A_bc[0:tp, :], ALU.mult),
                          reads=[dts, A_bc], writes=[av])
                inproj_tok(ODT, 32, consdt)

                if stage <= 3:
                    return
                def bc3(ap2, n):
                    return ap2.unsqueeze(2).to_broadcast([ap2.shape[0], ap2.shape[1], n])

                for c in range(ntt):
                    sl = slice(c * tp, (c + 1) * tp)
                    psA = next_ps()
                    fw.group("pe", [lambda e: e.matmul(psA[0:tp, 0:32], lhsT=triu[0:tp, 0:tp], rhs=av[0:tp, c, :], start=True, stop=True),
                                    lambda e: e.matmul(psA[:, 32:64], lhsT=onesf[0:tp, :], rhs=av[0:tp, c, :], start=True, stop=True)],
                             reads=[triu, onesf, av], writes=[psA])
                    fw.op("dve", lambda e: e.tensor_copy(acst[0:tp, :], psA[0:tp, 0:32]), reads=[psA], writes=[acst])
                    fw.op("dve", lambda e: e.tensor_copy(tot[:, :], psA[:, 32:64]), reads=[psA], writes=[tot])
                    fw.op("act", lambda e: e.activation(ea[0:tp, :], acst[0:tp, :], AF.Exp), reads=[acst], writes=[ea])
                    fw.op("act", lambda e: e.activation(dec[:, :], tot[:, :], AF.Exp), reads=[tot], writes=[dec])
                    fw.op("dve", lambda e: e.tensor_tensor(te[0:tp, :], tot[0:tp, :], acst[0:tp, :], ALU.subtract),
                          reads=[tot, acst], writes=[te])
                    fw.op("act", lambda e: e.activation(te[0:tp, :], te[0:tp, :], AF.Exp), reads=[te], writes=[te])
                    fw.op("dve", lambda e: e.tensor_tensor(te[0:tp, :], te[0:tp, :], dts[0:tp, c, :], ALU.mult),
                          reads=[te, dts], writes=[te])
                    xv = xs_tok[0:tp, c, :].rearrange("p (h q) -> p h q", h=32)
                    fw.op("dve", lambda e: e.tensor_tensor(xdt[0:tp, :].rearrange("p (h q) -> p h q", h=32), xv,
                                                           bc3(dts[0:tp, c, :], 64), ALU.mult), reads=[xs_tok, dts], writes=[xdt])
                    fw.op("pool", lambda e: e.tensor_tensor(xdte[0:tp, :].rearrange("p (h q) -> p h q", h=32), xv,
                                                            bc3(te[0:tp, :], 64), ALU.mult), reads=[xs_tok, te], writes=[xdte])
                    psC = next_ps()
                    fw.group("pe", [lambda e, g=g: e.matmul(psC[0:tp, g * 128:g * 128 + tp], lhsT=BT[:, g, sl], rhs=CT[:, g, sl],
                                                            start=True, stop=True) for g in range(4)],
                             reads=[BT, CT], writes=[psC])
                    fw.op("dve", lambda e: e.tensor_tensor(cbm[0:tp, :, 0:tp],
                                                           psC[0:tp, :].rearrange("p (g t) -> p g t", g=4)[:, :, 0:tp],
                                                           triu[0:tp, 0:tp].unsqueeze(1).to_broadcast([tp, 4, tp]), ALU.mult),
                          reads=[psC, triu], writes=[cbm])
                    for g in range(4):
                        psY = next_ps()
                        for hg in (2 * g, 2 * g + 1):
                            h0 = hg * 4
                            R = Rb[hg % 2]
                            sgm = segs[hg % 2]
                            M = MT[hg % 2]
                            R3 = R[0:tp, 0:4 * tp].rearrange("p (h t) -> p h t", h=4)
                            fw.op("pool", lambda e: e.tensor_tensor(R3, triu[0:tp, 0:tp].unsqueeze(1).to_broadcast([tp, 4, tp]),
                                                                    bc3(av[0:tp, c, h0:h0 + 4], tp), ALU.mult),
                                  reads=[triu, av], writes=[R])
                            psB = next_ps()
                            fw.group("pe", [lambda e: e.matmul(psB[0:tp, 0:4 * tp], lhsT=onesf[0:tp, 0:tp], rhs=R[0:tp, 0:4 * tp],
                                                               start=True, stop=True)], reads=[onesf, R], writes=[psB])
                            for hh in range(4):
                                fw.op("dve", lambda e, hh=hh: e.tensor_scalar(sgm[0:tp, hh * tp:(hh + 1) * tp], psB[0:tp, hh * tp:(hh + 1) * tp],
                                                                              acst[0:tp, h0 + hh:h0 + hh + 1], 0.0, ALU.subtract, ALU.min),
                                      reads=[psB, acst], writes=[sgm])
                            fw.op("act", lambda e: e.activation(sgm[0:tp, 0:4 * tp], sgm[0:tp, 0:4 * tp], AF.Exp), reads=[sgm], writes=[sgm])
                            fw.op("dve", lambda e: e.tensor_tensor(M[0:tp, 0:4 * tp].rearrange("p (h t) -> p h t", h=4),
                                                                   sgm[0:tp, 0:4 * tp].rearrange("p (h t) -> p h t", h=4),
                                                                   cbm[0:tp, g, 0:tp].unsqueeze(1).to_broadcast([tp, 4, tp]), ALU.mult),
                                  reads=[sgm, cbm], writes=[M])
                            fw.group("pe", [lambda e, hh=hh: e.matmul(psY[0:tp, ((h0 + hh) % 8) * 64:((h0 + hh) % 8) * 64 + 64],
                                                                      lhsT=M[0:tp, hh * tp:(hh + 1) * tp],
                                                                      rhs=xdt[0:tp, (h0 + hh) * 64:(h0 + hh + 1) * 64], start=True, stop=True)
                                            for hh in range(4)], reads=[M, xdt], writes=[psY])
                        psO = next_ps()
                        fw.group("pe", [lambda e: e.matmul(psO[0:tp, 0:512], lhsT=CT[:, g, sl], rhs=hTb[:, g * 512:(g + 1) * 512],
                                                           start=True, stop=True)], reads=[CT, hTb], writes=[psO])
                        a1, a2, yb = t1[g % 2], t2[g % 2], ym[g % 2]
                        gs = slice(g * 512, (g + 1) * 512)
                        v3 = lambda ap: ap.rearrange("p (h q) -> p h q", h=8)
                        fw.op("dve", lambda e: e.tensor_tensor(v3(a1[0:tp, :]), v3(psO[0:tp, 0:512]), bc3(ea[0:tp, 8 * g:8 * g + 8], 64), ALU.mult),
                              reads=[psO, ea], writes=[a1])
                        fw.op("dve", lambda e: e.tensor_tensor(a1[0:tp, :], a1[0:tp, :], psY[0:tp, 0:512], ALU.add), reads=[a1, psY], writes=[a1])
                        fw.op("pool", lambda e: e.tensor_tensor(v3(a2[0:tp, :]), v3(xs_tok[0:tp, c, gs]), bc3(D_bc[0:tp, 8 * g:8 * g + 8], 64), ALU.mult),
                              reads=[xs_tok, D_bc], writes=[a2])
                        fw.op("pool", lambda e: e.tensor_tensor(a1[0:tp, :], a1[0:tp, :], a2[0:tp, :], ALU.add), reads=[a1, a2], writes=[a1])
                        fw.op("pool", lambda e: e.tensor_tensor(a1[0:tp, :], a1[0:tp, :], sz[0:tp, c, gs], ALU.mult), reads=[a1, sz], writes=[a1])
                        cc_ = smallcol()
                        ssap = small[0:tp, cc_:cc_ + 1]
                        fw.op("pool", lambda e: e.memset(ssap, 0.0), writes=[small])
                        fw.op("act", lambda e: e.activation(a2[0:tp, :], a1[0:tp, :], AF.Square, accum_out=ssap), reads=[a1], writes=[a2, small])
                        rms_rstd(ssap, small, 512, tp)
                        fw.op("act", lambda e: e.activation(yb[0:tp, :], a1[0:tp, :], AF.Copy, scale=ssap), reads=[a1, small], writes=[yb])

                        def cp(pview, pbuf, g=g):
                            evac_alt(g, mixT[:, 16 + 4 * g:16 + 4 * g + 4, sl], pview, [pbuf], [mixT])
                        transposes_to(mixT, cp, [yb[0:tp, j * 128:(j + 1) * 128] for j in range(4)], [yb], tp, 128)
                    for g in range(4):
                        psH = next_ps()
                        gs = slice(g * 512, (g + 1) * 512)
                        fw.group("pe", [lambda e: e.matmul(psH[:, 0:512], lhsT=B_tok[0:tp, c, g * 128:(g + 1) * 128], rhs=xdte[0:tp, gs],
                                                           start=True, stop=True)], reads=[B_tok, xdte], writes=[psH])
                        h3 = hT[:, gs].rearrange("p (h q) -> p h q", h=8)
                        fw.op("dve", lambda e: e.tensor_tensor(h3, h3, bc3(dec[:, 8 * g:8 * g + 8], 64), ALU.mult), reads=[hT, dec], writes=[hT])
                        fw.op("dve", lambda e: e.tensor_tensor(hT[:, gs], hT[:, gs], psH[:, 0:512], ALU.add), reads=[hT, psH], writes=[hT])
                    fw.op("act", lambda e: e.copy(hTb[:, :], hT[:, :]), reads=[hT], writes=[hTb])

            if "B" in phases and cfg.get("blend_i") is not None:
                bi_ = cfg["blend_i"]
                if bi_ == 0:
                    fw.op("dve", lambda e: e.tensor_scalar_mul(mixT[:, 0:16, :], mixT[:, 16:32, :], wselt[:, 0:1]),
                          reads=[mixT, wselt], writes=[mixT])
                else:
                    fw.op("dve", lambda e: e.scalar_tensor_tensor(mixT[:, 0:16, :], mixT[:, 16:32, :], wselt[:, bi_:bi_ + 1],
                                                                   mixT[:, 0:16, :], ALU.mult, ALU.add), reads=[mixT, wselt], writes=[mixT])
            if "E" not in phases:
                return
            if cfg.get("move_own"):
                fw.op("pool", lambda e: e.tensor_copy(mixT[:, 16:32, :], mixT[:, 0:16, :]), reads=[mixT], writes=[mixT])
            if stage <= 4:
                return
            enter_att_phase()
            for v in vst:
                fw.op("pool", lambda e, v=v: e.memset(v[:, :, 256:257], 1.0), writes=[v])
            ktiles = cfg["ktiles"]
            pieces = cfg["pieces"]
            for h in range(8):
                wq = load_w(OQ + h * 256, 256)
                for m in range(2):
                    def consq(ps, m=m):
                        fw.op("act", lambda e: e.copy(QT[:, m, 0:NQ], ps[:, 0:NQ]), reads=[ps], writes=[QT])
                    inproj_feat(wq, m, consq)

                def consg(tt, ps):
                    fw.op("act", lambda e: e.activation(sg[0:tp, tt, :], ps[0:tp, 0:256], AF.Silu), reads=[ps], writes=[sg])
                inproj_tok(OG + h * 256, 256, consg)
                if cfg.get("use_bmd", False):
                    fw.dma("sp", bmt[:], bmd[h].rearrange("p (d q) -> p d q", d=5), reads=[bmbuf], writes=[bmt])
                for m in range(2):
                    loaded = {}

                    def get_piece(pid, m=m, h=h, loaded=loaded):
                        if pid in loaded:
                            return loaded[pid]
                        ksrc, vsrc, nkeys, bufs = pieces[pid]
                        kbuf_ = kst[pid % 2]
                        vbuf_ = vst[pid % 2]
                        fw.dma("sp", kbuf_[:, 0:nkeys], ksrc(h * 2 + m), reads=bufs, writes=[kbuf_])
                        nfull = nkeys // 128
                        if nfull > 0:
                            fw.dma("sp", vbuf_[:, 0:nfull, 0:256], vsrc(h, 0, nfull * 128).rearrange("(kt p) e -> p kt e", p=128),
                                   reads=bufs, writes=[vbuf_])
                        rem = nkeys - nfull * 128
                        if rem > 0:
                            fw.dma("sp", vbuf_[0:rem, nfull, 0:256], vsrc(h, nfull * 128, rem), reads=bufs, writes=[vbuf_])
                        loaded[pid] = (kbuf_, vbuf_)
                        return loaded[pid]

                    nt = len(ktiles)
                    first_for_qs = {}
                    sinfo = {}

                    def emitS(i):
                        pid, idx, nk, diag, qsv = ktiles[i]
                        kbuf_, vbuf_ = get_piece(pid)
                        ps = PS[i % 4]
                        fw.group("pe", [lambda e: e.matmul(ps[0:nk, 0:NQ], lhsT=kbuf_[:, idx * 128:idx * 128 + nk], rhs=QT[:, m, 0:NQ],
                                                           start=True, stop=True)], reads=[kbuf_, QT], writes=[ps])
                        p = PT[i % 3]
                        if diag is None:
                            fw.op("act", lambda e: e.activation(p[0:nk, 0:NQ], ps[0:nk, 0:NQ], AF.Exp, bias=cb15[0:nk, h:h + 1], scale=SCALE),
                                  reads=[ps, cb15], writes=[p])
                        elif isinstance(diag, tuple):
                            dg = dgt[i % 2]
                            slot = bms[diag[1] % 5]
                            fw.dma("sp", slot[:], bm2[h, diag[1]], reads=[bm2buf], writes=[slot])
                            fw.op("dve", lambda e: e.scalar_tensor_tensor(dg[0:nk, 0:NQ], ps[0:nk, 0:NQ], SCALE, slot[0:nk, 0:NQ],
                                                                          ALU.mult, ALU.add), reads=[ps, slot], writes=[dg])
                            fw.op("act", lambda e: e.activation(p[0:nk, 0:NQ], dg[0:nk, 0:NQ], AF.Exp), reads=[dg], writes=[p])
                        else:
                            dg = dgt[i % 2]
                            fw.op("dve", lambda e: e.scalar_tensor_tensor(dg[0:nk, 0:NQ], ps[0:nk, 0:NQ], SCALE, bmt[0:nk, diag, 0:NQ],
                                                                          ALU.mult, ALU.add), reads=[ps, bmt], writes=[dg])
                            fw.op("act", lambda e: e.activation(p[0:nk, 0:NQ], dg[0:nk, 0:NQ], AF.Exp), reads=[dg], writes=[p])
                        sinfo[i] = (p, vbuf_)

                    last_for_qs = {}
                    for i, (pid, idx, nk, diag, qsv) in enumerate(ktiles):
                        for qs in qsv:
                            last_for_qs[qs] = i
                            if qs not in first_for_qs:
                                first_for_qs[qs] = i

                    def emitPV(i):
                        pid, idx, nk, diag, qsv = ktiles[i]
                        p, vbuf_ = sinfo.pop(i)
                        fw.group("pe", [lambda e, qs=qs: e.matmul(PS[4 + qs][0:tp, 0:257], lhsT=p[0:nk, qs * tp:(qs + 1) * tp],
                                                                  rhs=vbuf_[0:nk, idx, 0:257], start=(i == first_for_qs[qs]),
                                                                  stop=(i == last_for_qs[qs])) for qs in qsv],
                                 reads=[p, vbuf_], writes=[PS[4 + qs] for qs in qsv])

                    LA = 2
                    for i in range(nt + LA):
                        if i < nt:
                            emitS(i)
                        if i >= LA:
                            emitPV(i - LA)
                        if i < nt and ktiles[i][1] == LA and (ktiles[i][0] + 1) in pieces:
                            get_piece(ktiles[i][0] + 1)
                    for qs in range(ntt):
                        acc = PS[4 + qs]
                        c_ = smallcol()
                        rc = small[0:tp, c_:c_ + 1]
                        fw.op("dve", lambda e: e.reciprocal(rc, acc[0:tp, 256:257]), reads=[acc], writes=[small])
                        if m == 0:
                            fw.op("dve", lambda e: e.tensor_scalar_mul(att[0:tp, qs, :], acc[0:tp, 0:256], rc), reads=[acc, small], writes=[att])
                        else:
                            fw.op("dve", lambda e: e.tensor_tensor(rc, rc, nlam[0:tp, :], ALU.mult), reads=[small, nlam], writes=[small])
                            fw.op("dve", lambda e: e.scalar_tensor_tensor(att[0:tp, qs, :], acc[0:tp, 0:256], rc, att[0:tp, qs, :],
                                                                          ALU.mult, ALU.add), reads=[acc, small, att], writes=[att])
                    if m == 1:
                        for qs in range(ntt):
                            c2 = smallcol()
                            ssap = small[0:tp, c2:c2 + 1]
                            mab = ma[qs % 2]
                            fw.op("pool", lambda e: e.memset(ssap, 0.0), writes=[small])
                            fw.op("act", lambda e: e.activation(mab[0:tp, :], att[0:tp, qs, :], AF.Square, accum_out=ssap),
                                  reads=[att], writes=[mab, small])
                            rms_rstd(ssap, small, 256, tp)
                            fw.op("dve", lambda e: e.scalar_tensor_tensor(mab[0:tp, :], att[0:tp, qs, :], ssap, sg[0:tp, qs, :],
                                                                          ALU.mult, ALU.mult), reads=[att, small, sg], writes=[mab])

                            def cp(pview, pbuf, qs=qs):
                                evac_alt(qs, mixT[:, 2 * h:2 * h + 2, qs * tp:(qs + 1) * tp], pview, [pbuf], [mixT])
                            transposes_to(mixT, cp, [mab[0:tp, j * 128:(j + 1) * 128] for j in range(2)], [mab], tp, 128)

            if stage <= 5:
                return
            fw.alias(hres, [sz, xs_tok])
            for oc in range(16):
                col0 = oc * 128
                w = next_w()
                wv = w[:, :, :].rearrange("p a b -> p (a b)").rearrange("p (a b) -> p a b", a=32)
                fw.dma("sp", wv, Wob[:, col0:col0 + 128].rearrange("(kc p) c -> p kc c", p=128), reads=[Wobuf], writes=[w])
                for tt in range(ntt):
                    ps = next_ps()
                    fw.group("pe", [lambda e, kc=kc: e.matmul(ps[0:tp, 0:128], lhsT=mixT[:, kc, tt * tp:(tt + 1) * tp], rhs=wv[:, kc, :],
                                                              start=(kc == 0), stop=(kc == 31)) for kc in range(32)],
                             reads=[mixT, w], writes=[ps])
                    evac_alt(tt, hres[0:tp, tt, col0:col0 + 128], ps[0:tp, 0:128], [ps], [hres])
            yout = cfg["yout"]
            for tt in range(ntt):
                xl = xld[tt % 2]
                fw.dma("sp", xl[0:tp, :], xsrc[tt * tp:(tt + 1) * tp, :], writes=[xl])
                fw.op("dve", lambda e: e.tensor_tensor(xl[0:tp, :], xl[0:tp, :], hres[0:tp, tt, :], ALU.add), reads=[xl, hres], writes=[xl])
                c_ = smallcol()
                ssap = small[0:tp, c_:c_ + 1]
                fw.op("pool", lambda e: e.memset(ssap, 0.0), writes=[small])
                fw.op("act", lambda e: e.activation(hres[0:tp, tt, :], xl[0:tp, :], AF.Square, accum_out=ssap), reads=[xl], writes=[hres, small])
                rms_rstd(ssap, small, D, tp)
                fw.op("dve", lambda e: e.scalar_tensor_tensor(xl[0:tp, :], xl[0:tp, :], ssap, fnw_bc[0:tp, :], ALU.mult, ALU.mult),
                      reads=[xl, small, fnw_bc], writes=[xl])
                fw.dma("pool", yout[tt * tp:(tt + 1) * tp, :], xl[0:tp, :], reads=[xl])

        def write_state_outputs(conv_out, ssm_out):
            enter_ssd_phase()
            for ct in range(24):
                fw.dma("pool", conv_out[:, ct * 128:(ct + 1) * 128].rearrange("j p -> p j"), tails[:, ct, :], reads=[tails],
                       allow_slow_non_contiguous=True)
            for i in range(16):
                ps = next_ps()
                fw.group("pe", [lambda e: e.transpose(ps[:, 0:128], hT[:, i * 128:(i + 1) * 128], idf[:, :])], reads=[hT, idf], writes=[ps])
                st = t1[i % 2]
                evac_alt(i, st[:, 0:128], ps[:, 0:128], [ps], [st])
                fw.dma("pool", ssm_out[i * 128:(i + 1) * 128, :], st[:, 0:128], reads=[st])

        kvbufs = [Buf(None, "kv%d" % i) for i in range(NBLK)]
        for mstep in range(NBLK // 4 if stage > 0 else 0):
            for bi_ in range(4):
                blk = 4 * mstep + bi_
                tok0 = blk * 512
                cfg = dict(tp=128, ntt=4, xsrc=xp[tok0:tok0 + 512, :], KT=KTd, V=Vd, kout=k_p[tok0:tok0 + 512, :],
                           vout=v_p[tok0:tok0 + 512, :], tok0=tok0, kvbuf=kvbufs[blk], phases="ABCD", blend_i=bi_)
                process_block(cfg)
            nkt = 16 * (mstep + 1)
            ktiles = []
            for kt in range(nkt):
                t_ = kt - 16 * mstep
                ktiles.append((kt // 16, kt % 16, 128, (None if t_ < -1 else ("dram", t_ + 1)), [0, 1, 2, 3]))
            pieces = {}
            for pid in range(mstep + 1):
                k0 = pid * 2048
                pieces[pid] = (lambda hm, k0=k0: KTd[hm, :, k0:k0 + 2048],
                               lambda h, o, n, k0=k0: Vd[k0 + o:k0 + o + n, h * 256:(h + 1) * 256],
                               2048, kvbufs[4 * pid:4 * pid + 4])
            cfg = dict(tp=128, ntt=4, xsrc=xown[mstep * 512:(mstep + 1) * 512, :], phases="AEF", move_own=True,
                       ktiles=ktiles, pieces=pieces, yout=y_p[mstep * 512:(mstep + 1) * 512, :])
            process_block(cfg)
        if stage > 5:
            write_state_outputs(conv_p, ssm_p)

        if sample:
            fw.barrier()
            skv = Buf(None, "skv")
            for kt in range(16):
                xl = xld[kt % 2]
                fw.dma("sp", xl[:, :], ck[kt * 128:(kt + 1) * 128, :], writes=[xl])
                fw.op("act", lambda e: e.copy(xn[:, :], xl[:, :]), reads=[xl], writes=[xn])
                kts = mixT
                for half in range(2):
                    def cp(pview, pbuf, half=half):
                        evac_alt(half, mixT[:, half * 8:(half + 1) * 8, 0:128], pview, [pbuf], [mixT])
                    transposes_to(mixT, cp, [xn[:, (half * 8 + j) * 128:(half * 8 + j + 1) * 128] for j in range(8)], [xn], 128, 128)
                fw.dma("pool", KTsd[:, :, kt * 128:(kt + 1) * 128].rearrange("a d t -> d a t"), mixT[:, 0:16, 0:128],
                       reads=[mixT], writes=[skv])
                xl2 = xld[(kt + 1) % 2]
                fw.dma("sp", xl2[:, :], cv[kt * 128:(kt + 1) * 128, :], writes=[xl2])
                fw.op("dve", lambda e: e.tensor_copy(xs_tok[:, 0, :], xl2[:, :]), reads=[xl2], writes=[xs_tok])
                fw.dma("pool", Vsd[kt * 128:(kt + 1) * 128, :], xs_tok[:, 0, :], reads=[xs_tok], writes=[skv])
            fw.dma("sp", tails[:], cconv, writes=[tails])
            enter_ssd_phase()
            for i in range(16):
                st = t1[i % 2]
                fw.dma("sp", st[:, 0:128], sst[i * 128:(i + 1) * 128, :], writes=[st])
                ps = next_ps()
                fw.group("pe", [lambda e: e.transpose(ps[:, 0:128], st[:, 0:128], idf[:, :])], reads=[st, idf], writes=[ps])
                evac_alt(i, hT[:, i * 128:(i + 1) * 128], ps[:, 0:128], [ps], [hT])
            fw.op("act", lambda e: e.copy(hTb[:, :], hT[:, :]), reads=[hT], writes=[hTb])
            ktiles = []
            for kt in range(16):
                ktiles.append((kt // 16, kt % 16, 128, (0 if kt == 15 else None), [0]))
            ktiles.append((1, 0, 32, 1, [0]))
            pieces = {
                0: (lambda hm: KTsd[hm, :, 0:2048], lambda h, o, n: Vsd[o:o + n, h * 256:(h + 1) * 256], 2048, [skv]),
                1: (lambda hm: KTsd[hm, :, 2048:2080], lambda h, o, n: Vsd[2048 + o:2048 + o + n, h * 256:(h + 1) * 256], 32, [skv]),
            }
            cfg = dict(tp=32, ntt=1, xsrc=xsm, KT=KTsd, V=Vsd, kout=k_s, vout=v_s, tok0=PAST, kvbuf=skv, ktiles=ktiles,
                       pieces=pieces, yout=y_s, use_bmd=True)
            process_block(cfg)
            write_state_outputs(conv_s, ssm_s)

        fw.barrier()
        print("instructions:", fw.ninstr)
    return nc


def host_consts():
    ident = np.eye(128, dtype=np.float32)
    triu = np.triu(np.ones((128, 128), np.float32))
    anti = np.ascontiguousarray(ident[::-1])
    idx = np.arange(1152)
    rel = (511 - idx).astype(np.int32)
    b = rel_bucket_np(rel)
    oh = np.zeros((32, 1152), np.float32)
    oh[b, idx] = 1.0
    mask = np.zeros((128, 5, 512), np.float32)
    k = np.arange(128)[:, None]
    q = np.arange(512)[None, :]
    for d in range(5):
        vis = ((128 * (d - 1) + k) // 64) <= (q // 64)
        mask[:, d, :] = np.where(vis, 0.0, NEG)
    return dict(c_ident=ident, c_triu=triu, c_anti=anti, c_oh=oh, c_mask=mask)


def make_in_maps(inp, NBLK=32):
    f = lambda a: np.ascontiguousarray(np.asarray(a, dtype=np.float32))
    SEQ = NBLK * 512
    consts = host_consts()
    common = dict(
        relb=f(inp["rel_bias"]),
        nw=f(np.asarray(inp["norm_w"])[0].reshape(16, 128).T),
        w_in=f(inp["w_in"][0]), w_out=f(inp["w_out"][0]),
        lamv=f(np.stack([np.asarray(inp["lambda_q1"])[0], np.asarray(inp["lambda_k1"])[0],
                         np.asarray(inp["lambda_q2"])[0], np.asarray(inp["lambda_k2"])[0]])),
        sublnT=f(np.asarray(inp["subln_w"])[0].reshape(2, 128).T),
        convw=f(np.asarray(inp["conv_w"])[0].reshape(4, 24, 128).transpose(2, 1, 0)),
        convb=f(np.asarray(inp["conv_b"])[0].reshape(24, 128).T),
        dtb=f(inp["dt_bias"]), alog=f(inp["A_log"]), dskip=f(inp["D_skip"]),
        ssmnwT=f(np.asarray(inp["ssm_norm_w"])[0].reshape(16, 128).T),
        fnw=f(np.asarray(inp["final_norm_w"]).reshape(1, 2048)),
        **consts)
    maps = []
    NM = NBLK // 4
    xpa = np.asarray(inp["x_prompt"])
    for c in range(8):
        b, j = c // 4, c % 4
        m = dict(common)
        m["xp"] = f(xpa[b, :SEQ])
        m["xown"] = f(xpa[b, :SEQ].reshape(NM, 4, 512, 2048)[:, j].reshape(NM * 512, 2048))
        sel = np.zeros((17, 7), np.float32)
        for ti in range(17):
            delta = 128 * (ti - 1) - 512 * j
            d = 5 if delta <= -256 else (6 if delta >= 512 else delta // 128 + 1)
            sel[ti, d] = 1.0
        m["selw"] = f(np.broadcast_to(sel.reshape(1, 119), (128, 119)))
        ws = np.zeros((1, 4), np.float32)
        ws[0, j] = 1.0
        m["wsel"] = f(np.broadcast_to(ws, (128, 4)))
        m["xsm"] = f(np.asarray(inp["x_sample"])[c])
        m["ck"] = f(np.asarray(inp["cache_k"])[0, c].reshape(PAST, 2048))
        m["cv"] = f(np.asarray(inp["cache_v"])[0, c].reshape(PAST, 2048))
        m["cconv"] = f(np.asarray(inp["cache_conv"])[0, c].reshape(3, 24, 128).transpose(2, 1, 0))
        m["sst"] = f(np.asarray(inp["state_ssm"])[0, c].reshape(2048, 128))
        maps.append(m)
    return maps


_NC_CACHE = {}


def run(inp, NBLK=32, sample=True, stage=99):
    key = (NBLK, sample, stage)
    if key not in _NC_CACHE:
        _NC_CACHE[key] = build(NBLK, sample, stage)
    nc = _NC_CACHE[key]
    maps = make_in_maps(inp, NBLK)
    res = run_bass_kernel_spmd(nc, maps, core_ids=list(range(8)))
    return res.results


def assemble_y(r, NBLK):
    NM = NBLK // 4
    y = np.zeros((2, NBLK * 512, 2048), np.float32)
    yv = y.reshape(2, NM, 4, 512, 2048)
    for c in range(8):
        b, j = c // 4, c % 4
        yv[b, :, j] = r[c]["y_p"].reshape(NM, 512, 2048)
    return y


def kernel(**inp):
    r = run(inp, 32, True)
    S = 16384
    y_prompt = assemble_y(r, 32)
    y_sample = np.stack([r[c]["y_s"] for c in range(8)]).reshape(8, 32, 2048)
    k_prompt = np.stack([r[4 * b]["k_p"] for b in range(2)]).reshape(1, 2, S, 8, 2, 128)
    v_prompt = np.stack([r[4 * b]["v_p"] for b in range(2)]).reshape(1, 2, S, 8, 256)
    conv_prompt = np.stack([r[4 * b]["conv_p"] for b in range(2)]).reshape(1, 2, 3, 3072)
    ssm_prompt = np.stack([r[4 * b]["ssm_p"] for b in range(2)]).reshape(1, 2, 32, 64, 128)
    k_sample = np.stack([r[c]["k_s"] for c in range(8)]).reshape(1, 8, 32, 8, 2, 128)
    v_sample = np.stack([r[c]["v_s"] for c in range(8)]).reshape(1, 8, 32, 8, 256)
    conv_sample = np.stack([r[c]["conv_s"] for c in range(8)]).reshape(1, 8, 3, 3072)
    ssm_sample = np.stack([r[c]["ssm_s"] for c in range(8)]).reshape(1, 8, 32, 64, 128)
    return tuple(np.ascontiguousarray(a.astype(np.float32)) for a in
                 (y_prompt, y_sample, k_prompt, v_prompt, conv_prompt, ssm_prompt, k_sample, v_sample, conv_sample, ssm_sample))
```

```python
import math
import numpy as np
from contextlib import ExitStack
import concourse.bass as bass
import concourse.mybir as mybir
from concourse.bass_utils import run_bass_kernel_spmd

F32 = mybir.dt.float32
BF16 = mybir.dt.bfloat16
ALU = mybir.AluOpType
AF = mybir.ActivationFunctionType
AX = mybir.AxisListType

D = 2048
KC = 16
NCOL = 13344
OQ, OK_, OV, OG, OZ, OX, ODT = 0, 2048, 4096, 6144, 8192, 10240, 13312
EPS = 1e-5
LAM0 = 0.8 - 0.6 * math.exp(-0.3 * 0)
SCALE = 128 ** -0.5
PAST = 2048
NEG = -30000.0


class Buf:
    __slots__ = ("t", "w", "r", "name")

    def __init__(self, t=None, name=""):
        self.t = t
        self.w = None
        self.r = {}
        self.name = name

    def __getitem__(self, k):
        return self.t[k]


class FW:
    def __init__(self, nc, es, ndma=16):
        self.nc = nc
        self.eng = {"pe": nc.tensor, "act": nc.scalar, "dve": nc.vector, "pool": nc.gpsimd, "sp": nc.sync}
        self.sem = {}
        self.cnt = {}
        self.semobj = {}
        for k in self.eng:
            self.sem[k] = es.enter_context(nc.semaphore("s_" + k))
            self.cnt[k] = 0
            self.semobj[k] = self.sem[k]
        self.waited = {}
        self.dring = {}
        self.dpos = {}
        for q in ("sp", "pool"):
            self.dring[q] = [[es.enter_context(nc.semaphore("d_%s%d" % (q, i))), 0] for i in range(ndma)]
            self.dpos[q] = 0
            for i, s in enumerate(self.dring[q]):
                self.semobj[("d", q, i)] = s[0]
        self.ninstr = 0

    def _wait(self, e, key, val):
        if val <= 0:
            return
        if e == "pe" and key == "pe":
            return
        if self.waited.get((e, key), 0) >= val:
            return
        self.eng[e].wait_ge(self.semobj[key], val)
        self.waited[(e, key)] = val
        self.ninstr += 1

    def _deps(self, e, reads, writes):
        deps = {}
        for b in reads:
            if b.w is not None:
                k, v = b.w
                if deps.get(k, 0) < v:
                    deps[k] = v
        for b in writes:
            if b.w is not None:
                k, v = b.w
                if deps.get(k, 0) < v:
                    deps[k] = v
            for k, v in b.r.items():
                if deps.get(k, 0) < v:
                    deps[k] = v
        for k, v in deps.items():
            self._wait(e, k, v)

    def _mark(self, tok, reads, writes):
        k, v = tok
        for b in reads:
            if b.r.get(k, 0) < v:
                b.r[k] = v
        for b in writes:
            b.w = tok
            b.r = {}

    def op(self, e, fn, reads=(), writes=()):
        return self.group(e, [fn], reads, writes)

    def group(self, e, fns, reads=(), writes=()):
        self._deps(e, reads, writes)
        ins = None
        for fn in fns:
            ins = fn(self.eng[e])
            self.ninstr += 1
        self.cnt[e] += 1
        ins.then_inc(self.sem[e], 1)
        tok = (e, self.cnt[e])
        self._mark(tok, reads, writes)
        return tok

    def dma(self, q, out, in_, reads=(), writes=(), **kw):
        self._deps(q, reads, writes)
        ring = self.dring[q]
        i = self.dpos[q]
        self.dpos[q] = (i + 1) % len(ring)
        slot = ring[i]
        key = ("d", q, i)
        self._wait(q, key, slot[1])
        ins = self.eng[q].dma_start(out=out, in_=in_, **kw)
        slot[1] += 16
        ins.then_inc(slot[0], 16)
        self.ninstr += 1
        tok = (key, slot[1])
        self._mark(tok, reads, writes)
        return tok

    def alias(self, dst, srcs):
        for s in srcs:
            if s.w is not None:
                k, v = s.w
                if dst.r.get(k, 0) < v:
                    dst.r[k] = v
            for k, v in s.r.items():
                if dst.r.get(k, 0) < v:
                    dst.r[k] = v

    def barrier(self):
        for e in self.eng:
            for k in self.eng:
                if k != e:
                    self._wait(e, k, self.cnt[k])
            for q in self.dring:
                for i, slot in enumerate(self.dring[q]):
                    self._wait(e, ("d", q, i), slot[1])


def rel_bucket_np(rel):
    half, max_exact = 16, 8
    ret = np.where(rel > 0, half, 0)
    n = np.abs(rel)
    nf = np.maximum(n, 1).astype(np.float32)
    large = max_exact + (np.log(nf / np.float32(max_exact)) / np.float32(math.log(128 / max_exact))
                         * np.float32(half - max_exact)).astype(np.int32)
    large = np.minimum(large, half - 1)
    return ret + np.where(n < max_exact, n, large)


def build(NBLK=32, sample=True, stage=99):
    nc = bass.Bass("TRN2", target_bir_lowering=False)
    SEQ = NBLK * 512

    def din(name, shape, dt=F32):
        return nc.dram_tensor(name, list(shape), dt, kind="ExternalInput").ap()

    def dout(name, shape, dt=F32):
        return nc.dram_tensor(name, list(shape), dt, kind="ExternalOutput").ap()

    def dscr(name, shape, dt):
        return nc.dram_tensor(name, list(shape), dt).ap()

    xp = din("xp", [SEQ, D])
    NM = NBLK // 4
    xown = din("xown", [NM * 512, D])
    selw = din("selw", [128, 17 * 7])
    wsel = din("wsel", [128, 4])
    xsm = din("xsm", [32, D])
    ck = din("ck", [PAST, D])
    cv = din("cv", [PAST, D])
    cconv = din("cconv", [128, 24, 3])
    sst = din("sst", [2048, 128])
    relb = din("relb", [32, 8])
    nw = din("nw", [128, 16])
    w_in = din("w_in", [D, NCOL])
    w_out = din("w_out", [4096, D])
    lamv = din("lamv", [4, 128])
    sublnT = din("sublnT", [128, 2])
    convw = din("convw", [128, 24, 4])
    convb = din("convb", [128, 24])
    dtb = din("dtb", [1, 32])
    alog = din("alog", [1, 32])
    dskip = din("dskip", [1, 32])
    ssmnwT = din("ssmnwT", [128, 16])
    fnw = din("fnw", [1, D])
    c_ident = din("c_ident", [128, 128])
    c_triu = din("c_triu", [128, 128])
    c_anti = din("c_anti", [128, 128])
    c_oh = din("c_oh", [32, 1152])
    c_mask = din("c_mask", [128, 5, 512])

    y_p = dout("y_p", [NM * 512, D])
    k_p = dout("k_p", [SEQ, D])
    v_p = dout("v_p", [SEQ, D])
    conv_p = dout("conv_p", [3, 3072])
    ssm_p = dout("ssm_p", [2048, 128])
    y_s = dout("y_s", [32, D])
    k_s = dout("k_s", [32, D])
    v_s = dout("v_s", [32, D])
    conv_s = dout("conv_s", [3, 3072])
    ssm_s = dout("ssm_s", [2048, 128])

    Wb = dscr("Wb", [D, NCOL], BF16)
    Wob = dscr("Wob", [4096, D], BF16)
    KTd = dscr("KTd", [16, 128, SEQ], BF16)
    Vd = dscr("Vd", [SEQ, D], BF16)
    KTsd = dscr("KTsd", [16, 128, PAST + 128], BF16)
    Vsd = dscr("Vsd", [PAST + 128, D], BF16)
    bvs = dscr("bvs", [8, 1152], F32)
    bmd = dscr("bmd", [8, 128, 5 * 512], F32)
    bm2 = dscr("bm2", [8, 17, 128, 512], F32)

    es = ExitStack()
    with es:
        fw = FW(nc, es)
        ARENA = 207 * 1024
        arena = es.enter_context(nc.sbuf_tensor("arena", [128, ARENA // 4], F32))
        apos = [0]

        def alloc(nbytes):
            o = apos[0]
            apos[0] = o + ((nbytes + 31) // 32) * 32
            assert apos[0] <= ARENA, ("SBUF arena overflow", apos[0])
            return o

        def view(off, shape, dt, name=""):
            n = int(np.prod(shape[1:]))
            if dt == F32:
                ap = arena[0:shape[0], off // 4: off // 4 + n]
            else:
                ap = arena[0:shape[0], off // 4: off // 4 + (n + 1) // 2].bitcast(BF16)[:, 0:n]
            if len(shape) == 3:
                ap = ap.rearrange("p (a b) -> p a b", a=shape[1])
            elif len(shape) == 4:
                ap = ap.rearrange("p (a b c) -> p a b c", a=shape[1], b=shape[2])
            return Buf(ap, name)

        def sb(name, shape, dt):
            n = int(np.prod(shape[1:])) * (4 if dt == F32 else 2)
            return view(alloc(n), shape, dt, name)

        PS = [Buf(es.enter_context(nc.psum_tensor("ps%d" % i, [128, 512], F32)), "ps%d" % i) for i in range(8)]
        PSb = [p.t[:].bitcast(BF16) for p in PS]

        idf = sb("idf", [128, 128], F32)
        idb = sb("idb", [128, 128], BF16)
        triu = sb("triu", [128, 128], F32)
        onesf = sb("onesf", [128, 128], F32)
        nwt = sb("nwt", [128, 16], F32)
        cw = sb("cw", [128, 24, 4], F32)
        cb = sb("cb", [128, 24], F32)
        dtb_bc = sb("dtb_bc", [128, 32], F32)
        A_bc = sb("A_bc", [128, 32], F32)
        D_bc = sb("D_bc", [128, 32], F32)
        fnw_bc = sb("fnw_bc", [128, D], F32)
        cb15 = sb("cb15", [128, 8], F32)
        nlam = sb("nlam", [128, 1], F32)
        hT = sb("hT", [128, 2048], F32)
        hTb = sb("hTb", [128, 2048], BF16)
        tails = sb("tails", [128, 24, 3], F32)
        small = sb("small", [128, 64], F32)
        selt = sb("selt", [128, 17 * 7], F32)
        wselt = sb("wselt", [128, 4], F32)
        MAIN0 = apos[0]

        fw.dma("sp", idf[:], c_ident, writes=[idf])
        fw.dma("sp", triu[:], c_triu, writes=[triu])
        fw.dma("sp", nwt[:], nw, writes=[nwt])
        fw.dma("sp", cw[:], convw, writes=[cw])
        fw.dma("sp", cb[:], convb, writes=[cb])
        fw.dma("sp", dtb_bc[:], dtb.partition_broadcast(128), writes=[dtb_bc])
        fw.dma("sp", A_bc[:], alog.partition_broadcast(128), writes=[A_bc])
        fw.dma("sp", D_bc[:], dskip.partition_broadcast(128), writes=[D_bc])
        fw.dma("sp", fnw_bc[:], fnw.partition_broadcast(128), writes=[fnw_bc])
        fw.dma("sp", cb15[:], relb[15:16, :].partition_broadcast(128), writes=[cb15])
        fw.dma("sp", selt[:], selw, writes=[selt])
        fw.dma("sp", wselt[:], wsel, writes=[wselt])
        fw.op("dve", lambda e: e.tensor_copy(idb[:], idf[:]), reads=[idf], writes=[idb])
        fw.op("pool", lambda e: e.memset(onesf[:], 1.0), writes=[onesf])
        fw.op("pool", lambda e: e.memset(hT[:], 0.0), writes=[hT])
        fw.op("pool", lambda e: e.memset(hTb[:], 0.0), writes=[hTb])
        fw.op("pool", lambda e: e.memset(tails[:], 0.0), writes=[tails])
        fw.op("act", lambda e: e.activation(A_bc[:], A_bc[:], AF.Exp), reads=[A_bc], writes=[A_bc])
        fw.op("dve", lambda e: e.tensor_scalar_mul(A_bc[:], A_bc[:], -1.0), reads=[A_bc], writes=[A_bc])

        apos[0] = MAIN0
        lv = sb("lv", [128, 4, 128], F32)
        lp = sb("lp", [128, 2, 128], F32)
        ls = sb("ls", [128, 2], F32)
        for i in range(4):
            fw.dma("sp", lv[:, i, :], lamv[i:i + 1, :].partition_broadcast(128), writes=[lv])
        fw.op("dve", lambda e: e.tensor_tensor(lp[:, 0, :], lv[:, 0, :], lv[:, 1, :], ALU.mult), reads=[lv], writes=[lp])
        fw.op("dve", lambda e: e.tensor_tensor(lp[:, 1, :], lv[:, 2, :], lv[:, 3, :], ALU.mult), reads=[lv, lp], writes=[lp])
        fw.op("dve", lambda e: e.reduce_sum(ls[:], lp[:], AX.X), reads=[lp], writes=[ls])
        fw.op("act", lambda e: e.activation(ls[:], ls[:], AF.Exp), reads=[ls], writes=[ls])
        fw.op("dve", lambda e: e.tensor_tensor(nlam[:], ls[:, 1:2], ls[:, 0:1], ALU.subtract), reads=[ls], writes=[nlam])
        fw.op("dve", lambda e: e.tensor_scalar_add(nlam[:], nlam[:], -LAM0), reads=[nlam], writes=[nlam])

        oh = sb("oh", [32, 1152], F32)
        tab = sb("tab", [32, 8], F32)
        bvsb = sb("bvsb", [8, 1152], F32)
        anti = sb("anti", [128, 128], F32)
        msk = sb("msk", [128, 5, 512], F32)
        bvd = Buf(None, "bvd")
        fw.dma("sp", oh[:], c_oh, writes=[oh])
        fw.dma("sp", tab[:], relb, writes=[tab])
        fw.dma("sp", anti[:], c_anti, writes=[anti])
        fw.dma("sp", msk[:], c_mask, writes=[msk])
        for j in range(3):
            fw.group("pe", [lambda e, j=j: e.matmul(PS[j][0:8, 0:384], lhsT=tab[:, :], rhs=oh[:, j * 384:(j + 1) * 384],
                                                   start=True, stop=True)], reads=[tab, oh], writes=[PS[j]])
            fw.op("dve", lambda e, j=j: e.tensor_copy(bvsb[:, j * 384:(j + 1) * 384], PS[j][0:8, 0:384]),
                  reads=[PS[j]], writes=[bvsb])
        fw.dma("sp", bvs, bvsb[:], reads=[bvsb], writes=[bvd])
        trev = [sb("trev%d" % i, [128, 512], F32) for i in range(2)]
        tfl = [sb("tfl%d" % i, [128, 512], F32) for i in range(2)]
        bmbuf = Buf(None, "bmd")
        it = 0
        for h in range(8):
            for d in range(5):
                delta = 128 * (d - 1)
                tr = trev[it % 2]
                tf = tfl[it % 2]
                pp = PS[3 + it % 2]
                src = bass.AP(tensor=bvs.tensor, offset=h * 1152 + 384 - delta, ap=[[1, 128], [1, 512]])
                fw.dma("sp", tr[:], src, reads=[bvd], writes=[tr])
                fw.group("pe", [lambda e, tr=tr, pp=pp: e.matmul(pp[:, :], lhsT=anti[:, :], rhs=tr[:, :], start=True, stop=True)],
                         reads=[anti, tr], writes=[pp])
                fw.op("dve", lambda e, tf=tf, pp=pp, d=d: e.tensor_tensor(tf[:], pp[:, :], msk[:, d, :], ALU.add),
                      reads=[pp, msk], writes=[tf])
                fw.dma("pool", bmd[h, :, d * 512:(d + 1) * 512], tf[:], reads=[tf], writes=[bmbuf])
                it += 1

        Tt = sb("Tt", [128, 5, 512], F32)
        s56 = sb("s56", [128, 2], F32)
        bm2buf = Buf(None, "bm2")
        it = 0
        for h in range(8):
            fw.dma("sp", Tt[:], bmd[h].rearrange("p (d q) -> p d q", d=5), reads=[bmbuf], writes=[Tt])
            for t_ in range(17):
                o = tfl[it % 2]
                c0 = t_ * 7
                fw.op("dve", lambda e: e.tensor_scalar_mul(o[:], Tt[:, 0, :], selt[:, c0:c0 + 1]), reads=[Tt, selt], writes=[o])
                for d in range(1, 5):
                    fw.op("dve", lambda e, d=d: e.scalar_tensor_tensor(o[:], Tt[:, d, :], selt[:, c0 + d:c0 + d + 1], o[:], ALU.mult, ALU.add),
                          reads=[Tt, selt, o], writes=[o])
                fw.op("dve", lambda e: e.tensor_tensor(s56[:, 0:1], selt[:, c0 + 5:c0 + 6], cb15[:, h:h + 1], ALU.mult),
                      reads=[selt, cb15], writes=[s56])
                fw.op("dve", lambda e: e.scalar_tensor_tensor(s56[:, 1:2], selt[:, c0 + 6:c0 + 7], NEG, s56[:, 0:1], ALU.mult, ALU.add),
                      reads=[selt, s56], writes=[s56])
                fw.op("dve", lambda e: e.tensor_scalar_add(o[:], o[:], s56[:, 1:2]), reads=[o, s56], writes=[o])
                fw.dma("pool", bm2[h, t_], o[:], reads=[o], writes=[bm2buf])
                it += 1

        wf = [sb("wf%d" % i, [128, 1668], F32) for i in range(3)]
        wb_ = [sb("wb%d" % i, [128, 1668], BF16) for i in range(3)]
        sT = sb("sT", [128, 2], F32)
        snT = sb("snT", [128, 16], F32)
        fw.dma("sp", sT[:], sublnT, writes=[sT])
        fw.dma("sp", snT[:], ssmnwT, writes=[snT])
        fw.op("dve", lambda e: e.tensor_scalar_mul(sT[:], sT[:], 1.0 - LAM0), reads=[sT], writes=[sT])
        Wbuf = Buf(None, "Wb")
        Wobuf = Buf(None, "Wob")
        it = 0
        for kc in range(16):
            for pc in range(8):
                a, b = wf[it % 3], wb_[it % 3]
                c0 = pc * 1668
                fw.dma("sp", a[:], w_in[kc * 128:(kc + 1) * 128, c0:c0 + 1668], writes=[a])
                if it % 2 == 0:
                    fw.op("act", lambda e, a=a, b=b, kc=kc: e.activation(b[:], a[:], AF.Copy, scale=nwt[:, kc:kc + 1]),
                          reads=[a, nwt], writes=[b])
                else:
                    fw.op("dve", lambda e, a=a, b=b, kc=kc: e.tensor_scalar_mul(b[:], a[:], nwt[:, kc:kc + 1]),
                          reads=[a, nwt], writes=[b])
                fw.dma("pool", Wb[kc * 128:(kc + 1) * 128, c0:c0 + 1668], b[:], reads=[b], writes=[Wbuf])
                it += 1
        for kc in range(32):
            for pc in range(2):
                a, b = wf[it % 3], wb_[it % 3]
                c0 = pc * 1024
                sc = sT[:, kc % 2:kc % 2 + 1] if kc < 16 else snT[:, kc - 16:kc - 15]
                fw.dma("sp", a[:, 0:1024], w_out[kc * 128:(kc + 1) * 128, c0:c0 + 1024], writes=[a])
                if it % 2 == 0:
                    fw.op("act", lambda e, a=a, b=b, sc=sc: e.activation(b[:, 0:1024], a[:, 0:1024], AF.Copy, scale=sc),
                          reads=[a, sT, snT], writes=[b])
                else:
                    fw.op("dve", lambda e, a=a, b=b, sc=sc: e.tensor_scalar_mul(b[:, 0:1024], a[:, 0:1024], sc),
                          reads=[a, sT, snT], writes=[b])
                fw.dma("pool", Wob[kc * 128:(kc + 1) * 128, c0:c0 + 1024], b[:, 0:1024], reads=[b], writes=[Wobuf])
                it += 1
        fw.barrier()

        apos[0] = MAIN0
        xT = sb("xT", [128, 16, 512], BF16)
        xld = [sb("xld%d" % i, [128, D], F32) for i in range(2)]
        xn = sb("xn", [128, D], BF16)
        wst = [sb("wst%d" % i, [128, 16, 256], BF16) for i in range(3)]
        kf = [sb("kf%d" % i, [128, 256], F32) for i in range(2)]
        kb = [sb("kb%d" % i, [128, 256], BF16) for i in range(2)]
        KTst = [sb("KTst%d" % i, [128, 2, 512], BF16) for i in range(2)]
        o_sz = alloc(16384)
        o_xs = alloc(16384)
        sz = view(o_sz, [128, 4, 2048], BF16, "sz")
        xs_tok = view(o_xs, [128, 4, 2048], BF16, "xs_tok")
        hres = view(o_sz, [128, 4, 2048], F32, "hres")
        dts = sb("dts", [128, 4, 32], F32)
        av = sb("av", [128, 4, 32], F32)
        dtmp = [sb("dtmp%d" % i, [128, 32], F32) for i in range(3)]
        acst = sb("acst", [128, 32], F32)
        tot = sb("tot", [128, 32], F32)
        ea = sb("ea", [128, 32], F32)
        te = sb("te", [128, 32], F32)
        dec = sb("dec", [128, 32], F32)
        o_mixT = apos[0]
        mixT = sb("mixT", [128, 32, 512], BF16)
        hT_own = view(o_mixT, [128, 2048], F32, "hT_own")
        hTb_own = view(o_mixT + 8192, [128, 2048], BF16, "hTb_own")
        tails_own = view(o_mixT + 12288, [128, 24, 3], F32, "tails_own")
        U0 = apos[0]
        BT = sb("BT", [128, 4, 512], BF16)
        CT = sb("CT", [128, 4, 512], BF16)
        B_tok = sb("B_tok", [128, 4, 512], BF16)
        raw = [sb("raw%d" % i, [128, 516], F32) for i in range(2)]
        cva = [sb("cva%d" % i, [128, 512], F32) for i in range(1)] * 2
        csb = [sb("csb%d" % i, [128, 512], BF16) for i in range(2)]
        xdt = sb("xdt", [128, 2048], BF16)
        xdte = sb("xdte", [128, 2048], BF16)
        Rb = [sb("R%d" % i, [128, 512], F32) for i in range(2)]
        segs = [sb("segs%d" % i, [128, 512], F32) for i in range(2)]
        MT = [sb("MT%d" % i, [128, 512], BF16) for i in range(2)]
        cbm = sb("cbm", [128, 4, 128], F32)
        t1 = [sb("t1_%d" % i, [128, 512], F32) for i in range(2)]
        t2 = [sb("t2_%d" % i, [128, 512], F32) for i in range(1)] * 2
        ym = [sb("ym%d" % i, [128, 512], BF16) for i in range(2)]
        U1 = apos[0]
        ssd_bufs = [BT, CT, B_tok, xdt, xdte, cbm] + raw + cva + csb + Rb + segs + MT + t1 + t2 + ym
        apos[0] = U0
        kst = [sb("kst%d" % i, [128, 2048], BF16) for i in range(2)]
        vst = [sb("vst%d" % i, [128, 16, 257], BF16) for i in range(2)]
        QT = sb("QT", [128, 2, 512], BF16)
        sg = sb("sg", [128, 4, 256], BF16)
        o_bmt0 = apos[0]
        bmt = sb("bmt", [128, 5, 512], F32)
        bms = [view(o_bmt0 + i * 2048, [128, 512], F32, "bms%d" % i) for i in range(5)]
        PT = [sb("PT%d" % i, [128, 512], BF16) for i in range(3)]
        dgt = [sb("dgt%d" % i, [128, 512], F32) for i in range(1)] * 2
        att = sb("att", [128, 4, 256], F32)
        ma = [sb("ma%d" % i, [128, 256], BF16) for i in range(1)] * 2
        o_bmt = apos[0] - 0
        att_bufs = kst + vst + [QT, sg, bmt, att] + PT + dgt + ma + bms
        apos[0] = max(U1, apos[0])

        def enter_ssd_phase():
            for b_ in ssd_bufs:
                fw.alias(b_, att_bufs)

        def enter_att_phase():
            for b_ in att_bufs:
                fw.alias(b_, ssd_bufs)
        print("SBUF bytes/partition used:", apos[0])


        rr = {"ps": 0, "w": 0, "tr": 0, "sm": 0}

        def next_ps():
            rr["ps"] = (rr["ps"] + 1) % 4
            return PS[rr["ps"]]

        def next_tr():
            rr["tr"] = (rr["tr"] + 1) % 2
            return 4 + rr["tr"]

        def next_w():
            rr["w"] = (rr["w"] + 1) % 3
            return wst[rr["w"]]

        def smallcol(n=1):
            o = rr["sm"]
            if o + n > 64:
                o = 0
            rr["sm"] = o + n
            return o

        def load_w(col0, ncols):
            w = next_w()
            fw.dma("sp", w[:, :, 0:ncols], Wb[:, col0:col0 + ncols].rearrange("(kc p) c -> p kc c", p=128),
                   reads=[Wbuf], writes=[w])
            return w

        def evac_alt(idx, out_ap, in_ap, reads, writes):
            if idx % 2 == 0:
                fw.op("dve", lambda e: e.tensor_copy(out_ap, in_ap), reads=reads, writes=writes)
            else:
                fw.op("act", lambda e: e.copy(out_ap, in_ap), reads=reads, writes=writes)

        def transposes_to(dst_buf, dst_ap_fn, srcs, src_bufs, tp_in, n_out):
            bi = next_tr()
            pv = PSb[bi]
            n = len(srcs)
            fw.group("pe", [lambda e, j=j, s=s: e.transpose(pv[0:n_out, j * 128:j * 128 + tp_in], s, idb[0:tp_in, 0:tp_in])
                            for j, s in enumerate(srcs)], reads=list(src_bufs) + [idb], writes=[PS[bi]])
            pview = pv[0:n_out, 0:n * 128].rearrange("p (a b) -> p a b", a=n)[:, :, 0:tp_in]
            dst_ap_fn(pview, PS[bi])

        def rms_rstd(ss_ap, ss_buf, n, tp):
            fw.op("act", lambda e: e.activation(ss_ap, ss_ap, AF.Sqrt, bias=EPS, scale=1.0 / n), reads=[ss_buf], writes=[ss_buf])
            fw.op("dve", lambda e: e.reciprocal(ss_ap, ss_ap), reads=[ss_buf], writes=[ss_buf])

        def process_block(cfg):
            tp, ntt = cfg["tp"], cfg["ntt"]
            NQ = tp * ntt
            xsrc = cfg["xsrc"]
            phases = cfg.get("phases", "ABCDEF")
            for tt in range(ntt):
                xl = xld[tt % 2]
                fw.dma("sp", xl[0:tp, :], xsrc[tt * tp:(tt + 1) * tp, :], writes=[xl])
                c = smallcol()
                ssap = small[0:tp, c:c + 1]
                fw.op("pool", lambda e: e.memset(ssap, 0.0), writes=[small])
                fw.op("act", lambda e: e.activation(xn[0:tp, :], xl[0:tp, :], AF.Square, accum_out=ssap),
                      reads=[xl], writes=[xn, small])
                rms_rstd(ssap, small, D, tp)
                fw.op("act", lambda e: e.activation(xn[0:tp, :], xl[0:tp, :], AF.Copy, scale=ssap),
                      reads=[xl, small], writes=[xn])
                for half in range(2):
                    def cp(pview, pbuf, half=half, tt=tt):
                        evac_alt(half, xT[:, half * 8:(half + 1) * 8, tt * tp:(tt + 1) * tp], pview, [pbuf], [xT])
                    transposes_to(xT, cp, [xn[0:tp, (half * 8 + j) * 128:(half * 8 + j + 1) * 128] for j in range(8)],
                                  [xn], tp, 128)

            def inproj_tok(col0, ncols, consume):
                w = load_w(col0, ncols)
                for tt in range(ntt):
                    ps = next_ps()
                    fw.group("pe", [lambda e, kc=kc: e.matmul(ps[0:tp, 0:ncols], lhsT=xT[:, kc, tt * tp:(tt + 1) * tp],
                                                              rhs=w[:, kc, 0:ncols], start=(kc == 0), stop=(kc == 15))
                                    for kc in range(16)], reads=[xT, w], writes=[ps])
                    consume(tt, ps)

            def inproj_feat(w, j, consume):
                ps = next_ps()
                fw.group("pe", [lambda e, kc=kc: e.matmul(ps[:, 0:NQ], lhsT=w[:, kc, j * 128:(j + 1) * 128],
                                                          rhs=xT[:, kc, 0:NQ], start=(kc == 0), stop=(kc == 15))
                                for kc in range(16)], reads=[xT, w], writes=[ps])
                consume(ps)

            if stage <= 1:
                return
            if "B" in phases or "C" in phases:
                ktd, vd_, kout, vout = cfg.get("KT"), cfg.get("V"), cfg.get("kout"), cfg.get("vout")
                tok0 = cfg.get("tok0")
                kvb = cfg.get("kvbuf")
                tl = cfg.get("tails", tails)
                hT_ = cfg.get("hT", hT)
                hTb_ = cfg.get("hTb", hTb)
                do_y = cfg.get("do_y", True)
                if cfg.get("save_i") is not None:
                    si_ = cfg["save_i"]
                    if si_ == 0:
                        fw.alias(hT_own, [mixT])
                        fw.alias(tails_own, [mixT])
                        fw.op("dve", lambda e: e.tensor_scalar_mul(hT_own[:, :], hT[:, :], wselt[:, 0:1]), reads=[hT, wselt], writes=[hT_own])
                        fw.op("dve", lambda e: e.tensor_scalar_mul(tails_own[:, :, :], tails[:, :, :], wselt[:, 0:1]),
                              reads=[tails, wselt], writes=[tails_own])
                    else:
                        fw.op("dve", lambda e: e.scalar_tensor_tensor(hT_own[:, :], hT[:, :], wselt[:, si_:si_ + 1], hT_own[:, :],
                                                                      ALU.mult, ALU.add), reads=[hT, wselt, hT_own], writes=[hT_own])
                        fw.op("dve", lambda e: e.scalar_tensor_tensor(tails_own[:, :, :], tails[:, :, :], wselt[:, si_:si_ + 1],
                                                                      tails_own[:, :, :], ALU.mult, ALU.add),
                              reads=[tails, wselt, tails_own], writes=[tails_own])
                if cfg.get("init_hTb"):
                    fw.alias(hTb_own, [mixT])
                    fw.op("act", lambda e: e.copy(hTb_[:, :], hT_[:, :]), reads=[hT_], writes=[hTb_])
                for cc in (range(16) if "B" in phases else []):
                    isk = cc < 8
                    col0 = (OK_ if isk else OV) + (cc % 8) * 256
                    kts = KTst[cc % 2]

                    def cons(tt, ps, cc=cc, isk=isk, kts=kts):
                        f = kf[(cc * ntt + tt) % 2]
                        b = kb[(cc * ntt + tt) % 2]
                        if stage <= 1.2:
                            return
                        fw.op("act", lambda e: e.copy(f[0:tp, :], ps[0:tp, 0:256]), reads=[ps], writes=[f])
                        fw.op("dve", lambda e: e.tensor_copy(b[0:tp, :], f[0:tp, :]), reads=[f], writes=[b])
                        if stage <= 1.4:
                            return
                        dst = kout if isk else vout
                        fw.dma("pool", dst[tt * tp:(tt + 1) * tp, (cc % 8) * 256:(cc % 8 + 1) * 256], f[0:tp, :], reads=[f])
                        if stage <= 1.6:
                            return
                        if isk:
                            def cp(pview, pbuf):
                                fw.op("dve", lambda e: e.tensor_copy(kts[:, :, tt * tp:(tt + 1) * tp], pview), reads=[pbuf], writes=[kts])
                            transposes_to(kts, cp, [b[0:tp, j * 128:(j + 1) * 128] for j in range(2)], [b], tp, 128)
                        else:
                            fw.dma("pool", vd_[tok0 + tt * tp: tok0 + (tt + 1) * tp, (cc % 8) * 256:(cc % 8 + 1) * 256], b[0:tp, :],
                                   reads=[b], writes=[kvb])
                    inproj_tok(col0, 256, cons)
                    if isk and stage > 1.8:
                        hm0 = (cc % 8) * 2
                        fw.dma("pool", ktd[hm0:hm0 + 2, :, tok0:tok0 + NQ].rearrange("a d t -> d a t"), kts[:, :, 0:NQ],
                               reads=[kts], writes=[kvb])

                if stage <= 2:
                    return
                enter_ssd_phase()
                fw.alias(sz, [hres])
                fw.alias(xs_tok, [hres])
                for cc in (range(8) if cfg.get("do_z", True) else []):
                    def consz(tt, ps, cc=cc):
                        fw.op("act", lambda e: e.activation(sz[0:tp, tt, cc * 256:(cc + 1) * 256], ps[0:tp, 0:256], AF.Silu),
                              reads=[ps], writes=[sz])
                    inproj_tok(OZ + cc * 256, 256, consz)
                for cc in range(cfg.get("n_xbc", 12)):
                    w = load_w(OX + cc * 256, 256)
                    for j in range(2):
                        ct = cc * 2 + j
                        rw = raw[ct % 2]
                        ca = cva[ct % 2]

                        def consx(ps, ct=ct, rw=rw, ca=ca):
                            fw.op("pool", lambda e: e.tensor_copy(rw[:, 0:3], tl[:, ct, :]), reads=[tl], writes=[rw])
                            fw.op("act", lambda e: e.copy(rw[:, 3:3 + NQ], ps[:, 0:NQ]), reads=[ps], writes=[rw])
                            fw.op("pool", lambda e: e.tensor_copy(tl[:, ct, :], rw[:, NQ:NQ + 3]), reads=[rw], writes=[tl])
                            fw.op("dve", lambda e: e.tensor_scalar(ca[:, 0:NQ], rw[:, 3:3 + NQ], cw[:, ct, 3:4], cb[:, ct:ct + 1],
                                                                   ALU.mult, ALU.add), reads=[rw, cw, cb], writes=[ca])
                            for jj in range(3):
                                fw.op("dve", lambda e, jj=jj: e.scalar_tensor_tensor(ca[:, 0:NQ], rw[:, jj:jj + NQ], cw[:, ct, jj:jj + 1],
                                                                                   ca[:, 0:NQ], ALU.mult, ALU.add),
                                      reads=[rw, cw, ca], writes=[ca])
                            if ct < 16:
                                cs = csb[ct % 2]
                                fw.op("act", lambda e: e.activation(cs[:, 0:NQ], ca[:, 0:NQ], AF.Silu), reads=[ca], writes=[cs])

                                def cp(pview, pbuf):
                                    evac_alt(ct, xs_tok[0:tp, 0:ntt, ct * 128:(ct + 1) * 128], pview, [pbuf], [xs_tok])
                                transposes_to(xs_tok, cp, [cs[:, tt * tp:(tt + 1) * tp] for tt in range(ntt)], [cs], 128, tp)
                            elif ct < 20:
                                g = ct - 16
                                fw.op("act", lambda e: e.activation(BT[:, g, 0:NQ], ca[:, 0:NQ], AF.Silu), reads=[ca], writes=[BT])

                                def cp(pview, pbuf):
                                    evac_alt(ct, B_tok[0:tp, 0:ntt, g * 128:(g + 1) * 128], pview, [pbuf], [B_tok])
                                transposes_to(B_tok, cp, [BT[:, g, tt * tp:(tt + 1) * tp] for tt in range(ntt)], [BT], 128, tp)
                            else:
                                g = ct - 20
                                fw.op("act", lambda e: e.activation(CT[:, g, 0:NQ], ca[:, 0:NQ], AF.Silu), reads=[ca], writes=[CT])
                        inproj_feat(w, j, consx)

                def consdt(tt, ps):
                    d0, d1, d2 = dtmp
                    fw.op("dve", lambda e: e.tensor_tensor(d0[0:tp, :], ps[0:tp, 0:32], dtb_bc[0:tp, :], ALU.add),
                          reads=[ps, dtb_bc], writes=[d0])
                    fw.op("dve", lambda e: e.scalar_tensor_tensor(d1[0:tp, :], d0[0:tp, :], -1.0, d0[0:tp, :], ALU.mult, ALU.max),
                          reads=[d0], writes=[d1])
                    fw.op("act", lambda e: e.activation(d1[0:tp, :], d1[0:tp, :], AF.Exp, scale=-1.0), reads=[d1], writes=[d1])
                    fw.op("act", lambda e: e.activation(d1[0:tp, :], d1[0:tp, :], AF.Ln, bias=1.0), reads=[d1], writes=[d1])
                    fw.op("dve", lambda e: e.scalar_tensor_tensor(dts[0:tp, tt, :], d0[0:tp, :], 0.0, d1[0:tp, :], ALU.max, ALU.add),
                          reads=[d0, d1], writes=[dts])
                    fw.op("dve", lambda e: e.tensor_tensor(av[0:tp, tt, :], dts[0:tp, tt, :], A_bc[0:tp, :], ALU.mult),
                          reads=[dts, A_bc], writes=[av])
                inproj_tok(ODT, 32, consdt)

                if stage <= 3:
                    return
                def bc3(ap2, n):
                    return ap2.unsqueeze(2).to_broadcast([ap2.shape[0], ap2.shape[1], n])

                for c in range(ntt):
                    sl = slice(c * tp, (c + 1) * tp)
                    psA = next_ps()
                    fw.group("pe", [lambda e: e.matmul(psA[0:tp, 0:32], lhsT=triu[0:tp, 0:tp], rhs=av[0:tp, c, :], start=True, stop=True),
                                    lambda e: e.matmul(psA[:, 32:64], lhsT=onesf[0:tp, :], rhs=av[0:tp, c, :], start=True, stop=True)],
                             reads=[triu, onesf, av], writes=[psA])
                    fw.op("dve", lambda e: e.tensor_copy(acst[0:tp, :], psA[0:tp, 0:32]), reads=[psA], writes=[acst])
                    fw.op("dve", lambda e: e.tensor_copy(tot[:, :], psA[:, 32:64]), reads=[psA], writes=[tot])
                    fw.op("act", lambda e: e.activation(ea[0:tp, :], acst[0:tp, :], AF.Exp), reads=[acst], writes=[ea])
                    fw.op("act", lambda e: e.activation(dec[:, :], tot[:, :], AF.Exp), reads=[tot], writes=[dec])
                    fw.op("dve", lambda e: e.tensor_tensor(te[0:tp, :], tot[0:tp, :], acst[0:tp, :], ALU.subtract),
                          reads=[tot, acst], writes=[te])
                    fw.op("act", lambda e: e.activation(te[0:tp, :], te[0:tp, :], AF.Exp), reads=[te], writes=[te])
                    fw.op("dve", lambda e: e.tensor_tensor(te[0:tp, :], te[0:tp, :], dts[0:tp, c, :], ALU.mult),
                          reads=[te, dts], writes=[te])
                    xv = xs_tok[0:tp, c, :].rearrange("p (h q) -> p h q", h=32)
                    fw.op("dve", lambda e: e.tensor_tensor(xdt[0:tp, :].rearrange("p (h q) -> p h q", h=32), xv,
                                                           bc3(dts[0:tp, c, :], 64), ALU.mult), reads=[xs_tok, dts], writes=[xdt])
                    fw.op("pool", lambda e: e.tensor_tensor(xdte[0:tp, :].rearrange("p (h q) -> p h q", h=32), xv,
                                                            bc3(te[0:tp, :], 64), ALU.mult), reads=[xs_tok, te], writes=[xdte])
                    if do_y:
                        psC = next_ps()
                        fw.group("pe", [lambda e, g=g: e.matmul(psC[0:tp, g * 128:g * 128 + tp], lhsT=BT[:, g, sl], rhs=CT[:, g, sl],
                                                                start=True, stop=True) for g in range(4)],
                                 reads=[BT, CT], writes=[psC])
                        fw.op("dve", lambda e: e.tensor_tensor(cbm[0:tp, :, 0:tp],
                                                               psC[0:tp, :].rearrange("p (g t) -> p g t", g=4)[:, :, 0:tp],
                                                               triu[0:tp, 0:tp].unsqueeze(1).to_broadcast([tp, 4, tp]), ALU.mult),
                              reads=[psC, triu], writes=[cbm])
                        for g in range(4):
                            psY = next_ps()
                            for hg in (2 * g, 2 * g + 1):
                                h0 = hg * 4
                                R = Rb[hg % 2]
                                sgm = segs[hg % 2]
                                M = MT[hg % 2]
                                R3 = R[0:tp, 0:4 * tp].rearrange("p (h t) -> p h t", h=4)
                                fw.op("pool", lambda e: e.tensor_tensor(R3, triu[0:tp, 0:tp].unsqueeze(1).to_broadcast([tp, 4, tp]),
                                                                        bc3(av[0:tp, c, h0:h0 + 4], tp), ALU.mult),
                                      reads=[triu, av], writes=[R])
                                psB = next_ps()
                                fw.group("pe", [lambda e: e.matmul(psB[0:tp, 0:4 * tp], lhsT=onesf[0:tp, 0:tp], rhs=R[0:tp, 0:4 * tp],
                                                                   start=True, stop=True)], reads=[onesf, R], writes=[psB])
                                for hh in range(4):
                                    fw.op("dve", lambda e, hh=hh: e.tensor_scalar(sgm[0:tp, hh * tp:(hh + 1) * tp], psB[0:tp, hh * tp:(hh + 1) * tp],
                                                                                  acst[0:tp, h0 + hh:h0 + hh + 1], 0.0, ALU.subtract, ALU.min),
                                          reads=[psB, acst], writes=[sgm])
                                fw.op("act", lambda e: e.activation(sgm[0:tp, 0:4 * tp], sgm[0:tp, 0:4 * tp], AF.Exp), reads=[sgm], writes=[sgm])
                                fw.op("dve", lambda e: e.tensor_tensor(M[0:tp, 0:4 * tp].rearrange("p (h t) -> p h t", h=4),
                                                                       sgm[0:tp, 0:4 * tp].rearrange("p (h t) -> p h t", h=4),
                                                                       cbm[0:tp, g, 0:tp].unsqueeze(1).to_broadcast([tp, 4, tp]), ALU.mult),
                                      reads=[sgm, cbm], writes=[M])
                                fw.group("pe", [lambda e, hh=hh: e.matmul(psY[0:tp, ((h0 + hh) % 8) * 64:((h0 + hh) % 8) * 64 + 64],
                                                                          lhsT=M[0:tp, hh * tp:(hh + 1) * tp],
                                                                          rhs=xdt[0:tp, (h0 + hh) * 64:(h0 + hh + 1) * 64], start=True, stop=True)
                                                for hh in range(4)], reads=[M, xdt], writes=[psY])
                            psO = next_ps()
                            fw.group("pe", [lambda e: e.matmul(psO[0:tp, 0:512], lhsT=CT[:, g, sl], rhs=hTb_[:, g * 512:(g + 1) * 512],
                                                               start=True, stop=True)], reads=[CT, hTb_], writes=[psO])
                            a1, a2, yb = t1[g % 2], t2[g % 2], ym[g % 2]
                            gs = slice(g * 512, (g + 1) * 512)
                            v3 = lambda ap: ap.rearrange("p (h q) -> p h q", h=8)
                            fw.op("dve", lambda e: e.tensor_tensor(v3(a1[0:tp, :]), v3(psO[0:tp, 0:512]), bc3(ea[0:tp, 8 * g:8 * g + 8], 64), ALU.mult),
                                  reads=[psO, ea], writes=[a1])
                            fw.op("dve", lambda e: e.tensor_tensor(a1[0:tp, :], a1[0:tp, :], psY[0:tp, 0:512], ALU.add), reads=[a1, psY], writes=[a1])
                            fw.op("pool", lambda e: e.tensor_tensor(v3(a2[0:tp, :]), v3(xs_tok[0:tp, c, gs]), bc3(D_bc[0:tp, 8 * g:8 * g + 8], 64), ALU.mult),
                                  reads=[xs_tok, D_bc], writes=[a2])
                            fw.op("pool", lambda e: e.tensor_tensor(a1[0:tp, :], a1[0:tp, :], a2[0:tp, :], ALU.add), reads=[a1, a2], writes=[a1])
                            fw.op("pool", lambda e: e.tensor_tensor(a1[0:tp, :], a1[0:tp, :], sz[0:tp, c, gs], ALU.mult), reads=[a1, sz], writes=[a1])
                            cc_ = smallcol()
                            ssap = small[0:tp, cc_:cc_ + 1]
                            fw.op("pool", lambda e: e.memset(ssap, 0.0), writes=[small])
                            fw.op("act", lambda e: e.activation(a2[0:tp, :], a1[0:tp, :], AF.Square, accum_out=ssap), reads=[a1], writes=[a2, small])
                            rms_rstd(ssap, small, 512, tp)
                            fw.op("act", lambda e: e.activation(yb[0:tp, :], a1[0:tp, :], AF.Copy, scale=ssap), reads=[a1, small], writes=[yb])

                            def cp(pview, pbuf, g=g):
                                evac_alt(g, mixT[:, 16 + 4 * g:16 + 4 * g + 4, sl], pview, [pbuf], [mixT])
                            transposes_to(mixT, cp, [yb[0:tp, j * 128:(j + 1) * 128] for j in range(4)], [yb], tp, 128)
                    for g in range(4):
                        psH = next_ps()
                        gs = slice(g * 512, (g + 1) * 512)
                        fw.group("pe", [lambda e: e.matmul(psH[:, 0:512], lhsT=B_tok[0:tp, c, g * 128:(g + 1) * 128], rhs=xdte[0:tp, gs],
                                                           start=True, stop=True)], reads=[B_tok, xdte], writes=[psH])
                        h3 = hT_[:, gs].rearrange("p (h q) -> p h q", h=8)
                        fw.op("dve", lambda e: e.tensor_tensor(h3, h3, bc3(dec[:, 8 * g:8 * g + 8], 64), ALU.mult), reads=[hT_, dec], writes=[hT_])
                        fw.op("dve", lambda e: e.tensor_tensor(hT_[:, gs], hT_[:, gs], psH[:, 0:512], ALU.add), reads=[hT_, psH], writes=[hT_])
                    fw.op("act", lambda e: e.copy(hTb_[:, :], hT_[:, :]), reads=[hT_], writes=[hTb_])

            if "E" not in phases:
                return
            fw.alias(mixT, [hT_own, hTb_own, tails_own])
            if stage <= 4:
                return
            enter_att_phase()
            for v in vst:
                fw.op("pool", lambda e, v=v: e.memset(v[:, :, 256:257], 1.0), writes=[v])
            ktiles = cfg["ktiles"]
            pieces = cfg["pieces"]
            for h in range(8):
                wq = load_w(OQ + h * 256, 256)
                for m in range(2):
                    def consq(ps, m=m):
                        fw.op("act", lambda e: e.copy(QT[:, m, 0:NQ], ps[:, 0:NQ]), reads=[ps], writes=[QT])
                    inproj_feat(wq, m, consq)

                def consg(tt, ps):
                    fw.op("act", lambda e: e.activation(sg[0:tp, tt, :], ps[0:tp, 0:256], AF.Silu), reads=[ps], writes=[sg])
                inproj_tok(OG + h * 256, 256, consg)
                if cfg.get("use_bmd", False):
                    fw.dma("sp", bmt[:], bmd[h].rearrange("p (d q) -> p d q", d=5), reads=[bmbuf], writes=[bmt])
                for m in range(2):
                    loaded = {}

                    def get_piece(pid, m=m, h=h, loaded=loaded):
                        if pid in loaded:
                            return loaded[pid]
                        ksrc, vsrc, nkeys, bufs = pieces[pid]
                        kbuf_ = kst[pid % 2]
                        vbuf_ = vst[pid % 2]
                        fw.dma("sp", kbuf_[:, 0:nkeys], ksrc(h * 2 + m), reads=bufs, writes=[kbuf_])
                        nfull = nkeys // 128
                        if nfull > 0:
                            fw.dma("sp", vbuf_[:, 0:nfull, 0:256], vsrc(h, 0, nfull * 128).rearrange("(kt p) e -> p kt e", p=128),
                                   reads=bufs, writes=[vbuf_])
                        rem = nkeys - nfull * 128
                        if rem > 0:
                            fw.dma("sp", vbuf_[0:rem, nfull, 0:256], vsrc(h, nfull * 128, rem), reads=bufs, writes=[vbuf_])
                        loaded[pid] = (kbuf_, vbuf_)
                        return loaded[pid]

                    nt = len(ktiles)
                    first_for_qs = {}
                    sinfo = {}

                    def emitS(i):
                        pid, idx, nk, diag, qsv = ktiles[i]
                        kbuf_, vbuf_ = get_piece(pid)
                        ps = PS[i % 4]
                        fw.group("pe", [lambda e: e.matmul(ps[0:nk, 0:NQ], lhsT=kbuf_[:, idx * 128:idx * 128 + nk], rhs=QT[:, m, 0:NQ],
                                                           start=True, stop=True)], reads=[kbuf_, QT], writes=[ps])
                        p = PT[i % 3]
                        if diag is None:
                            fw.op("act", lambda e: e.activation(p[0:nk, 0:NQ], ps[0:nk, 0:NQ], AF.Exp, bias=cb15[0:nk, h:h + 1], scale=SCALE),
                                  reads=[ps, cb15], writes=[p])
                        elif isinstance(diag, tuple):
                            dg = dgt[i % 2]
                            slot = bms[diag[1] % 5]
                            fw.dma("sp", slot[:], bm2[h, diag[1]], reads=[bm2buf], writes=[slot])
                            fw.op("dve", lambda e: e.scalar_tensor_tensor(dg[0:nk, 0:NQ], ps[0:nk, 0:NQ], SCALE, slot[0:nk, 0:NQ],
                                                                          ALU.mult, ALU.add), reads=[ps, slot], writes=[dg])
                            fw.op("act", lambda e: e.activation(p[0:nk, 0:NQ], dg[0:nk, 0:NQ], AF.Exp), reads=[dg], writes=[p])
                        else:
                            dg = dgt[i % 2]
                            fw.op("dve", lambda e: e.scalar_tensor_tensor(dg[0:nk, 0:NQ], ps[0:nk, 0:NQ], SCALE, bmt[0:nk, diag, 0:NQ],
                                                                          ALU.mult, ALU.add), reads=[ps, bmt], writes=[dg])
                            fw.op("act", lambda e: e.activation(p[0:nk, 0:NQ], dg[0:nk, 0:NQ], AF.Exp), reads=[dg], writes=[p])
                        sinfo[i] = (p, vbuf_)

                    last_for_qs = {}
                    for i, (pid, idx, nk, diag, qsv) in enumerate(ktiles):
                        for qs in qsv:
                            last_for_qs[qs] = i
                            if qs not in first_for_qs:
                                first_for_qs[qs] = i

                    def emitPV(i):
                        pid, idx, nk, diag, qsv = ktiles[i]
                        p, vbuf_ = sinfo.pop(i)
                        fw.group("pe", [lambda e, qs=qs: e.matmul(PS[4 + qs][0:tp, 0:257], lhsT=p[0:nk, qs * tp:(qs + 1) * tp],
                                                                  rhs=vbuf_[0:nk, idx, 0:257], start=(i == first_for_qs[qs]),
                                                                  stop=(i == last_for_qs[qs])) for qs in qsv],
                                 reads=[p, vbuf_], writes=[PS[4 + qs] for qs in qsv])

                    LA = 2
                    for i in range(nt + LA):
                        if i < nt:
                            emitS(i)
                        if i >= LA:
                            emitPV(i - LA)
                        if i < nt and ktiles[i][1] == LA and (ktiles[i][0] + 1) in pieces:
                            get_piece(ktiles[i][0] + 1)
                    for qs in range(ntt):
                        acc = PS[4 + qs]
                        c_ = smallcol()
                        rc = small[0:tp, c_:c_ + 1]
                        fw.op("dve", lambda e: e.reciprocal(rc, acc[0:tp, 256:257]), reads=[acc], writes=[small])
                        if m == 0:
                            fw.op("dve", lambda e: e.tensor_scalar_mul(att[0:tp, qs, :], acc[0:tp, 0:256], rc), reads=[acc, small], writes=[att])
                        else:
                            fw.op("dve", lambda e: e.tensor_tensor(rc, rc, nlam[0:tp, :], ALU.mult), reads=[small, nlam], writes=[small])
                            fw.op("dve", lambda e: e.scalar_tensor_tensor(att[0:tp, qs, :], acc[0:tp, 0:256], rc, att[0:tp, qs, :],
                                                                          ALU.mult, ALU.add), reads=[acc, small, att], writes=[att])
                    if m == 1:
                        for qs in range(ntt):
                            c2 = smallcol()
                            ssap = small[0:tp, c2:c2 + 1]
                            mab = ma[qs % 2]
                            fw.op("pool", lambda e: e.memset(ssap, 0.0), writes=[small])
                            fw.op("act", lambda e: e.activation(mab[0:tp, :], att[0:tp, qs, :], AF.Square, accum_out=ssap),
                                  reads=[att], writes=[mab, small])
                            rms_rstd(ssap, small, 256, tp)
                            fw.op("dve", lambda e: e.scalar_tensor_tensor(mab[0:tp, :], att[0:tp, qs, :], ssap, sg[0:tp, qs, :],
                                                                          ALU.mult, ALU.mult), reads=[att, small, sg], writes=[mab])

                            def cp(pview, pbuf, qs=qs):
                                evac_alt(qs, mixT[:, 2 * h:2 * h + 2, qs * tp:(qs + 1) * tp], pview, [pbuf], [mixT])
                            transposes_to(mixT, cp, [mab[0:tp, j * 128:(j + 1) * 128] for j in range(2)], [mab], tp, 128)

            if stage <= 5:
                return
            fw.alias(hres, [sz, xs_tok])
            for oc in range(16):
                col0 = oc * 128
                w = next_w()
                wv = w[:, :, :].rearrange("p a b -> p (a b)").rearrange("p (a b) -> p a b", a=32)
                fw.dma("sp", wv, Wob[:, col0:col0 + 128].rearrange("(kc p) c -> p kc c", p=128), reads=[Wobuf], writes=[w])
                for tt in range(ntt):
                    ps = next_ps()
                    fw.group("pe", [lambda e, kc=kc: e.matmul(ps[0:tp, 0:128], lhsT=mixT[:, kc, tt * tp:(tt + 1) * tp], rhs=wv[:, kc, :],
                                                              start=(kc == 0), stop=(kc == 31)) for kc in range(32)],
                             reads=[mixT, w], writes=[ps])
                    evac_alt(tt, hres[0:tp, tt, col0:col0 + 128], ps[0:tp, 0:128], [ps], [hres])
            yout = cfg["yout"]
            for tt in range(ntt):
                xl = xld[tt % 2]
                fw.dma("sp", xl[0:tp, :], xsrc[tt * tp:(tt + 1) * tp, :], writes=[xl])
                fw.op("dve", lambda e: e.tensor_tensor(xl[0:tp, :], xl[0:tp, :], hres[0:tp, tt, :], ALU.add), reads=[xl, hres], writes=[xl])
                c_ = smallcol()
                ssap = small[0:tp, c_:c_ + 1]
                fw.op("pool", lambda e: e.memset(ssap, 0.0), writes=[small])
                fw.op("act", lambda e: e.activation(hres[0:tp, tt, :], xl[0:tp, :], AF.Square, accum_out=ssap), reads=[xl], writes=[hres, small])
                rms_rstd(ssap, small, D, tp)
                fw.op("dve", lambda e: e.scalar_tensor_tensor(xl[0:tp, :], xl[0:tp, :], ssap, fnw_bc[0:tp, :], ALU.mult, ALU.mult),
                      reads=[xl, small, fnw_bc], writes=[xl])
                fw.dma("pool", yout[tt * tp:(tt + 1) * tp, :], xl[0:tp, :], reads=[xl])

        def write_state_outputs(conv_out, ssm_out):
            enter_ssd_phase()
            for ct in range(24):
                fw.dma("pool", conv_out[:, ct * 128:(ct + 1) * 128].rearrange("j p -> p j"), tails[:, ct, :], reads=[tails],
                       allow_slow_non_contiguous=True)
            for i in range(16):
                ps = next_ps()
                fw.group("pe", [lambda e: e.transpose(ps[:, 0:128], hT[:, i * 128:(i + 1) * 128], idf[:, :])], reads=[hT, idf], writes=[ps])
                st = t1[i % 2]
                evac_alt(i, st[:, 0:128], ps[:, 0:128], [ps], [st])
                fw.dma("pool", ssm_out[i * 128:(i + 1) * 128, :], st[:, 0:128], reads=[st])

        kvbufs = [Buf(None, "kv%d" % i) for i in range(NBLK)]
        for mstep in range(NBLK // 4 if stage > 0 else 0):
            for bi_ in range(4):
                blk = 4 * mstep + bi_
                tok0 = blk * 512
                cfg = dict(tp=128, ntt=4, xsrc=xp[tok0:tok0 + 512, :], KT=KTd, V=Vd, kout=k_p[tok0:tok0 + 512, :],
                           vout=v_p[tok0:tok0 + 512, :], tok0=tok0, kvbuf=kvbufs[blk], phases="ABCD", save_i=bi_, do_z=False, n_xbc=12, do_y=False)
                process_block(cfg)
            nkt = 16 * (mstep + 1)
            ktiles = []
            for kt in range(nkt):
                t_ = kt - 16 * mstep
                ktiles.append((kt // 16, kt % 16, 128, (None if t_ < -1 else ("dram", t_ + 1)), [0, 1, 2, 3]))
            pieces = {}
            for pid in range(mstep + 1):
                k0 = pid * 2048
                pieces[pid] = (lambda hm, k0=k0: KTd[hm, :, k0:k0 + 2048],
                               lambda h, o, n, k0=k0: Vd[k0 + o:k0 + o + n, h * 256:(h + 1) * 256],
                               2048, kvbufs[4 * pid:4 * pid + 4])
            cfg = dict(tp=128, ntt=4, xsrc=xown[mstep * 512:(mstep + 1) * 512, :], phases="ACDEF", tails=tails_own, hT=hT_own,
                       hTb=hTb_own, init_hTb=True,
                       ktiles=ktiles, pieces=pieces, yout=y_p[mstep * 512:(mstep + 1) * 512, :])
            process_block(cfg)
        if stage > 5:
            write_state_outputs(conv_p, ssm_p)

        if sample:
            fw.barrier()
            skv = Buf(None, "skv")
            for kt in range(16):
                xl = xld[kt % 2]
                fw.dma("sp", xl[:, :], ck[kt * 128:(kt + 1) * 128, :], writes=[xl])
                fw.op("act", lambda e: e.copy(xn[:, :], xl[:, :]), reads=[xl], writes=[xn])
                kts = mixT
                for half in range(2):
                    def cp(pview, pbuf, half=half):
                        evac_alt(half, mixT[:, half * 8:(half + 1) * 8, 0:128], pview, [pbuf], [mixT])
                    transposes_to(mixT, cp, [xn[:, (half * 8 + j) * 128:(half * 8 + j + 1) * 128] for j in range(8)], [xn], 128, 128)
                fw.dma("pool", KTsd[:, :, kt * 128:(kt + 1) * 128].rearrange("a d t -> d a t"), mixT[:, 0:16, 0:128],
                       reads=[mixT], writes=[skv])
                xl2 = xld[(kt + 1) % 2]
                fw.dma("sp", xl2[:, :], cv[kt * 128:(kt + 1) * 128, :], writes=[xl2])
                fw.op("dve", lambda e: e.tensor_copy(xs_tok[:, 0, :], xl2[:, :]), reads=[xl2], writes=[xs_tok])
                fw.dma("pool", Vsd[kt * 128:(kt + 1) * 128, :], xs_tok[:, 0, :], reads=[xs_tok], writes=[skv])
            fw.dma("sp", tails[:], cconv, writes=[tails])
            enter_ssd_phase()
            for i in range(16):
                st = t1[i % 2]
                fw.dma("sp", st[:, 0:128], sst[i * 128:(i + 1) * 128, :], writes=[st])
                ps = next_ps()
                fw.group("pe", [lambda e: e.transpose(ps[:, 0:128], st[:, 0:128], idf[:, :])], reads=[st, idf], writes=[ps])
                evac_alt(i, hT[:, i * 128:(i + 1) * 128], ps[:, 0:128], [ps], [hT])
            fw.op("act", lambda e: e.copy(hTb[:, :], hT[:, :]), reads=[hT], writes=[hTb])
            ktiles = []
            for kt in range(16):
                ktiles.append((kt // 16, kt % 16, 128, (0 if kt == 15 else None), [0]))
            ktiles.append((1, 0, 32, 1, [0]))
            pieces = {
                0: (lambda hm: KTsd[hm, :, 0:2048], lambda h, o, n: Vsd[o:o + n, h * 256:(h + 1) * 256], 2048, [skv]),
                1: (lambda hm: KTsd[hm, :, 2048:2080], lambda h, o, n: Vsd[2048 + o:2048 + o + n, h * 256:(h + 1) * 256], 32, [skv]),
            }
            cfg = dict(tp=32, ntt=1, xsrc=xsm, KT=KTsd, V=Vsd, kout=k_s, vout=v_s, tok0=PAST, kvbuf=skv, ktiles=ktiles,
                       pieces=pieces, yout=y_s, use_bmd=True)
            process_block(cfg)
            write_state_outputs(conv_s, ssm_s)

        fw.barrier()
        print("instructions:", fw.ninstr)
    return nc


def host_consts():
    ident = np.eye(128, dtype=np.float32)
    triu = np.triu(np.ones((128, 128), np.float32))
    anti = np.ascontiguousarray(ident[::-1])
    idx = np.arange(1152)
    rel = (511 - idx).astype(np.int32)
    b = rel_bucket_np(rel)
    oh = np.zeros((32, 1152), np.float32)
    oh[b, idx] = 1.0
    mask = np.zeros((128, 5, 512), np.float32)
    k = np.arange(128)[:, None]
    q = np.arange(512)[None, :]
    for d in range(5):
        vis = ((128 * (d - 1) + k) // 64) <= (q // 64)
        mask[:, d, :] = np.where(vis, 0.0, NEG)
    return dict(c_ident=ident, c_triu=triu, c_anti=anti, c_oh=oh, c_mask=mask)


def make_in_maps(inp, NBLK=32):
    f = lambda a: np.ascontiguousarray(np.asarray(a, dtype=np.float32))
    SEQ = NBLK * 512
    consts = host_consts()
    common = dict(
        relb=f(inp["rel_bias"]),
        nw=f(np.asarray(inp["norm_w"])[0].reshape(16, 128).T),
        w_in=f(inp["w_in"][0]), w_out=f(inp["w_out"][0]),
        lamv=f(np.stack([np.asarray(inp["lambda_q1"])[0], np.asarray(inp["lambda_k1"])[0],
                         np.asarray(inp["lambda_q2"])[0], np.asarray(inp["lambda_k2"])[0]])),
        sublnT=f(np.asarray(inp["subln_w"])[0].reshape(2, 128).T),
        convw=f(np.asarray(inp["conv_w"])[0].reshape(4, 24, 128).transpose(2, 1, 0)),
        convb=f(np.asarray(inp["conv_b"])[0].reshape(24, 128).T),
        dtb=f(inp["dt_bias"]), alog=f(inp["A_log"]), dskip=f(inp["D_skip"]),
        ssmnwT=f(np.asarray(inp["ssm_norm_w"])[0].reshape(16, 128).T),
        fnw=f(np.asarray(inp["final_norm_w"]).reshape(1, 2048)),
        **consts)
    maps = []
    NM = NBLK // 4
    xpa = np.asarray(inp["x_prompt"])
    for c in range(8):
        b, j = c // 4, c % 4
        m = dict(common)
        m["xp"] = f(xpa[b, :SEQ])
        m["xown"] = f(xpa[b, :SEQ].reshape(NM, 4, 512, 2048)[:, j].reshape(NM * 512, 2048))
        sel = np.zeros((17, 7), np.float32)
        for ti in range(17):
            delta = 128 * (ti - 1) - 512 * j
            d = 5 if delta <= -256 else (6 if delta >= 512 else delta // 128 + 1)
            sel[ti, d] = 1.0
        m["selw"] = f(np.broadcast_to(sel.reshape(1, 119), (128, 119)))
        ws = np.zeros((1, 4), np.float32)
        ws[0, j] = 1.0
        m["wsel"] = f(np.broadcast_to(ws, (128, 4)))
        m["xsm"] = f(np.asarray(inp["x_sample"])[c])
        m["ck"] = f(np.asarray(inp["cache_k"])[0, c].reshape(PAST, 2048))
        m["cv"] = f(np.asarray(inp["cache_v"])[0, c].reshape(PAST, 2048))
        m["cconv"] = f(np.asarray(inp["cache_conv"])[0, c].reshape(3, 24, 128).transpose(2, 1, 0))
        m["sst"] = f(np.asarray(inp["state_ssm"])[0, c].reshape(2048, 128))
        maps.append(m)
    return maps


_NC_CACHE = {}


def run(inp, NBLK=32, sample=True, stage=99):
    key = (NBLK, sample, stage)
    if key not in _NC_CACHE:
        _NC_CACHE[key] = build(NBLK, sample, stage)
    nc = _NC_CACHE[key]
    maps = make_in_maps(inp, NBLK)
    res = run_bass_kernel_spmd(nc, maps, core_ids=list(range(8)))
    return res.results


def assemble_y(r, NBLK):
    NM = NBLK // 4
    y = np.zeros((2, NBLK * 512, 2048), np.float32)
    yv = y.reshape(2, NM, 4, 512, 2048)
    for c in range(8):
        b, j = c // 4, c % 4
        yv[b, :, j] = r[c]["y_p"].reshape(NM, 512, 2048)
    return y


def kernel(**inp):
    r = run(inp, 32, True)
    S = 16384
    y_prompt = assemble_y(r, 32)
    y_sample = np.stack([r[c]["y_s"] for c in range(8)]).reshape(8, 32, 2048)
    k_prompt = np.stack([r[4 * b]["k_p"] for b in range(2)]).reshape(1, 2, S, 8, 2, 128)
    v_prompt = np.stack([r[4 * b]["v_p"] for b in range(2)]).reshape(1, 2, S, 8, 256)
    conv_prompt = np.stack([r[4 * b]["conv_p"] for b in range(2)]).reshape(1, 2, 3, 3072)
    ssm_prompt = np.stack([r[4 * b]["ssm_p"] for b in range(2)]).reshape(1, 2, 32, 64, 128)
    k_sample = np.stack([r[c]["k_s"] for c in range(8)]).reshape(1, 8, 32, 8, 2, 128)
    v_sample = np.stack([r[c]["v_s"] for c in range(8)]).reshape(1, 8, 32, 8, 256)
    conv_sample = np.stack([r[c]["conv_s"] for c in range(8)]).reshape(1, 8, 3, 3072)
    ssm_sample = np.stack([r[c]["ssm_s"] for c in range(8)]).reshape(1, 8, 32, 64, 128)
    return tuple(np.ascontiguousarray(a.astype(np.float32)) for a in
                 (y_prompt, y_sample, k_prompt, v_prompt, conv_prompt, ssm_prompt, k_sample, v_sample, conv_sample, ssm_sample))
```

```python
import math
import numpy as np
from contextlib import ExitStack
import concourse.bass as bass
import concourse.mybir as mybir
from concourse.bass_utils import run_bass_kernel_spmd

F32 = mybir.dt.float32
BF16 = mybir.dt.bfloat16
ALU = mybir.AluOpType
AF = mybir.ActivationFunctionType
AX = mybir.AxisListType

D = 2048
KC = 16
NCOL = 13344
OQ, OK_, OV, OG, OZ, OX, ODT = 0, 2048, 4096, 6144, 8192, 10240, 13312
EPS = 1e-5
LAM0 = 0.8 - 0.6 * math.exp(-0.3 * 0)
SCALE = 128 ** -0.5
PAST = 2048
NEG = -30000.0


class Buf:
    __slots__ = ("t", "w", "r", "name")

    def __init__(self, t=None, name=""):
        self.t = t
        self.w = None
        self.r = {}
        self.name = name

    def __getitem__(self, k):
        return self.t[k]


class FW:
    def __init__(self, nc, es, ndma=16):
        self.nc = nc
        self.eng = {"pe": nc.tensor, "act": nc.scalar, "dve": nc.vector, "pool": nc.gpsimd, "sp": nc.sync}
        self.sem = {}
        self.cnt = {}
        self.semobj = {}
        for k in self.eng:
            self.sem[k] = es.enter_context(nc.semaphore("s_" + k))
            self.cnt[k] = 0
            self.semobj[k] = self.sem[k]
        self.waited = {}
        self.dring = {}
        self.dpos = {}
        for q in ("sp", "pool"):
            self.dring[q] = [[es.enter_context(nc.semaphore("d_%s%d" % (q, i))), 0] for i in range(ndma)]
            self.dpos[q] = 0
            for i, s in enumerate(self.dring[q]):
                self.semobj[("d", q, i)] = s[0]
        self.ninstr = 0

    def _wait(self, e, key, val):
        if val <= 0:
            return
        if e == "pe" and key == "pe":
            return
        if self.waited.get((e, key), 0) >= val:
            return
        self.eng[e].wait_ge(self.semobj[key], val)
        self.waited[(e, key)] = val
        self.ninstr += 1

    def _deps(self, e, reads, writes):
        deps = {}
        for b in reads:
            if b.w is not None:
                k, v = b.w
                if deps.get(k, 0) < v:
                    deps[k] = v
        for b in writes:
            if b.w is not None:
                k, v = b.w
                if deps.get(k, 0) < v:
                    deps[k] = v
            for k, v in b.r.items():
                if deps.get(k, 0) < v:
                    deps[k] = v
        for k, v in deps.items():
            self._wait(e, k, v)

    def _mark(self, tok, reads, writes):
        k, v = tok
        for b in reads:
            if b.r.get(k, 0) < v:
                b.r[k] = v
        for b in writes:
            b.w = tok
            b.r = {}

    def op(self, e, fn, reads=(), writes=()):
        return self.group(e, [fn], reads, writes)

    def group(self, e, fns, reads=(), writes=()):
        self._deps(e, reads, writes)
        ins = None
        for fn in fns:
            ins = fn(self.eng[e])
            self.ninstr += 1
        self.cnt[e] += 1
        ins.then_inc(self.sem[e], 1)
        tok = (e, self.cnt[e])
        self._mark(tok, reads, writes)
        return tok

    def dma(self, q, out, in_, reads=(), writes=(), **kw):
        self._deps(q, reads, writes)
        ring = self.dring[q]
        i = self.dpos[q]
        self.dpos[q] = (i + 1) % len(ring)
        slot = ring[i]
        key = ("d", q, i)
        self._wait(q, key, slot[1])
        ins = self.eng[q].dma_start(out=out, in_=in_, **kw)
        slot[1] += 16
        ins.then_inc(slot[0], 16)
        self.ninstr += 1
        tok = (key, slot[1])
        self._mark(tok, reads, writes)
        return tok

    def alias(self, dst, srcs):
        for s in srcs:
            if s.w is not None:
                k, v = s.w
                if dst.r.get(k, 0) < v:
                    dst.r[k] = v
            for k, v in s.r.items():
                if dst.r.get(k, 0) < v:
                    dst.r[k] = v

    def barrier(self):
        for e in self.eng:
            for k in self.eng:
                if k != e:
                    self._wait(e, k, self.cnt[k])
            for q in self.dring:
                for i, slot in enumerate(self.dring[q]):
                    self._wait(e, ("d", q, i), slot[1])


def rel_bucket_np(rel):
    half, max_exact = 16, 8
    ret = np.where(rel > 0, half, 0)
    n = np.abs(rel)
    nf = np.maximum(n, 1).astype(np.float32)
    large = max_exact + (np.log(nf / np.float32(max_exact)) / np.float32(math.log(128 / max_exact))
                         * np.float32(half - max_exact)).astype(np.int32)
    large = np.minimum(large, half - 1)
    return ret + np.where(n < max_exact, n, large)


def build(NBLK=32, sample=True, stage=99):
    nc = bass.Bass("TRN2", target_bir_lowering=False)
    SEQ = NBLK * 512

    def din(name, shape, dt=F32):
        return nc.dram_tensor(name, list(shape), dt, kind="ExternalInput").ap()

    def dout(name, shape, dt=F32):
        return nc.dram_tensor(name, list(shape), dt, kind="ExternalOutput").ap()

    def dscr(name, shape, dt):
        return nc.dram_tensor(name, list(shape), dt).ap()

    xp = din("xp", [SEQ, D])
    NM = NBLK // 4
    xown = din("xown", [NM * 512, D])
    selw = din("selw", [128, 17 * 7])
    wsel = din("wsel", [128, 4])
    xsm = din("xsm", [32, D])
    ck = din("ck", [PAST, D])
    cv = din("cv", [PAST, D])
    cconv = din("cconv", [128, 24, 3])
    sst = din("sst", [2048, 128])
    relb = din("relb", [32, 8])
    nw = din("nw", [128, 16])
    w_in = din("w_in", [D, NCOL])
    w_out = din("w_out", [4096, D])
    lamv = din("lamv", [4, 128])
    sublnT = din("sublnT", [128, 2])
    convw = din("convw", [128, 24, 4])
    convb = din("convb", [128, 24])
    dtb = din("dtb", [1, 32])
    alog = din("alog", [1, 32])
    dskip = din("dskip", [1, 32])
    ssmnwT = din("ssmnwT", [128, 16])
    fnw = din("fnw", [1, D])
    c_ident = din("c_ident", [128, 128])
    c_triu = din("c_triu", [128, 128])
    c_anti = din("c_anti", [128, 128])
    c_oh = din("c_oh", [32, 1152])
    c_mask = din("c_mask", [128, 5, 512])

    y_p = dout("y_p", [NM * 512, D])
    k_p = dout("k_p", [SEQ, D])
    v_p = dout("v_p", [SEQ, D])
    conv_p = dout("conv_p", [3, 3072])
    ssm_p = dout("ssm_p", [2048, 128])
    y_s = dout("y_s", [32, D])
    k_s = dout("k_s", [32, D])
    v_s = dout("v_s", [32, D])
    conv_s = dout("conv_s", [3, 3072])
    ssm_s = dout("ssm_s", [2048, 128])

    Wb = dscr("Wb", [D, NCOL], BF16)
    Wob = dscr("Wob", [4096, D], BF16)
    KTd = dscr("KTd", [16, 128, SEQ], BF16)
    Vd = dscr("Vd", [SEQ, D], BF16)
    KTsd = dscr("KTsd", [16, 128, PAST + 128], BF16)
    Vsd = dscr("Vsd", [PAST + 128, D], BF16)
    bvs = dscr("bvs", [8, 1152], F32)
    bmd = dscr("bmd", [8, 128, 5 * 512], F32)
    bm2 = dscr("bm2", [8, 17, 128, 512], F32)

    es = ExitStack()
    with es:
        fw = FW(nc, es)
        ARENA = 207 * 1024
        arena = es.enter_context(nc.sbuf_tensor("arena", [128, ARENA // 4], F32))
        apos = [0]

        def alloc(nbytes):
            o = apos[0]
            apos[0] = o + ((nbytes + 31) // 32) * 32
            assert apos[0] <= ARENA, ("SBUF arena overflow", apos[0])
            return o

        def view(off, shape, dt, name=""):
            n = int(np.prod(shape[1:]))
            if dt == F32:
                ap = arena[0:shape[0], off // 4: off // 4 + n]
            else:
                ap = arena[0:shape[0], off // 4: off // 4 + (n + 1) // 2].bitcast(BF16)[:, 0:n]
            if len(shape) == 3:
                ap = ap.rearrange("p (a b) -> p a b", a=shape[1])
            elif len(shape) == 4:
                ap = ap.rearrange("p (a b c) -> p a b c", a=shape[1], b=shape[2])
            return Buf(ap, name)

        def sb(name, shape, dt):
            n = int(np.prod(shape[1:])) * (4 if dt == F32 else 2)
            return view(alloc(n), shape, dt, name)

        PS = [Buf(es.enter_context(nc.psum_tensor("ps%d" % i, [128, 512], F32)), "ps%d" % i) for i in range(8)]
        PSb = [p.t[:].bitcast(BF16) for p in PS]

        idf = sb("idf", [128, 128], F32)
        idb = sb("idb", [128, 128], BF16)
        triu = sb("triu", [128, 128], F32)
        onesf = sb("onesf", [128, 128], F32)
        nwt = sb("nwt", [128, 16], F32)
        cw = sb("cw", [128, 24, 4], F32)
        cb = sb("cb", [128, 24], F32)
        dtb_bc = sb("dtb_bc", [128, 32], F32)
        A_bc = sb("A_bc", [128, 32], F32)
        D_bc = sb("D_bc", [128, 32], F32)
        fnw_bc = sb("fnw_bc", [128, D], F32)
        cb15 = sb("cb15", [128, 8], F32)
        nlam = sb("nlam", [128, 1], F32)
        hT = sb("hT", [128, 2048], F32)
        hTb = sb("hTb", [128, 2048], BF16)
        tails = sb("tails", [128, 24, 3], F32)
        small = sb("small", [128, 64], F32)
        selt = sb("selt", [128, 17 * 7], F32)
        wselt = sb("wselt", [128, 4], F32)
        MAIN0 = apos[0]

        fw.dma("sp", idf[:], c_ident, writes=[idf])
        fw.dma("sp", triu[:], c_triu, writes=[triu])
        fw.dma("sp", nwt[:], nw, writes=[nwt])
        fw.dma("sp", cw[:], convw, writes=[cw])
        fw.dma("sp", cb[:], convb, writes=[cb])
        fw.dma("sp", dtb_bc[:], dtb.partition_broadcast(128), writes=[dtb_bc])
        fw.dma("sp", A_bc[:], alog.partition_broadcast(128), writes=[A_bc])
        fw.dma("sp", D_bc[:], dskip.partition_broadcast(128), writes=[D_bc])
        fw.dma("sp", fnw_bc[:], fnw.partition_broadcast(128), writes=[fnw_bc])
        fw.dma("sp", cb15[:], relb[15:16, :].partition_broadcast(128), writes=[cb15])
        fw.dma("sp", selt[:], selw, writes=[selt])
        fw.dma("sp", wselt[:], wsel, writes=[wselt])
        fw.op("dve", lambda e: e.tensor_copy(idb[:], idf[:]), reads=[idf], writes=[idb])
        fw.op("pool", lambda e: e.memset(onesf[:], 1.0), writes=[onesf])
        fw.op("pool", lambda e: e.memset(hT[:], 0.0), writes=[hT])
        fw.op("pool", lambda e: e.memset(hTb[:], 0.0), writes=[hTb])
        fw.op("pool", lambda e: e.memset(tails[:], 0.0), writes=[tails])
        fw.op("act", lambda e: e.activation(A_bc[:], A_bc[:], AF.Exp), reads=[A_bc], writes=[A_bc])
        fw.op("dve", lambda e: e.tensor_scalar_mul(A_bc[:], A_bc[:], -1.0), reads=[A_bc], writes=[A_bc])

        apos[0] = MAIN0
        lv = sb("lv", [128, 4, 128], F32)
        lp = sb("lp", [128, 2, 128], F32)
        ls = sb("ls", [128, 2], F32)
        for i in range(4):
            fw.dma("sp", lv[:, i, :], lamv[i:i + 1, :].partition_broadcast(128), writes=[lv])
        fw.op("dve", lambda e: e.tensor_tensor(lp[:, 0, :], lv[:, 0, :], lv[:, 1, :], ALU.mult), reads=[lv], writes=[lp])
        fw.op("dve", lambda e: e.tensor_tensor(lp[:, 1, :], lv[:, 2, :], lv[:, 3, :], ALU.mult), reads=[lv, lp], writes=[lp])
        fw.op("dve", lambda e: e.reduce_sum(ls[:], lp[:], AX.X), reads=[lp], writes=[ls])
        fw.op("act", lambda e: e.activation(ls[:], ls[:], AF.Exp), reads=[ls], writes=[ls])
        fw.op("dve", lambda e: e.tensor_tensor(nlam[:], ls[:, 1:2], ls[:, 0:1], ALU.subtract), reads=[ls], writes=[nlam])
        fw.op("dve", lambda e: e.tensor_scalar_add(nlam[:], nlam[:], -LAM0), reads=[nlam], writes=[nlam])

        oh = sb("oh", [32, 1152], F32)
        tab = sb("tab", [32, 8], F32)
        bvsb = sb("bvsb", [8, 1152], F32)
        anti = sb("anti", [128, 128], F32)
        msk = sb("msk", [128, 5, 512], F32)
        bvd = Buf(None, "bvd")
        fw.dma("sp", oh[:], c_oh, writes=[oh])
        fw.dma("sp", tab[:], relb, writes=[tab])
        fw.dma("sp", anti[:], c_anti, writes=[anti])
        fw.dma("sp", msk[:], c_mask, writes=[msk])
        for j in range(3):
            fw.group("pe", [lambda e, j=j: e.matmul(PS[j][0:8, 0:384], lhsT=tab[:, :], rhs=oh[:, j * 384:(j + 1) * 384],
                                                   start=True, stop=True)], reads=[tab, oh], writes=[PS[j]])
            fw.op("dve", lambda e, j=j: e.tensor_copy(bvsb[:, j * 384:(j + 1) * 384], PS[j][0:8, 0:384]),
                  reads=[PS[j]], writes=[bvsb])
        fw.dma("sp", bvs, bvsb[:], reads=[bvsb], writes=[bvd])
        trev = [sb("trev%d" % i, [128, 512], F32) for i in range(2)]
        tfl = [sb("tfl%d" % i, [128, 512], F32) for i in range(2)]
        bmbuf = Buf(None, "bmd")
        it = 0
        for h in range(8):
            for d in range(5):
                delta = 128 * (d - 1)
                tr = trev[it % 2]
                tf = tfl[it % 2]
                pp = PS[3 + it % 2]
                src = bass.AP(tensor=bvs.tensor, offset=h * 1152 + 384 - delta, ap=[[1, 128], [1, 512]])
                fw.dma("sp", tr[:], src, reads=[bvd], writes=[tr])
                fw.group("pe", [lambda e, tr=tr, pp=pp: e.matmul(pp[:, :], lhsT=anti[:, :], rhs=tr[:, :], start=True, stop=True)],
                         reads=[anti, tr], writes=[pp])
                fw.op("dve", lambda e, tf=tf, pp=pp, d=d: e.tensor_tensor(tf[:], pp[:, :], msk[:, d, :], ALU.add),
                      reads=[pp, msk], writes=[tf])
                fw.dma("pool", bmd[h, :, d * 512:(d + 1) * 512], tf[:], reads=[tf], writes=[bmbuf])
                it += 1

        Tt = sb("Tt", [128, 5, 512], F32)
        s56 = sb("s56", [128, 2], F32)
        bm2buf = Buf(None, "bm2")
        it = 0
        for h in range(8):
            fw.dma("sp", Tt[:], bmd[h].rearrange("p (d q) -> p d q", d=5), reads=[bmbuf], writes=[Tt])
            for t_ in range(17):
                o = tfl[it % 2]
                c0 = t_ * 7
                fw.op("dve", lambda e: e.tensor_scalar_mul(o[:], Tt[:, 0, :], selt[:, c0:c0 + 1]), reads=[Tt, selt], writes=[o])
                for d in range(1, 5):
                    fw.op("dve", lambda e, d=d: e.scalar_tensor_tensor(o[:], Tt[:, d, :], selt[:, c0 + d:c0 + d + 1], o[:], ALU.mult, ALU.add),
                          reads=[Tt, selt, o], writes=[o])
                fw.op("dve", lambda e: e.tensor_tensor(s56[:, 0:1], selt[:, c0 + 5:c0 + 6], cb15[:, h:h + 1], ALU.mult),
                      reads=[selt, cb15], writes=[s56])
                fw.op("dve", lambda e: e.scalar_tensor_tensor(s56[:, 1:2], selt[:, c0 + 6:c0 + 7], NEG, s56[:, 0:1], ALU.mult, ALU.add),
                      reads=[selt, s56], writes=[s56])
                fw.op("dve", lambda e: e.tensor_scalar_add(o[:], o[:], s56[:, 1:2]), reads=[o, s56], writes=[o])
                fw.dma("pool", bm2[h, t_], o[:], reads=[o], writes=[bm2buf])
                it += 1

        wf = [sb("wf%d" % i, [128, 1668], F32) for i in range(3)]
        wb_ = [sb("wb%d" % i, [128, 1668], BF16) for i in range(3)]
        sT = sb("sT", [128, 2], F32)
        snT = sb("snT", [128, 16], F32)
        fw.dma("sp", sT[:], sublnT, writes=[sT])
        fw.dma("sp", snT[:], ssmnwT, writes=[snT])
        fw.op("dve", lambda e: e.tensor_scalar_mul(sT[:], sT[:], 1.0 - LAM0), reads=[sT], writes=[sT])
        Wbuf = Buf(None, "Wb")
        Wobuf = Buf(None, "Wob")
        it = 0
        for kc in range(16):
            for pc in range(8):
                a, b = wf[it % 3], wb_[it % 3]
                c0 = pc * 1668
                fw.dma("sp", a[:], w_in[kc * 128:(kc + 1) * 128, c0:c0 + 1668], writes=[a])
                if it % 2 == 0:
                    fw.op("act", lambda e, a=a, b=b, kc=kc: e.activation(b[:], a[:], AF.Copy, scale=nwt[:, kc:kc + 1]),
                          reads=[a, nwt], writes=[b])
                else:
                    fw.op("dve", lambda e, a=a, b=b, kc=kc: e.tensor_scalar_mul(b[:], a[:], nwt[:, kc:kc + 1]),
                          reads=[a, nwt], writes=[b])
                fw.dma("pool", Wb[kc * 128:(kc + 1) * 128, c0:c0 + 1668], b[:], reads=[b], writes=[Wbuf])
                it += 1
        for kc in range(32):
            for pc in range(2):
                a, b = wf[it % 3], wb_[it % 3]
                c0 = pc * 1024
                sc = sT[:, kc % 2:kc % 2 + 1] if kc < 16 else snT[:, kc - 16:kc - 15]
                fw.dma("sp", a[:, 0:1024], w_out[kc * 128:(kc + 1) * 128, c0:c0 + 1024], writes=[a])
                if it % 2 == 0:
                    fw.op("act", lambda e, a=a, b=b, sc=sc: e.activation(b[:, 0:1024], a[:, 0:1024], AF.Copy, scale=sc),
                          reads=[a, sT, snT], writes=[b])
                else:
                    fw.op("dve", lambda e, a=a, b=b, sc=sc: e.tensor_scalar_mul(b[:, 0:1024], a[:, 0:1024], sc),
                          reads=[a, sT, snT], writes=[b])
                fw.dma("pool", Wob[kc * 128:(kc + 1) * 128, c0:c0 + 1024], b[:, 0:1024], reads=[b], writes=[Wobuf])
                it += 1
        fw.barrier()

        apos[0] = MAIN0
        xT = sb("xT", [128, 16, 512], BF16)
        xld = [sb("xld%d" % i, [128, D], F32) for i in range(2)]
        xn = sb("xn", [128, D], BF16)
        wst = [sb("wst%d" % i, [128, 16, 256], BF16) for i in range(3)]
        kf = [sb("kf%d" % i, [128, 256], F32) for i in range(2)]
        kb = [sb("kb%d" % i, [128, 256], BF16) for i in range(2)]
        KTst = [sb("KTst%d" % i, [128, 2, 512], BF16) for i in range(2)]
        o_sz = alloc(16384)
        o_xs = alloc(16384)
        sz = view(o_sz, [128, 4, 2048], BF16, "sz")
        xs_tok = view(o_xs, [128, 4, 2048], BF16, "xs_tok")
        hres = view(o_sz, [128, 4, 2048], F32, "hres")
        dts = sb("dts", [128, 4, 32], F32)
        av = sb("av", [128, 4, 32], F32)
        dtmp = [sb("dtmp%d" % i, [128, 32], F32) for i in range(3)]
        acst = sb("acst", [128, 32], F32)
        tot = sb("tot", [128, 32], F32)
        ea = sb("ea", [128, 32], F32)
        te = sb("te", [128, 32], F32)
        dec = sb("dec", [128, 32], F32)
        o_mixT = apos[0]
        mixT = sb("mixT", [128, 32, 512], BF16)
        hT_own = view(o_mixT, [128, 2048], F32, "hT_own")
        hTb_own = view(o_mixT + 8192, [128, 2048], BF16, "hTb_own")
        tails_own = view(o_mixT + 12288, [128, 24, 3], F32, "tails_own")
        U0 = apos[0]
        BT = sb("BT", [128, 4, 512], BF16)
        CT = sb("CT", [128, 4, 512], BF16)
        B_tok = sb("B_tok", [128, 4, 512], BF16)
        raw = [sb("raw%d" % i, [128, 516], F32) for i in range(2)]
        cva = [sb("cva%d" % i, [128, 512], F32) for i in range(1)] * 2
        csb = [sb("csb%d" % i, [128, 512], BF16) for i in range(2)]
        xdt = sb("xdt", [128, 2048], BF16)
        xdte = sb("xdte", [128, 2048], BF16)
        Rb = [sb("R%d" % i, [128, 512], F32) for i in range(2)]
        segs = [sb("segs%d" % i, [128, 512], F32) for i in range(2)]
        MT = [sb("MT%d" % i, [128, 512], BF16) for i in range(2)]
        cbm = sb("cbm", [128, 4, 128], F32)
        t1 = [sb("t1_%d" % i, [128, 512], F32) for i in range(2)]
        t2 = [sb("t2_%d" % i, [128, 512], F32) for i in range(1)] * 2
        ym = [sb("ym%d" % i, [128, 512], BF16) for i in range(2)]
        U1 = apos[0]
        ssd_bufs = [BT, CT, B_tok, xdt, xdte, cbm] + raw + cva + csb + Rb + segs + MT + t1 + t2 + ym
        apos[0] = U0
        kst = [sb("kst%d" % i, [128, 2048], BF16) for i in range(2)]
        vst = [sb("vst%d" % i, [128, 16, 257], BF16) for i in range(2)]
        QT = sb("QT", [128, 2, 512], BF16)
        sg = sb("sg", [128, 4, 256], BF16)
        o_bmt0 = apos[0]
        bmt = sb("bmt", [128, 5, 512], F32)
        bms = [view(o_bmt0 + i * 2048, [128, 512], F32, "bms%d" % i) for i in range(5)]
        PT = [sb("PT%d" % i, [128, 512], BF16) for i in range(3)]
        dgt = [sb("dgt%d" % i, [128, 512], F32) for i in range(1)] * 2
        att = sb("att", [128, 4, 256], F32)
        ma = [sb("ma%d" % i, [128, 256], BF16) for i in range(1)] * 2
        o_bmt = apos[0] - 0
        att_bufs = kst + vst + [QT, sg, bmt, att] + PT + dgt + ma + bms
        apos[0] = max(U1, apos[0])

        def enter_ssd_phase():
            for b_ in ssd_bufs:
                fw.alias(b_, att_bufs)

        def enter_att_phase():
            for b_ in att_bufs:
                fw.alias(b_, ssd_bufs)
        print("SBUF bytes/partition used:", apos[0])


        rr = {"ps": 0, "w": 0, "tr": 0, "sm": 0}
        smallbufs = [Buf(None, "sm%d" % i) for i in range(64)]

        def next_ps():
            rr["ps"] = (rr["ps"] + 1) % 4
            return PS[rr["ps"]]

        def next_tr():
            rr["tr"] = (rr["tr"] + 1) % 2
            return 4 + rr["tr"]

        def next_w():
            rr["w"] = (rr["w"] + 1) % 3
            return wst[rr["w"]]

        def smallcol(n=1):
            o = rr["sm"]
            if o + n > 64:
                o = 0
            rr["sm"] = o + n
            rr["smb"] = smallbufs[o]
            return o

        def load_w(col0, ncols):
            w = next_w()
            fw.dma("sp", w[:, :, 0:ncols], Wb[:, col0:col0 + ncols].rearrange("(kc p) c -> p kc c", p=128),
                   reads=[Wbuf], writes=[w])
            return w

        def evac_alt(idx, out_ap, in_ap, reads, writes):
            if idx % 2 == 0:
                fw.op("dve", lambda e: e.tensor_copy(out_ap, in_ap), reads=reads, writes=writes)
            else:
                fw.op("act", lambda e: e.copy(out_ap, in_ap), reads=reads, writes=writes)

        def transposes_to(dst_buf, dst_ap_fn, srcs, src_bufs, tp_in, n_out):
            bi = next_tr()
            pv = PSb[bi]
            n = len(srcs)
            fw.group("pe", [lambda e, j=j, s=s: e.transpose(pv[0:n_out, j * 128:j * 128 + tp_in], s, idb[0:tp_in, 0:tp_in])
                            for j, s in enumerate(srcs)], reads=list(src_bufs) + [idb], writes=[PS[bi]])
            pview = pv[0:n_out, 0:n * 128].rearrange("p (a b) -> p a b", a=n)[:, :, 0:tp_in]
            dst_ap_fn(pview, PS[bi])

        def rms_rstd(ss_ap, ss_buf, n, tp):
            fw.op("act", lambda e: e.activation(ss_ap, ss_ap, AF.Sqrt, bias=EPS, scale=1.0 / n), reads=[ss_buf], writes=[ss_buf])
            fw.op("dve", lambda e: e.reciprocal(ss_ap, ss_ap), reads=[ss_buf], writes=[ss_buf])

        def process_block(cfg):
            tp, ntt = cfg["tp"], cfg["ntt"]
            NQ = tp * ntt
            xsrc = cfg["xsrc"]
            phases = cfg.get("phases", "ABCDEF")
            for tt in range(ntt):
                xl = xld[tt % 2]
                fw.dma("sp", xl[0:tp, :], xsrc[tt * tp:(tt + 1) * tp, :], writes=[xl])
                c = smallcol()
                ssap = small[0:tp, c:c + 1]
                fw.op("pool", lambda e: e.memset(ssap, 0.0), writes=[rr["smb"]])
                fw.op("act", lambda e: e.activation(xn[0:tp, :], xl[0:tp, :], AF.Square, accum_out=ssap),
                      reads=[xl], writes=[xn, rr["smb"]])
                rms_rstd(ssap, rr["smb"], D, tp)
                fw.op("act", lambda e: e.activation(xn[0:tp, :], xl[0:tp, :], AF.Copy, scale=ssap),
                      reads=[xl, rr["smb"]], writes=[xn])
                for half in range(2):
                    def cp(pview, pbuf, half=half, tt=tt):
                        evac_alt(half, xT[:, half * 8:(half + 1) * 8, tt * tp:(tt + 1) * tp], pview, [pbuf], [xT])
                    transposes_to(xT, cp, [xn[0:tp, (half * 8 + j) * 128:(half * 8 + j + 1) * 128] for j in range(8)],
                                  [xn], tp, 128)

            def inproj_tok(col0, ncols, consume):
                w = load_w(col0, ncols)
                for tt in range(ntt):
                    ps = next_ps()
                    fw.group("pe", [lambda e, kc=kc: e.matmul(ps[0:tp, 0:ncols], lhsT=xT[:, kc, tt * tp:(tt + 1) * tp],
                                                              rhs=w[:, kc, 0:ncols], start=(kc == 0), stop=(kc == 15))
                                    for kc in range(16)], reads=[xT, w], writes=[ps])
                    consume(tt, ps)

            def inproj_feat(w, j, consume):
                ps = next_ps()
                fw.group("pe", [lambda e, kc=kc: e.matmul(ps[:, 0:NQ], lhsT=w[:, kc, j * 128:(j + 1) * 128],
                                                          rhs=xT[:, kc, 0:NQ], start=(kc == 0), stop=(kc == 15))
                                for kc in range(16)], reads=[xT, w], writes=[ps])
                consume(ps)

            if stage <= 1:
                return
            if "B" in phases or "C" in phases:
                ktd, vd_, kout, vout = cfg.get("KT"), cfg.get("V"), cfg.get("kout"), cfg.get("vout")
                tok0 = cfg.get("tok0")
                kvb = cfg.get("kvbuf")
                tl = cfg.get("tails", tails)
                hT_ = cfg.get("hT", hT)
                hTb_ = cfg.get("hTb", hTb)
                do_y = cfg.get("do_y", True)
                if cfg.get("save_i") is not None:
                    si_ = cfg["save_i"]
                    if si_ == 0:
                        fw.alias(hT_own, [mixT])
                        fw.alias(tails_own, [mixT])
                        fw.op("dve", lambda e: e.tensor_scalar_mul(hT_own[:, :], hT[:, :], wselt[:, 0:1]), reads=[hT, wselt], writes=[hT_own])
                        fw.op("dve", lambda e: e.tensor_scalar_mul(tails_own[:, :, :], tails[:, :, :], wselt[:, 0:1]),
                              reads=[tails, wselt], writes=[tails_own])
                    else:
                        fw.op("dve", lambda e: e.scalar_tensor_tensor(hT_own[:, :], hT[:, :], wselt[:, si_:si_ + 1], hT_own[:, :],
                                                                      ALU.mult, ALU.add), reads=[hT, wselt, hT_own], writes=[hT_own])
                        fw.op("dve", lambda e: e.scalar_tensor_tensor(tails_own[:, :, :], tails[:, :, :], wselt[:, si_:si_ + 1],
                                                                      tails_own[:, :, :], ALU.mult, ALU.add),
                              reads=[tails, wselt, tails_own], writes=[tails_own])
                if cfg.get("init_hTb"):
                    fw.alias(hTb_own, [mixT])
                    fw.op("act", lambda e: e.copy(hTb_[:, :], hT_[:, :]), reads=[hT_], writes=[hTb_])
                for cc in (range(16) if "B" in phases else []):
                    isk = cc < 8
                    col0 = (OK_ if isk else OV) + (cc % 8) * 256
                    kts = KTst[cc % 2]

                    def cons(tt, ps, cc=cc, isk=isk, kts=kts):
                        f = kf[(cc * ntt + tt) % 2]
                        b = kb[(cc * ntt + tt) % 2]
                        if stage <= 1.2:
                            return
                        fw.op("act", lambda e: e.copy(f[0:tp, :], ps[0:tp, 0:256]), reads=[ps], writes=[f])
                        fw.op("dve", lambda e: e.tensor_copy(b[0:tp, :], f[0:tp, :]), reads=[f], writes=[b])
                        if stage <= 1.4:
                            return
                        dst = kout if isk else vout
                        fw.dma("pool", dst[tt * tp:(tt + 1) * tp, (cc % 8) * 256:(cc % 8 + 1) * 256], f[0:tp, :], reads=[f])
                        if stage <= 1.6:
                            return
                        if isk:
                            def cp(pview, pbuf):
                                fw.op("dve", lambda e: e.tensor_copy(kts[:, :, tt * tp:(tt + 1) * tp], pview), reads=[pbuf], writes=[kts])
                            transposes_to(kts, cp, [b[0:tp, j * 128:(j + 1) * 128] for j in range(2)], [b], tp, 128)
                        else:
                            fw.dma("pool", vd_[tok0 + tt * tp: tok0 + (tt + 1) * tp, (cc % 8) * 256:(cc % 8 + 1) * 256], b[0:tp, :],
                                   reads=[b], writes=[kvb])
                    inproj_tok(col0, 256, cons)
                    if isk and stage > 1.8:
                        hm0 = (cc % 8) * 2
                        fw.dma("pool", ktd[hm0:hm0 + 2, :, tok0:tok0 + NQ].rearrange("a d t -> d a t"), kts[:, :, 0:NQ],
                               reads=[kts], writes=[kvb])

                if stage <= 2:
                    return
                enter_ssd_phase()
                fw.alias(sz, [hres])
                fw.alias(xs_tok, [hres])
                for cc in (range(8) if cfg.get("do_z", True) else []):
                    def consz(tt, ps, cc=cc):
                        fw.op("act", lambda e: e.activation(sz[0:tp, tt, cc * 256:(cc + 1) * 256], ps[0:tp, 0:256], AF.Silu),
                              reads=[ps], writes=[sz])
                    inproj_tok(OZ + cc * 256, 256, consz)
                for cc in range(cfg.get("n_xbc", 12)):
                    w = load_w(OX + cc * 256, 256)
                    for j in range(2):
                        ct = cc * 2 + j
                        rw = raw[ct % 2]
                        ca = cva[ct % 2]

                        def consx(ps, ct=ct, rw=rw, ca=ca):
                            fw.op("pool", lambda e: e.tensor_copy(rw[:, 0:3], tl[:, ct, :]), reads=[tl], writes=[rw])
                            fw.op("act", lambda e: e.copy(rw[:, 3:3 + NQ], ps[:, 0:NQ]), reads=[ps], writes=[rw])
                            fw.op("pool", lambda e: e.tensor_copy(tl[:, ct, :], rw[:, NQ:NQ + 3]), reads=[rw], writes=[tl])
                            fw.op("dve", lambda e: e.tensor_scalar(ca[:, 0:NQ], rw[:, 3:3 + NQ], cw[:, ct, 3:4], cb[:, ct:ct + 1],
                                                                   ALU.mult, ALU.add), reads=[rw, cw, cb], writes=[ca])
                            for jj in range(3):
                                fw.op("dve", lambda e, jj=jj: e.scalar_tensor_tensor(ca[:, 0:NQ], rw[:, jj:jj + NQ], cw[:, ct, jj:jj + 1],
                                                                                   ca[:, 0:NQ], ALU.mult, ALU.add),
                                      reads=[rw, cw, ca], writes=[ca])
                            if ct < 16:
                                cs = csb[ct % 2]
                                fw.op("act", lambda e: e.activation(cs[:, 0:NQ], ca[:, 0:NQ], AF.Silu), reads=[ca], writes=[cs])

                                def cp(pview, pbuf):
                                    evac_alt(ct, xs_tok[0:tp, 0:ntt, ct * 128:(ct + 1) * 128], pview, [pbuf], [xs_tok])
                                transposes_to(xs_tok, cp, [cs[:, tt * tp:(tt + 1) * tp] for tt in range(ntt)], [cs], 128, tp)
                            elif ct < 20:
                                g = ct - 16
                                fw.op("act", lambda e: e.activation(BT[:, g, 0:NQ], ca[:, 0:NQ], AF.Silu), reads=[ca], writes=[BT])

                                def cp(pview, pbuf):
                                    evac_alt(ct, B_tok[0:tp, 0:ntt, g * 128:(g + 1) * 128], pview, [pbuf], [B_tok])
                                transposes_to(B_tok, cp, [BT[:, g, tt * tp:(tt + 1) * tp] for tt in range(ntt)], [BT], 128, tp)
                            else:
                                g = ct - 20
                                fw.op("act", lambda e: e.activation(CT[:, g, 0:NQ], ca[:, 0:NQ], AF.Silu), reads=[ca], writes=[CT])
                        inproj_feat(w, j, consx)

                def consdt(tt, ps):
                    d0, d1, d2 = dtmp
                    fw.op("dve", lambda e: e.tensor_tensor(d0[0:tp, :], ps[0:tp, 0:32], dtb_bc[0:tp, :], ALU.add),
                          reads=[ps, dtb_bc], writes=[d0])
                    fw.op("dve", lambda e: e.scalar_tensor_tensor(d1[0:tp, :], d0[0:tp, :], -1.0, d0[0:tp, :], ALU.mult, ALU.max),
                          reads=[d0], writes=[d1])
                    fw.op("act", lambda e: e.activation(d1[0:tp, :], d1[0:tp, :], AF.Exp, scale=-1.0), reads=[d1], writes=[d1])
                    fw.op("act", lambda e: e.activation(d1[0:tp, :], d1[0:tp, :], AF.Ln, bias=1.0), reads=[d1], writes=[d1])
                    fw.op("dve", lambda e: e.scalar_tensor_tensor(dts[0:tp, tt, :], d0[0:tp, :], 0.0, d1[0:tp, :], ALU.max, ALU.add),
                          reads=[d0, d1], writes=[dts])
                    fw.op("dve", lambda e: e.tensor_tensor(av[0:tp, tt, :], dts[0:tp, tt, :], A_bc[0:tp, :], ALU.mult),
                          reads=[dts, A_bc], writes=[av])
                inproj_tok(ODT, 32, consdt)

                if stage <= 3:
                    return
                def bc3(ap2, n):
                    return ap2.unsqueeze(2).to_broadcast([ap2.shape[0], ap2.shape[1], n])

                for c in range(ntt):
                    sl = slice(c * tp, (c + 1) * tp)
                    psA = next_ps()
                    fw.group("pe", [lambda e: e.matmul(psA[0:tp, 0:32], lhsT=triu[0:tp, 0:tp], rhs=av[0:tp, c, :], start=True, stop=True),
                                    lambda e: e.matmul(psA[:, 32:64], lhsT=onesf[0:tp, :], rhs=av[0:tp, c, :], start=True, stop=True)],
                             reads=[triu, onesf, av], writes=[psA])
                    fw.op("dve", lambda e: e.tensor_copy(acst[0:tp, :], psA[0:tp, 0:32]), reads=[psA], writes=[acst])
                    fw.op("dve", lambda e: e.tensor_copy(tot[:, :], psA[:, 32:64]), reads=[psA], writes=[tot])
                    if do_y:
                        fw.op("act", lambda e: e.activation(ea[0:tp, :], acst[0:tp, :], AF.Exp), reads=[acst], writes=[ea])
                    fw.op("act", lambda e: e.activation(dec[:, :], tot[:, :], AF.Exp), reads=[tot], writes=[dec])
                    fw.op("dve", lambda e: e.tensor_tensor(te[0:tp, :], tot[0:tp, :], acst[0:tp, :], ALU.subtract),
                          reads=[tot, acst], writes=[te])
                    fw.op("act", lambda e: e.activation(te[0:tp, :], te[0:tp, :], AF.Exp), reads=[te], writes=[te])
                    fw.op("dve", lambda e: e.tensor_tensor(te[0:tp, :], te[0:tp, :], dts[0:tp, c, :], ALU.mult),
                          reads=[te, dts], writes=[te])
                    xv = xs_tok[0:tp, c, :].rearrange("p (h q) -> p h q", h=32)
                    if do_y:
                        fw.op("dve", lambda e: e.tensor_tensor(xdt[0:tp, :].rearrange("p (h q) -> p h q", h=32), xv,
                                                               bc3(dts[0:tp, c, :], 64), ALU.mult), reads=[xs_tok, dts], writes=[xdt])
                    fw.op("pool", lambda e: e.tensor_tensor(xdte[0:tp, :].rearrange("p (h q) -> p h q", h=32), xv,
                                                            bc3(te[0:tp, :], 64), ALU.mult), reads=[xs_tok, te], writes=[xdte])
                    if do_y:
                        psC = next_ps()
                        fw.group("pe", [lambda e, g=g: e.matmul(psC[0:tp, g * 128:g * 128 + tp], lhsT=BT[:, g, sl], rhs=CT[:, g, sl],
                                                                start=True, stop=True) for g in range(4)],
                                 reads=[BT, CT], writes=[psC])
                        fw.op("dve", lambda e: e.tensor_tensor(cbm[0:tp, :, 0:tp],
                                                               psC[0:tp, :].rearrange("p (g t) -> p g t", g=4)[:, :, 0:tp],
                                                               triu[0:tp, 0:tp].unsqueeze(1).to_broadcast([tp, 4, tp]), ALU.mult),
                              reads=[psC, triu], writes=[cbm])
                        for g in range(4):
                            psY = next_ps()
                            for hg in (2 * g, 2 * g + 1):
                                h0 = hg * 4
                                R = Rb[hg % 2]
                                sgm = segs[hg % 2]
                                M = MT[hg % 2]
                                R3 = R[0:tp, 0:4 * tp].rearrange("p (h t) -> p h t", h=4)
                                fw.op("pool", lambda e: e.tensor_tensor(R3, triu[0:tp, 0:tp].unsqueeze(1).to_broadcast([tp, 4, tp]),
                                                                        bc3(av[0:tp, c, h0:h0 + 4], tp), ALU.mult),
                                      reads=[triu, av], writes=[R])
                                psB = next_ps()
                                fw.group("pe", [lambda e: e.matmul(psB[0:tp, 0:4 * tp], lhsT=onesf[0:tp, 0:tp], rhs=R[0:tp, 0:4 * tp],
                                                                   start=True, stop=True)], reads=[onesf, R], writes=[psB])
                                for hh in range(4):
                                    fw.op("dve", lambda e, hh=hh: e.tensor_scalar(sgm[0:tp, hh * tp:(hh + 1) * tp], psB[0:tp, hh * tp:(hh + 1) * tp],
                                                                                  acst[0:tp, h0 + hh:h0 + hh + 1], 0.0, ALU.subtract, ALU.min),
                                          reads=[psB, acst], writes=[sgm])
                                fw.op("act", lambda e: e.activation(sgm[0:tp, 0:4 * tp], sgm[0:tp, 0:4 * tp], AF.Exp), reads=[sgm], writes=[sgm])
                                fw.op("dve", lambda e: e.tensor_tensor(M[0:tp, 0:4 * tp].rearrange("p (h t) -> p h t", h=4),
                                                                       sgm[0:tp, 0:4 * tp].rearrange("p (h t) -> p h t", h=4),
                                                                       cbm[0:tp, g, 0:tp].unsqueeze(1).to_broadcast([tp, 4, tp]), ALU.mult),
                                      reads=[sgm, cbm], writes=[M])
                                fw.group("pe", [lambda e, hh=hh: e.matmul(psY[0:tp, ((h0 + hh) % 8) * 64:((h0 + hh) % 8) * 64 + 64],
                                                                          lhsT=M[0:tp, hh * tp:(hh + 1) * tp],
                                                                          rhs=xdt[0:tp, (h0 + hh) * 64:(h0 + hh + 1) * 64], start=True, stop=True)
                                                for hh in range(4)], reads=[M, xdt], writes=[psY])
                            psO = next_ps()
                            fw.group("pe", [lambda e: e.matmul(psO[0:tp, 0:512], lhsT=CT[:, g, sl], rhs=hTb_[:, g * 512:(g + 1) * 512],
                                                               start=True, stop=True)], reads=[CT, hTb_], writes=[psO])
                            a1, a2, yb = t1[g % 2], t2[g % 2], ym[g % 2]
                            gs = slice(g * 512, (g + 1) * 512)
                            v3 = lambda ap: ap.rearrange("p (h q) -> p h q", h=8)
                            fw.op("dve", lambda e: e.tensor_tensor(v3(a1[0:tp, :]), v3(psO[0:tp, 0:512]), bc3(ea[0:tp, 8 * g:8 * g + 8], 64), ALU.mult),
                                  reads=[psO, ea], writes=[a1])
                            fw.op("dve", lambda e: e.tensor_tensor(a1[0:tp, :], a1[0:tp, :], psY[0:tp, 0:512], ALU.add), reads=[a1, psY], writes=[a1])
                            fw.op("pool", lambda e: e.tensor_tensor(v3(a2[0:tp, :]), v3(xs_tok[0:tp, c, gs]), bc3(D_bc[0:tp, 8 * g:8 * g + 8], 64), ALU.mult),
                                  reads=[xs_tok, D_bc], writes=[a2])
                            fw.op("pool", lambda e: e.tensor_tensor(a1[0:tp, :], a1[0:tp, :], a2[0:tp, :], ALU.add), reads=[a1, a2], writes=[a1])
                            fw.op("pool", lambda e: e.tensor_tensor(a1[0:tp, :], a1[0:tp, :], sz[0:tp, c, gs], ALU.mult), reads=[a1, sz], writes=[a1])
                            cc_ = smallcol()
                            ssap = small[0:tp, cc_:cc_ + 1]
                            fw.op("pool", lambda e: e.memset(ssap, 0.0), writes=[rr["smb"]])
                            fw.op("act", lambda e: e.activation(a2[0:tp, :], a1[0:tp, :], AF.Square, accum_out=ssap), reads=[a1], writes=[a2, rr["smb"]])
                            rms_rstd(ssap, rr["smb"], 512, tp)
                            fw.op("act", lambda e: e.activation(yb[0:tp, :], a1[0:tp, :], AF.Copy, scale=ssap), reads=[a1, rr["smb"]], writes=[yb])

                            def cp(pview, pbuf, g=g):
                                evac_alt(g, mixT[:, 16 + 4 * g:16 + 4 * g + 4, sl], pview, [pbuf], [mixT])
                            transposes_to(mixT, cp, [yb[0:tp, j * 128:(j + 1) * 128] for j in range(4)], [yb], tp, 128)
                    for g in range(4):
                        psH = next_ps()
                        gs = slice(g * 512, (g + 1) * 512)
                        fw.group("pe", [lambda e: e.matmul(psH[:, 0:512], lhsT=B_tok[0:tp, c, g * 128:(g + 1) * 128], rhs=xdte[0:tp, gs],
                                                           start=True, stop=True)], reads=[B_tok, xdte], writes=[psH])
                        h3 = hT_[:, gs].rearrange("p (h q) -> p h q", h=8)
                        fw.op("dve", lambda e: e.tensor_tensor(h3, h3, bc3(dec[:, 8 * g:8 * g + 8], 64), ALU.mult), reads=[hT_, dec], writes=[hT_])
                        fw.op("dve", lambda e: e.tensor_tensor(hT_[:, gs], hT_[:, gs], psH[:, 0:512], ALU.add), reads=[hT_, psH], writes=[hT_])
                    if do_y:
                        fw.op("act", lambda e: e.copy(hTb_[:, :], hT_[:, :]), reads=[hT_], writes=[hTb_])

            if "E" not in phases:
                return
            fw.alias(mixT, [hT_own, hTb_own, tails_own])
            if stage <= 4:
                return
            enter_att_phase()
            for v in vst:
                fw.op("pool", lambda e, v=v: e.memset(v[:, :, 256:257], 1.0), writes=[v])
            ktiles = cfg["ktiles"]
            pieces = cfg["pieces"]
            for h in range(8):
                wq = load_w(OQ + h * 256, 256)
                for m in range(2):
                    def consq(ps, m=m):
                        fw.op("act", lambda e: e.copy(QT[:, m, 0:NQ], ps[:, 0:NQ]), reads=[ps], writes=[QT])
                    inproj_feat(wq, m, consq)

                def consg(tt, ps):
                    fw.op("act", lambda e: e.activation(sg[0:tp, tt, :], ps[0:tp, 0:256], AF.Silu), reads=[ps], writes=[sg])
                inproj_tok(OG + h * 256, 256, consg)
                if cfg.get("use_bmd", False):
                    fw.dma("sp", bmt[:], bmd[h].rearrange("p (d q) -> p d q", d=5), reads=[bmbuf], writes=[bmt])
                for m in range(2):
                    loaded = {}

                    def get_piece(pid, m=m, h=h, loaded=loaded):
                        if pid in loaded:
                            return loaded[pid]
                        ksrc, vsrc, nkeys, bufs = pieces[pid]
                        kbuf_ = kst[pid % 2]
                        vbuf_ = vst[pid % 2]
                        fw.dma("sp", kbuf_[:, 0:nkeys], ksrc(h * 2 + m), reads=bufs, writes=[kbuf_])
                        nfull = nkeys // 128
                        if nfull > 0:
                            fw.dma("sp", vbuf_[:, 0:nfull, 0:256], vsrc(h, 0, nfull * 128).rearrange("(kt p) e -> p kt e", p=128),
                                   reads=bufs, writes=[vbuf_])
                        rem = nkeys - nfull * 128
                        if rem > 0:
                            fw.dma("sp", vbuf_[0:rem, nfull, 0:256], vsrc(h, nfull * 128, rem), reads=bufs, writes=[vbuf_])
                        loaded[pid] = (kbuf_, vbuf_)
                        return loaded[pid]

                    nt = len(ktiles)
                    first_for_qs = {}
                    sinfo = {}

                    def emitS(i):
                        pid, idx, nk, diag, qsv = ktiles[i]
                        kbuf_, vbuf_ = get_piece(pid)
                        ps = PS[i % 4]
                        fw.group("pe", [lambda e: e.matmul(ps[0:nk, 0:NQ], lhsT=kbuf_[:, idx * 128:idx * 128 + nk], rhs=QT[:, m, 0:NQ],
                                                           start=True, stop=True)], reads=[kbuf_, QT], writes=[ps])
                        p = PT[i % 3]
                        if diag is None:
                            fw.op("act", lambda e: e.activation(p[0:nk, 0:NQ], ps[0:nk, 0:NQ], AF.Exp, bias=cb15[0:nk, h:h + 1], scale=SCALE),
                                  reads=[ps, cb15], writes=[p])
                        elif isinstance(diag, tuple):
                            dg = dgt[i % 2]
                            slot = bms[diag[1] % 5]
                            fw.dma("sp", slot[:], bm2[h, diag[1]], reads=[bm2buf], writes=[slot])
                            fw.op("dve", lambda e: e.scalar_tensor_tensor(dg[0:nk, 0:NQ], ps[0:nk, 0:NQ], SCALE, slot[0:nk, 0:NQ],
                                                                          ALU.mult, ALU.add), reads=[ps, slot], writes=[dg])
                            fw.op("act", lambda e: e.activation(p[0:nk, 0:NQ], dg[0:nk, 0:NQ], AF.Exp), reads=[dg], writes=[p])
                        else:
                            dg = dgt[i % 2]
                            fw.op("dve", lambda e: e.scalar_tensor_tensor(dg[0:nk, 0:NQ], ps[0:nk, 0:NQ], SCALE, bmt[0:nk, diag, 0:NQ],
                                                                          ALU.mult, ALU.add), reads=[ps, bmt], writes=[dg])
                            fw.op("act", lambda e: e.activation(p[0:nk, 0:NQ], dg[0:nk, 0:NQ], AF.Exp), reads=[dg], writes=[p])
                        sinfo[i] = (p, vbuf_)

                    last_for_qs = {}
                    for i, (pid, idx, nk, diag, qsv) in enumerate(ktiles):
                        for qs in qsv:
                            last_for_qs[qs] = i
                            if qs not in first_for_qs:
                                first_for_qs[qs] = i

                    def emitPV(i):
                        pid, idx, nk, diag, qsv = ktiles[i]
                        p, vbuf_ = sinfo.pop(i)
                        fw.group("pe", [lambda e, qs=qs: e.matmul(PS[4 + qs][0:tp, 0:257], lhsT=p[0:nk, qs * tp:(qs + 1) * tp],
                                                                  rhs=vbuf_[0:nk, idx, 0:257], start=(i == first_for_qs[qs]),
                                                                  stop=(i == last_for_qs[qs])) for qs in qsv],
                                 reads=[p, vbuf_], writes=[PS[4 + qs] for qs in qsv])

                    LA = 2
                    for i in range(nt + LA):
                        if i < nt:
                            emitS(i)
                        if i >= LA:
                            emitPV(i - LA)
                        if i < nt and ktiles[i][1] == LA and (ktiles[i][0] + 1) in pieces:
                            get_piece(ktiles[i][0] + 1)
                    for qs in range(ntt):
                        acc = PS[4 + qs]
                        c_ = smallcol()
                        rc = small[0:tp, c_:c_ + 1]
                        fw.op("dve", lambda e: e.reciprocal(rc, acc[0:tp, 256:257]), reads=[acc], writes=[rr["smb"]])
                        if m == 0:
                            fw.op("dve", lambda e: e.tensor_scalar_mul(att[0:tp, qs, :], acc[0:tp, 0:256], rc), reads=[acc, rr["smb"]], writes=[att])
                        else:
                            fw.op("dve", lambda e: e.tensor_tensor(rc, rc, nlam[0:tp, :], ALU.mult), reads=[rr["smb"], nlam], writes=[rr["smb"]])
                            fw.op("dve", lambda e: e.scalar_tensor_tensor(att[0:tp, qs, :], acc[0:tp, 0:256], rc, att[0:tp, qs, :],
                                                                          ALU.mult, ALU.add), reads=[acc, rr["smb"], att], writes=[att])
                    if m == 1:
                        for qs in range(ntt):
                            c2 = smallcol()
                            ssap = small[0:tp, c2:c2 + 1]
                            mab = ma[qs % 2]
                            fw.op("pool", lambda e: e.memset(ssap, 0.0), writes=[rr["smb"]])
                            fw.op("act", lambda e: e.activation(mab[0:tp, :], att[0:tp, qs, :], AF.Square, accum_out=ssap),
                                  reads=[att], writes=[mab, rr["smb"]])
                            rms_rstd(ssap, rr["smb"], 256, tp)
                            fw.op("dve", lambda e: e.scalar_tensor_tensor(mab[0:tp, :], att[0:tp, qs, :], ssap, sg[0:tp, qs, :],
                                                                          ALU.mult, ALU.mult), reads=[att, rr["smb"], sg], writes=[mab])

                            def cp(pview, pbuf, qs=qs):
                                evac_alt(qs, mixT[:, 2 * h:2 * h + 2, qs * tp:(qs + 1) * tp], pview, [pbuf], [mixT])
                            transposes_to(mixT, cp, [mab[0:tp, j * 128:(j + 1) * 128] for j in range(2)], [mab], tp, 128)

            if stage <= 5:
                return
            fw.alias(hres, [sz, xs_tok])
            for oc in range(16):
                col0 = oc * 128
                w = next_w()
                wv = w[:, :, :].rearrange("p a b -> p (a b)").rearrange("p (a b) -> p a b", a=32)
                fw.dma("sp", wv, Wob[:, col0:col0 + 128].rearrange("(kc p) c -> p kc c", p=128), reads=[Wobuf], writes=[w])
                for tt in range(ntt):
                    ps = next_ps()
                    fw.group("pe", [lambda e, kc=kc: e.matmul(ps[0:tp, 0:128], lhsT=mixT[:, kc, tt * tp:(tt + 1) * tp], rhs=wv[:, kc, :],
                                                              start=(kc == 0), stop=(kc == 31)) for kc in range(32)],
                             reads=[mixT, w], writes=[ps])
                    evac_alt(tt, hres[0:tp, tt, col0:col0 + 128], ps[0:tp, 0:128], [ps], [hres])
            yout = cfg["yout"]
            for tt in range(ntt):
                xl = xld[tt % 2]
                fw.dma("sp", xl[0:tp, :], xsrc[tt * tp:(tt + 1) * tp, :], writes=[xl])
                fw.op("dve", lambda e: e.tensor_tensor(xl[0:tp, :], xl[0:tp, :], hres[0:tp, tt, :], ALU.add), reads=[xl, hres], writes=[xl])
                c_ = smallcol()
                ssap = small[0:tp, c_:c_ + 1]
                fw.op("pool", lambda e: e.memset(ssap, 0.0), writes=[rr["smb"]])
                fw.op("act", lambda e: e.activation(hres[0:tp, tt, :], xl[0:tp, :], AF.Square, accum_out=ssap), reads=[xl], writes=[hres, rr["smb"]])
                rms_rstd(ssap, rr["smb"], D, tp)
                fw.op("dve", lambda e: e.scalar_tensor_tensor(xl[0:tp, :], xl[0:tp, :], ssap, fnw_bc[0:tp, :], ALU.mult, ALU.mult),
                      reads=[xl, rr["smb"], fnw_bc], writes=[xl])
                fw.dma("pool", yout[tt * tp:(tt + 1) * tp, :], xl[0:tp, :], reads=[xl])

        def write_state_outputs(conv_out, ssm_out):
            enter_ssd_phase()
            for ct in range(24):
                fw.dma("pool", conv_out[:, ct * 128:(ct + 1) * 128].rearrange("j p -> p j"), tails[:, ct, :], reads=[tails],
                       allow_slow_non_contiguous=True)
            for i in range(16):
                ps = next_ps()
                fw.group("pe", [lambda e: e.transpose(ps[:, 0:128], hT[:, i * 128:(i + 1) * 128], idf[:, :])], reads=[hT, idf], writes=[ps])
                st = t1[i % 2]
                evac_alt(i, st[:, 0:128], ps[:, 0:128], [ps], [st])
                fw.dma("pool", ssm_out[i * 128:(i + 1) * 128, :], st[:, 0:128], reads=[st])

        kvbufs = [Buf(None, "kv%d" % i) for i in range(NBLK)]
        for mstep in range(NBLK // 4 if stage > 0 else 0):
            for bi_ in range(4):
                blk = 4 * mstep + bi_
                tok0 = blk * 512
                cfg = dict(tp=128, ntt=4, xsrc=xp[tok0:tok0 + 512, :], KT=KTd, V=Vd, kout=k_p[tok0:tok0 + 512, :],
                           vout=v_p[tok0:tok0 + 512, :], tok0=tok0, kvbuf=kvbufs[blk], phases="ABCD", save_i=bi_, do_z=False, n_xbc=12, do_y=False)
                process_block(cfg)
            nkt = 16 * (mstep + 1)
            ktiles = []
            for kt in range(nkt):
                t_ = kt - 16 * mstep
                ktiles.append((kt // 16, kt % 16, 128, (None if t_ < -1 else ("dram", t_ + 1)), [0, 1, 2, 3]))
            pieces = {}
            for pid in range(mstep + 1):
                k0 = pid * 2048
                pieces[pid] = (lambda hm, k0=k0: KTd[hm, :, k0:k0 + 2048],
                               lambda h, o, n, k0=k0: Vd[k0 + o:k0 + o + n, h * 256:(h + 1) * 256],
                               2048, kvbufs[4 * pid:4 * pid + 4])
            cfg = dict(tp=128, ntt=4, xsrc=xown[mstep * 512:(mstep + 1) * 512, :], phases="ACDEF", tails=tails_own, hT=hT_own,
                       hTb=hTb_own, init_hTb=True,
                       ktiles=ktiles, pieces=pieces, yout=y_p[mstep * 512:(mstep + 1) * 512, :])
            process_block(cfg)
        if stage > 5:
            write_state_outputs(conv_p, ssm_p)

        if sample:
            fw.barrier()
            skv = Buf(None, "skv")
            for kt in range(16):
                xl = xld[kt % 2]
                fw.dma("sp", xl[:, :], ck[kt * 128:(kt + 1) * 128, :], writes=[xl])
                fw.op("act", lambda e: e.copy(xn[:, :], xl[:, :]), reads=[xl], writes=[xn])
                kts = mixT
                for half in range(2):
                    def cp(pview, pbuf, half=half):
                        evac_alt(half, mixT[:, half * 8:(half + 1) * 8, 0:128], pview, [pbuf], [mixT])
                    transposes_to(mixT, cp, [xn[:, (half * 8 + j) * 128:(half * 8 + j + 1) * 128] for j in range(8)], [xn], 128, 128)
                fw.dma("pool", KTsd[:, :, kt * 128:(kt + 1) * 128].rearrange("a d t -> d a t"), mixT[:, 0:16, 0:128],
                       reads=[mixT], writes=[skv])
                xl2 = xld[(kt + 1) % 2]
                fw.dma("sp", xl2[:, :], cv[kt * 128:(kt + 1) * 128, :], writes=[xl2])
                fw.op("dve", lambda e: e.tensor_copy(xs_tok[:, 0, :], xl2[:, :]), reads=[xl2], writes=[xs_tok])
                fw.dma("pool", Vsd[kt * 128:(kt + 1) * 128, :], xs_tok[:, 0, :], reads=[xs_tok], writes=[skv])
            fw.dma("sp", tails[:], cconv, writes=[tails])
            enter_ssd_phase()
            for i in range(16):
                st = t1[i % 2]
                fw.dma("sp", st[:, 0:128], sst[i * 128:(i + 1) * 128, :], writes=[st])
                ps = next_ps()
                fw.group("pe", [lambda e: e.transpose(ps[:, 0:128], st[:, 0:128], idf[:, :])], reads=[st, idf], writes=[ps])
                evac_alt(i, hT[:, i * 128:(i + 1) * 128], ps[:, 0:128], [ps], [hT])
            fw.op("act", lambda e: e.copy(hTb[:, :], hT[:, :]), reads=[hT], writes=[hTb])
            ktiles = []
            for kt in range(16):
                ktiles.append((kt // 16, kt % 16, 128, (0 if kt == 15 else None), [0]))
            ktiles.append((1, 0, 32, 1, [0]))
            pieces = {
                0: (lambda hm: KTsd[hm, :, 0:2048], lambda h, o, n: Vsd[o:o + n, h * 256:(h + 1) * 256], 2048, [skv]),
                1: (lambda hm: KTsd[hm, :, 2048:2080], lambda h, o, n: Vsd[2048 + o:2048 + o + n, h * 256:(h + 1) * 256], 32, [skv]),
            }
            cfg = dict(tp=32, ntt=1, xsrc=xsm, KT=KTsd, V=Vsd, kout=k_s, vout=v_s, tok0=PAST, kvbuf=skv, ktiles=ktiles,
                       pieces=pieces, yout=y_s, use_bmd=True)
            process_block(cfg)
            write_state_outputs(conv_s, ssm_s)

        fw.barrier()
        print("instructions:", fw.ninstr)
    return nc


def host_consts():
    ident = np.eye(128, dtype=np.float32)
    triu = np.triu(np.ones((128, 128), np.float32))
    anti = np.ascontiguousarray(ident[::-1])
    idx = np.arange(1152)
    rel = (511 - idx).astype(np.int32)
    b = rel_bucket_np(rel)
    oh = np.zeros((32, 1152), np.float32)
    oh[b, idx] = 1.0
    mask = np.zeros((128, 5, 512), np.float32)
    k = np.arange(128)[:, None]
    q = np.arange(512)[None, :]
    for d in range(5):
        vis = ((128 * (d - 1) + k) // 64) <= (q // 64)
        mask[:, d, :] = np.where(vis, 0.0, NEG)
    return dict(c_ident=ident, c_triu=triu, c_anti=anti, c_oh=oh, c_mask=mask)


def make_in_maps(inp, NBLK=32):
    f = lambda a: np.ascontiguousarray(np.asarray(a, dtype=np.float32))
    SEQ = NBLK * 512
    consts = host_consts()
    common = dict(
        relb=f(inp["rel_bias"]),
        nw=f(np.asarray(inp["norm_w"])[0].reshape(16, 128).T),
        w_in=f(inp["w_in"][0]), w_out=f(inp["w_out"][0]),
        lamv=f(np.stack([np.asarray(inp["lambda_q1"])[0], np.asarray(inp["lambda_k1"])[0],
                         np.asarray(inp["lambda_q2"])[0], np.asarray(inp["lambda_k2"])[0]])),
        sublnT=f(np.asarray(inp["subln_w"])[0].reshape(2, 128).T),
        convw=f(np.asarray(inp["conv_w"])[0].reshape(4, 24, 128).transpose(2, 1, 0)),
        convb=f(np.asarray(inp["conv_b"])[0].reshape(24, 128).T),
        dtb=f(inp["dt_bias"]), alog=f(inp["A_log"]), dskip=f(inp["D_skip"]),
        ssmnwT=f(np.asarray(inp["ssm_norm_w"])[0].reshape(16, 128).T),
        fnw=f(np.asarray(inp["final_norm_w"]).reshape(1, 2048)),
        **consts)
    maps = []
    NM = NBLK // 4
    xpa = np.asarray(inp["x_prompt"])
    for c in range(8):
        b, j = c // 4, c % 4
        m = dict(common)
        m["xp"] = f(xpa[b, :SEQ])
        m["xown"] = f(xpa[b, :SEQ].reshape(NM, 4, 512, 2048)[:, j].reshape(NM * 512, 2048))
        sel = np.zeros((17, 7), np.float32)
        for ti in range(17):
            delta = 128 * (ti - 1) - 512 * j
            d = 5 if delta <= -256 else (6 if delta >= 512 else delta // 128 + 1)
            sel[ti, d] = 1.0
        m["selw"] = f(np.broadcast_to(sel.reshape(1, 119), (128, 119)))
        ws = np.zeros((1, 4), np.float32)
        ws[0, j] = 1.0
        m["wsel"] = f(np.broadcast_to(ws, (128, 4)))
        m["xsm"] = f(np.asarray(inp["x_sample"])[c])
        m["ck"] = f(np.asarray(inp["cache_k"])[0, c].reshape(PAST, 2048))
        m["cv"] = f(np.asarray(inp["cache_v"])[0, c].reshape(PAST, 2048))
        m["cconv"] = f(np.asarray(inp["cache_conv"])[0, c].reshape(3, 24, 128).transpose(2, 1, 0))
        m["sst"] = f(np.asarray(inp["state_ssm"])[0, c].reshape(2048, 128))
        maps.append(m)
    return maps


_NC_CACHE = {}


def run(inp, NBLK=32, sample=True, stage=99):
    key = (NBLK, sample, stage)
    if key not in _NC_CACHE:
        _NC_CACHE[key] = build(NBLK, sample, stage)
    nc = _NC_CACHE[key]
    maps = make_in_maps(inp, NBLK)
    res = run_bass_kernel_spmd(nc, maps, core_ids=list(range(8)))
    return res.results


def assemble_y(r, NBLK):
    NM = NBLK // 4
    y = np.zeros((2, NBLK * 512, 2048), np.float32)
    yv = y.reshape(2, NM, 4, 512, 2048)
    for c in range(8):
        b, j = c // 4, c % 4
        yv[b, :, j] = r[c]["y_p"].reshape(NM, 512, 2048)
    return y


def kernel(**inp):
    r = run(inp, 32, True)
    S = 16384
    y_prompt = assemble_y(r, 32)
    y_sample = np.stack([r[c]["y_s"] for c in range(8)]).reshape(8, 32, 2048)
    k_prompt = np.stack([r[4 * b]["k_p"] for b in range(2)]).reshape(1, 2, S, 8, 2, 128)
    v_prompt = np.stack([r[4 * b]["v_p"] for b in range(2)]).reshape(1, 2, S, 8, 256)
    conv_prompt = np.stack([r[4 * b]["conv_p"] for b in range(2)]).reshape(1, 2, 3, 3072)
    ssm_prompt = np.stack([r[4 * b]["ssm_p"] for b in range(2)]).reshape(1, 2, 32, 64, 128)
    k_sample = np.stack([r[c]["k_s"] for c in range(8)]).reshape(1, 8, 32, 8, 2, 128)
    v_sample = np.stack([r[c]["v_s"] for c in range(8)]).reshape(1, 8, 32, 8, 256)
    conv_sample = np.stack([r[c]["conv_s"] for c in range(8)]).reshape(1, 8, 3, 3072)
    ssm_sample = np.stack([r[c]["ssm_s"] for c in range(8)]).reshape(1, 8, 32, 64, 128)
    return tuple(np.ascontiguousarray(a.astype(np.float32)) for a in
                 (y_prompt, y_sample, k_prompt, v_prompt, conv_prompt, ssm_prompt, k_sample, v_sample, conv_sample, ssm_sample))
```

```python
import math
import numpy as np
from contextlib import ExitStack
import concourse.bass as bass
import concourse.mybir as mybir
from concourse.bass_utils import run_bass_kernel_spmd

F32 = mybir.dt.float32
BF16 = mybir.dt.bfloat16
ALU = mybir.AluOpType
AF = mybir.ActivationFunctionType
AX = mybir.AxisListType

D = 2048
KC = 16
NCOL = 13344
OQ, OK_, OV, OG, OZ, OX, ODT = 0, 2048, 4096, 6144, 8192, 10240, 13312
EPS = 1e-5
LAM0 = 0.8 - 0.6 * math.exp(-0.3 * 0)
SCALE = 128 ** -0.5
PAST = 2048
NEG = -30000.0


class Buf:
    __slots__ = ("t", "w", "r", "name")

    def __init__(self, t=None, name=""):
        self.t = t
        self.w = None
        self.r = {}
        self.name = name

    def __getitem__(self, k):
        return self.t[k]


class FW:
    def __init__(self, nc, es, ndma=16):
        self.nc = nc
        self.eng = {"pe": nc.tensor, "act": nc.scalar, "dve": nc.vector, "pool": nc.gpsimd, "sp": nc.sync}
        self.sem = {}
        self.cnt = {}
        self.semobj = {}
        for k in self.eng:
            self.sem[k] = es.enter_context(nc.semaphore("s_" + k))
            self.cnt[k] = 0
            self.semobj[k] = self.sem[k]
        self.waited = {}
        self.dring = {}
        self.dpos = {}
        for q in ("sp", "pool"):
            self.dring[q] = [[es.enter_context(nc.semaphore("d_%s%d" % (q, i))), 0] for i in range(ndma)]
            self.dpos[q] = 0
            for i, s in enumerate(self.dring[q]):
                self.semobj[("d", q, i)] = s[0]
        self.ninstr = 0

    def _wait(self, e, key, val):
        if val <= 0:
            return
        if e == "pe" and key == "pe":
            return
        if self.waited.get((e, key), 0) >= val:
            return
        self.eng[e].wait_ge(self.semobj[key], val)
        self.waited[(e, key)] = val
        self.ninstr += 1

    def _deps(self, e, reads, writes):
        deps = {}
        for b in reads:
            if b.w is not None:
                k, v = b.w
                if deps.get(k, 0) < v:
                    deps[k] = v
        for b in writes:
            if b.w is not None:
                k, v = b.w
                if deps.get(k, 0) < v:
                    deps[k] = v
            for k, v in b.r.items():
                if deps.get(k, 0) < v:
                    deps[k] = v
        for k, v in deps.items():
            self._wait(e, k, v)

    def _mark(self, tok, reads, writes):
        k, v = tok
        for b in reads:
            if b.r.get(k, 0) < v:
                b.r[k] = v
        for b in writes:
            b.w = tok
            b.r = {}

    def op(self, e, fn, reads=(), writes=()):
        return self.group(e, [fn], reads, writes)

    def group(self, e, fns, reads=(), writes=()):
        self._deps(e, reads, writes)
        ins = None
        for fn in fns:
            ins = fn(self.eng[e])
            self.ninstr += 1
        self.cnt[e] += 1
        ins.then_inc(self.sem[e], 1)
        tok = (e, self.cnt[e])
        self._mark(tok, reads, writes)
        return tok

    def dma(self, q, out, in_, reads=(), writes=(), **kw):
        self._deps(q, reads, writes)
        ring = self.dring[q]
        i = self.dpos[q]
        self.dpos[q] = (i + 1) % len(ring)
        slot = ring[i]
        key = ("d", q, i)
        self._wait(q, key, slot[1])
        ins = self.eng[q].dma_start(out=out, in_=in_, **kw)
        slot[1] += 16
        ins.then_inc(slot[0], 16)
        self.ninstr += 1
        tok = (key, slot[1])
        self._mark(tok, reads, writes)
        return tok

    def alias(self, dst, srcs):
        for s in srcs:
            if s.w is not None:
                k, v = s.w
                if dst.r.get(k, 0) < v:
                    dst.r[k] = v
            for k, v in s.r.items():
                if dst.r.get(k, 0) < v:
                    dst.r[k] = v

    def barrier(self):
        for e in self.eng:
            for k in self.eng:
                if k != e:
                    self._wait(e, k, self.cnt[k])
            for q in self.dring:
                for i, slot in enumerate(self.dring[q]):
                    self._wait(e, ("d", q, i), slot[1])


def rel_bucket_np(rel):
    half, max_exact = 16, 8
    ret = np.where(rel > 0, half, 0)
    n = np.abs(rel)
    nf = np.maximum(n, 1).astype(np.float32)
    large = max_exact + (np.log(nf / np.float32(max_exact)) / np.float32(math.log(128 / max_exact))
                         * np.float32(half - max_exact)).astype(np.int32)
    large = np.minimum(large, half - 1)
    return ret + np.where(n < max_exact, n, large)


def build(NBLK=32, sample=True, stage=99):
    nc = bass.Bass("TRN2", target_bir_lowering=False)
    SEQ = NBLK * 512

    def din(name, shape, dt=F32):
        return nc.dram_tensor(name, list(shape), dt, kind="ExternalInput").ap()

    def dout(name, shape, dt=F32):
        return nc.dram_tensor(name, list(shape), dt, kind="ExternalOutput").ap()

    def dscr(name, shape, dt):
        return nc.dram_tensor(name, list(shape), dt).ap()

    xp = din("xp", [SEQ, D])
    NM = NBLK // 4
    xown = din("xown", [NM * 512, D])
    selw = din("selw", [128, 17 * 7])
    wsel = din("wsel", [128, 4])
    xsm = din("xsm", [32, D])
    ck = din("ck", [PAST, D])
    cv = din("cv", [PAST, D])
    cconv = din("cconv", [128, 24, 3])
    sst = din("sst", [2048, 128])
    relb = din("relb", [32, 8])
    nw = din("nw", [128, 16])
    w_in = din("w_in", [D, NCOL])
    w_out = din("w_out", [4096, D])
    lamv = din("lamv", [4, 128])
    sublnT = din("sublnT", [128, 2])
    convw = din("convw", [128, 24, 4])
    convb = din("convb", [128, 24])
    dtb = din("dtb", [1, 32])
    alog = din("alog", [1, 32])
    dskip = din("dskip", [1, 32])
    ssmnwT = din("ssmnwT", [128, 16])
    fnw = din("fnw", [1, D])
    c_ident = din("c_ident", [128, 128])
    c_triu = din("c_triu", [128, 128])
    c_anti = din("c_anti", [128, 128])
    c_oh = din("c_oh", [32, 1152])
    c_mask = din("c_mask", [128, 5, 512])

    y_p = dout("y_p", [NM * 512, D])
    k_p = dout("k_p", [SEQ, D])
    v_p = dout("v_p", [SEQ, D])
    conv_p = dout("conv_p", [3, 3072])
    ssm_p = dout("ssm_p", [2048, 128])
    y_s = dout("y_s", [32, D])
    k_s = dout("k_s", [32, D])
    v_s = dout("v_s", [32, D])
    conv_s = dout("conv_s", [3, 3072])
    ssm_s = dout("ssm_s", [2048, 128])

    Wb = dscr("Wb", [D, NCOL], BF16)
    Wob = dscr("Wob", [4096, D], BF16)
    KTd = dscr("KTd", [16, 128, SEQ], BF16)
    Vd = dscr("Vd", [SEQ, D], BF16)
    KTsd = dscr("KTsd", [16, 128, PAST + 128], BF16)
    Vsd = dscr("Vsd", [PAST + 128, D], BF16)
    bvs = dscr("bvs", [8, 1152], F32)
    bmd = dscr("bmd", [8, 128, 5 * 512], F32)
    bm2 = dscr("bm2", [8, 17, 128, 512], F32)

    es = ExitStack()
    with es:
        fw = FW(nc, es)
        ARENA = 207 * 1024
        arena = es.enter_context(nc.sbuf_tensor("arena", [128, ARENA // 4], F32))
        apos = [0]

        def alloc(nbytes):
            o = apos[0]
            apos[0] = o + ((nbytes + 31) // 32) * 32
            assert apos[0] <= ARENA, ("SBUF arena overflow", apos[0])
            return o

        def view(off, shape, dt, name=""):
            n = int(np.prod(shape[1:]))
            if dt == F32:
                ap = arena[0:shape[0], off // 4: off // 4 + n]
            else:
                ap = arena[0:shape[0], off // 4: off // 4 + (n + 1) // 2].bitcast(BF16)[:, 0:n]
            if len(shape) == 3:
                ap = ap.rearrange("p (a b) -> p a b", a=shape[1])
            elif len(shape) == 4:
                ap = ap.rearrange("p (a b c) -> p a b c", a=shape[1], b=shape[2])
            return Buf(ap, name)

        def sb(name, shape, dt):
            n = int(np.prod(shape[1:])) * (4 if dt == F32 else 2)
            return view(alloc(n), shape, dt, name)

        PS = [Buf(es.enter_context(nc.psum_tensor("ps%d" % i, [128, 512], F32)), "ps%d" % i) for i in range(8)]
        PSb = [p.t[:].bitcast(BF16) for p in PS]

        idf = sb("idf", [128, 128], F32)
        idb = sb("idb", [128, 128], BF16)
        triu = sb("triu", [128, 128], F32)
        onesf = sb("onesf", [128, 128], F32)
        nwt = sb("nwt", [128, 16], F32)
        cw = sb("cw", [128, 24, 4], F32)
        cb = sb("cb", [128, 24], F32)
        dtb_bc = sb("dtb_bc", [128, 32], F32)
        A_bc = sb("A_bc", [128, 32], F32)
        D_bc = sb("D_bc", [128, 32], F32)
        fnw_bc = sb("fnw_bc", [128, D], F32)
        cb15 = sb("cb15", [128, 8], F32)
        nlam = sb("nlam", [128, 1], F32)
        hT = sb("hT", [128, 2048], F32)
        hTb = sb("hTb", [128, 2048], BF16)
        tails = sb("tails", [128, 24, 3], F32)
        small = sb("small", [128, 64], F32)
        selt = sb("selt", [128, 17 * 7], F32)
        wselt = sb("wselt", [128, 4], F32)
        MAIN0 = apos[0]

        fw.dma("sp", idf[:], c_ident, writes=[idf])
        fw.dma("sp", triu[:], c_triu, writes=[triu])
        fw.dma("sp", nwt[:], nw, writes=[nwt])
        fw.dma("sp", cw[:], convw, writes=[cw])
        fw.dma("sp", cb[:], convb, writes=[cb])
        fw.dma("sp", dtb_bc[:], dtb.partition_broadcast(128), writes=[dtb_bc])
        fw.dma("sp", A_bc[:], alog.partition_broadcast(128), writes=[A_bc])
        fw.dma("sp", D_bc[:], dskip.partition_broadcast(128), writes=[D_bc])
        fw.dma("sp", fnw_bc[:], fnw.partition_broadcast(128), writes=[fnw_bc])
        fw.dma("sp", cb15[:], relb[15:16, :].partition_broadcast(128), writes=[cb15])
        fw.dma("sp", selt[:], selw, writes=[selt])
        fw.dma("sp", wselt[:], wsel, writes=[wselt])
        fw.op("dve", lambda e: e.tensor_copy(idb[:], idf[:]), reads=[idf], writes=[idb])
        fw.op("pool", lambda e: e.memset(onesf[:], 1.0), writes=[onesf])
        fw.op("pool", lambda e: e.memset(hT[:], 0.0), writes=[hT])
        fw.op("pool", lambda e: e.memset(hTb[:], 0.0), writes=[hTb])
        fw.op("pool", lambda e: e.memset(tails[:], 0.0), writes=[tails])
        fw.op("act", lambda e: e.activation(A_bc[:], A_bc[:], AF.Exp), reads=[A_bc], writes=[A_bc])
        fw.op("dve", lambda e: e.tensor_scalar_mul(A_bc[:], A_bc[:], -1.0), reads=[A_bc], writes=[A_bc])

        apos[0] = MAIN0
        lv = sb("lv", [128, 4, 128], F32)
        lp = sb("lp", [128, 2, 128], F32)
        ls = sb("ls", [128, 2], F32)
        for i in range(4):
            fw.dma("sp", lv[:, i, :], lamv[i:i + 1, :].partition_broadcast(128), writes=[lv])
        fw.op("dve", lambda e: e.tensor_tensor(lp[:, 0, :], lv[:, 0, :], lv[:, 1, :], ALU.mult), reads=[lv], writes=[lp])
        fw.op("dve", lambda e: e.tensor_tensor(lp[:, 1, :], lv[:, 2, :], lv[:, 3, :], ALU.mult), reads=[lv, lp], writes=[lp])
        fw.op("dve", lambda e: e.reduce_sum(ls[:], lp[:], AX.X), reads=[lp], writes=[ls])
        fw.op("act", lambda e: e.activation(ls[:], ls[:], AF.Exp), reads=[ls], writes=[ls])
        fw.op("dve", lambda e: e.tensor_tensor(nlam[:], ls[:, 1:2], ls[:, 0:1], ALU.subtract), reads=[ls], writes=[nlam])
        fw.op("dve", lambda e: e.tensor_scalar_add(nlam[:], nlam[:], -LAM0), reads=[nlam], writes=[nlam])

        oh = sb("oh", [32, 1152], F32)
        tab = sb("tab", [32, 8], F32)
        bvsb = sb("bvsb", [8, 1152], F32)
        anti = sb("anti", [128, 128], F32)
        msk = sb("msk", [128, 5, 512], F32)
        bvd = Buf(None, "bvd")
        fw.dma("sp", oh[:], c_oh, writes=[oh])
        fw.dma("sp", tab[:], relb, writes=[tab])
        fw.dma("sp", anti[:], c_anti, writes=[anti])
        fw.dma("sp", msk[:], c_mask, writes=[msk])
        for j in range(3):
            fw.group("pe", [lambda e, j=j: e.matmul(PS[j][0:8, 0:384], lhsT=tab[:, :], rhs=oh[:, j * 384:(j + 1) * 384],
                                                   start=True, stop=True)], reads=[tab, oh], writes=[PS[j]])
            fw.op("dve", lambda e, j=j: e.tensor_copy(bvsb[:, j * 384:(j + 1) * 384], PS[j][0:8, 0:384]),
                  reads=[PS[j]], writes=[bvsb])
        fw.dma("sp", bvs, bvsb[:], reads=[bvsb], writes=[bvd])
        trev = [sb("trev%d" % i, [128, 512], F32) for i in range(2)]
        tfl = [sb("tfl%d" % i, [128, 512], F32) for i in range(2)]
        bmbuf = Buf(None, "bmd")
        it = 0
        for h in range(8):
            for d in range(5):
                delta = 128 * (d - 1)
                tr = trev[it % 2]
                tf = tfl[it % 2]
                pp = PS[3 + it % 2]
                src = bass.AP(tensor=bvs.tensor, offset=h * 1152 + 384 - delta, ap=[[1, 128], [1, 512]])
                fw.dma("sp", tr[:], src, reads=[bvd], writes=[tr])
                fw.group("pe", [lambda e, tr=tr, pp=pp: e.matmul(pp[:, :], lhsT=anti[:, :], rhs=tr[:, :], start=True, stop=True)],
                         reads=[anti, tr], writes=[pp])
                fw.op("dve", lambda e, tf=tf, pp=pp, d=d: e.tensor_tensor(tf[:], pp[:, :], msk[:, d, :], ALU.add),
                      reads=[pp, msk], writes=[tf])
                fw.dma("pool", bmd[h, :, d * 512:(d + 1) * 512], tf[:], reads=[tf], writes=[bmbuf])
                it += 1

        Tt = sb("Tt", [128, 5, 512], F32)
        s56 = sb("s56", [128, 2], F32)
        bm2buf = Buf(None, "bm2")
        it = 0
        for h in range(8):
            fw.dma("sp", Tt[:], bmd[h].rearrange("p (d q) -> p d q", d=5), reads=[bmbuf], writes=[Tt])
            for t_ in range(17):
                o = tfl[it % 2]
                c0 = t_ * 7
                fw.op("dve", lambda e: e.tensor_scalar_mul(o[:], Tt[:, 0, :], selt[:, c0:c0 + 1]), reads=[Tt, selt], writes=[o])
                for d in range(1, 5):
                    fw.op("dve", lambda e, d=d: e.scalar_tensor_tensor(o[:], Tt[:, d, :], selt[:, c0 + d:c0 + d + 1], o[:], ALU.mult, ALU.add),
                          reads=[Tt, selt, o], writes=[o])
                fw.op("dve", lambda e: e.tensor_tensor(s56[:, 0:1], selt[:, c0 + 5:c0 + 6], cb15[:, h:h + 1], ALU.mult),
                      reads=[selt, cb15], writes=[s56])
                fw.op("dve", lambda e: e.scalar_tensor_tensor(s56[:, 1:2], selt[:, c0 + 6:c0 + 7], NEG, s56[:, 0:1], ALU.mult, ALU.add),
                      reads=[selt, s56], writes=[s56])
                fw.op("dve", lambda e: e.tensor_scalar_add(o[:], o[:], s56[:, 1:2]), reads=[o, s56], writes=[o])
                fw.dma("pool", bm2[h, t_], o[:], reads=[o], writes=[bm2buf])
                it += 1

        wf = [sb("wf%d" % i, [128, 1668], F32) for i in range(3)]
        wb_ = [sb("wb%d" % i, [128, 1668], BF16) for i in range(3)]
        sT = sb("sT", [128, 2], F32)
        snT = sb("snT", [128, 16], F32)
        fw.dma("sp", sT[:], sublnT, writes=[sT])
        fw.dma("sp", snT[:], ssmnwT, writes=[snT])
        fw.op("dve", lambda e: e.tensor_scalar_mul(sT[:], sT[:], 1.0 - LAM0), reads=[sT], writes=[sT])
        Wbuf = Buf(None, "Wb")
        Wobuf = Buf(None, "Wob")
        it = 0
        for kc in range(16):
            for pc in range(8):
                a, b = wf[it % 3], wb_[it % 3]
                c0 = pc * 1668
                fw.dma("sp", a[:], w_in[kc * 128:(kc + 1) * 128, c0:c0 + 1668], writes=[a])
                if it % 2 == 0:
                    fw.op("act", lambda e, a=a, b=b, kc=kc: e.activation(b[:], a[:], AF.Copy, scale=nwt[:, kc:kc + 1]),
                          reads=[a, nwt], writes=[b])
                else:
                    fw.op("dve", lambda e, a=a, b=b, kc=kc: e.tensor_scalar_mul(b[:], a[:], nwt[:, kc:kc + 1]),
                          reads=[a, nwt], writes=[b])
                fw.dma("pool", Wb[kc * 128:(kc + 1) * 128, c0:c0 + 1668], b[:], reads=[b], writes=[Wbuf])
                it += 1
        for kc in range(32):
            for pc in range(2):
                a, b = wf[it % 3], wb_[it % 3]
                c0 = pc * 1024
                sc = sT[:, kc % 2:kc % 2 + 1] if kc < 16 else snT[:, kc - 16:kc - 15]
                fw.dma("sp", a[:, 0:1024], w_out[kc * 128:(kc + 1) * 128, c0:c0 + 1024], writes=[a])
                if it % 2 == 0:
                    fw.op("act", lambda e, a=a, b=b, sc=sc: e.activation(b[:, 0:1024], a[:, 0:1024], AF.Copy, scale=sc),
                          reads=[a, sT, snT], writes=[b])
                else:
                    fw.op("dve", lambda e, a=a, b=b, sc=sc: e.tensor_scalar_mul(b[:, 0:1024], a[:, 0:1024], sc),
                          reads=[a, sT, snT], writes=[b])
                fw.dma("pool", Wob[kc * 128:(kc + 1) * 128, c0:c0 + 1024], b[:, 0:1024], reads=[b], writes=[Wobuf])
                it += 1
        fw.barrier()

        apos[0] = MAIN0
        xT = sb("xT", [128, 16, 512], BF16)
        xld = [sb("xld%d" % i, [128, D], F32) for i in range(2)]
        xn = sb("xn", [128, D], BF16)
        wst = [sb("wst%d" % i, [128, 16, 256], BF16) for i in range(3)]
        kf = [sb("kf%d" % i, [128, 256], F32) for i in range(2)]
        kb = [sb("kb%d" % i, [128, 256], BF16) for i in range(2)]
        KTst = [sb("KTst%d" % i, [128, 2, 512], BF16) for i in range(2)]
        o_sz = alloc(16384)
        o_xs = alloc(16384)
        sz = view(o_sz, [128, 4, 2048], BF16, "sz")
        xs_tok = view(o_xs, [128, 4, 2048], BF16, "xs_tok")
        hres = view(o_sz, [128, 4, 2048], F32, "hres")
        dts = sb("dts", [128, 4, 32], F32)
        av = sb("av", [128, 4, 32], F32)
        dtmp = [sb("dtmp%d" % i, [128, 32], F32) for i in range(3)]
        acst = sb("acst", [128, 32], F32)
        tot = sb("tot", [128, 32], F32)
        ea = sb("ea", [128, 32], F32)
        te = sb("te", [128, 32], F32)
        dec = sb("dec", [128, 32], F32)
        o_mixT = apos[0]
        mixT = sb("mixT", [128, 32, 512], BF16)
        hT_own = view(o_mixT, [128, 2048], F32, "hT_own")
        hTb_own = view(o_mixT + 8192, [128, 2048], BF16, "hTb_own")
        tails_own = view(o_mixT + 12288, [128, 24, 3], F32, "tails_own")
        U0 = apos[0]
        BT = sb("BT", [128, 4, 512], BF16)
        CT = sb("CT", [128, 4, 512], BF16)
        B_tok = sb("B_tok", [128, 4, 512], BF16)
        raw = [sb("raw%d" % i, [128, 516], F32) for i in range(2)]
        cva = [sb("cva%d" % i, [128, 512], F32) for i in range(1)] * 2
        csb = [sb("csb%d" % i, [128, 512], BF16) for i in range(2)]
        xdt = sb("xdt", [128, 2048], BF16)
        xdte = sb("xdte", [128, 2048], BF16)
        Rb = [sb("R%d" % i, [128, 512], F32) for i in range(2)]
        segs = [sb("segs%d" % i, [128, 512], F32) for i in range(2)]
        MT = [sb("MT%d" % i, [128, 512], BF16) for i in range(2)]
        cbm = sb("cbm", [128, 4, 128], F32)
        t1 = [sb("t1_%d" % i, [128, 512], F32) for i in range(2)]
        t2 = [sb("t2_%d" % i, [128, 512], F32) for i in range(1)] * 2
        ym = [sb("ym%d" % i, [128, 512], BF16) for i in range(2)]
        U1 = apos[0]
        ssd_bufs = [BT, CT, B_tok, xdt, xdte, cbm] + raw + cva + csb + Rb + segs + MT + t1 + t2 + ym
        apos[0] = U0
        kst = [sb("kst%d" % i, [128, 2048], BF16) for i in range(2)]
        vst = [sb("vst%d" % i, [128, 16, 257], BF16) for i in range(2)]
        QT = sb("QT", [128, 2, 512], BF16)
        sg = sb("sg", [128, 4, 256], BF16)
        o_bmt0 = apos[0]
        bmt = sb("bmt", [128, 5, 512], F32)
        bms = [view(o_bmt0 + i * 2048, [128, 512], F32, "bms%d" % i) for i in range(5)]
        PT = [sb("PT%d" % i, [128, 512], BF16) for i in range(3)]
        dgt = [sb("dgt%d" % i, [128, 512], F32) for i in range(2)]
        att = sb("att", [128, 4, 256], F32)
        ma = [sb("ma%d" % i, [128, 256], BF16) for i in range(1)] * 2
        o_bmt = apos[0] - 0
        att_bufs = kst + vst + [QT, sg, bmt, att] + PT + dgt + ma + bms
        apos[0] = max(U1, apos[0])

        def enter_ssd_phase():
            for b_ in ssd_bufs:
                fw.alias(b_, att_bufs)

        def enter_att_phase():
            for b_ in att_bufs:
                fw.alias(b_, ssd_bufs)
        print("SBUF bytes/partition used:", apos[0])


        rr = {"ps": 0, "w": 0, "tr": 0, "sm": 0}
        smallbufs = [Buf(None, "sm%d" % i) for i in range(64)]

        def next_ps():
            rr["ps"] = (rr["ps"] + 1) % 4
            return PS[rr["ps"]]

        def next_tr():
            rr["tr"] = (rr["tr"] + 1) % 2
            return 4 + rr["tr"]

        def next_w():
            rr["w"] = (rr["w"] + 1) % 3
            return wst[rr["w"]]

        def smallcol(n=1):
            o = rr["sm"]
            if o + n > 64:
                o = 0
            rr["sm"] = o + n
            rr["smb"] = smallbufs[o]
            return o

        def load_w(col0, ncols):
            w = next_w()
            fw.dma("sp", w[:, :, 0:ncols], Wb[:, col0:col0 + ncols].rearrange("(kc p) c -> p kc c", p=128),
                   reads=[Wbuf], writes=[w])
            return w

        def evac_alt(idx, out_ap, in_ap, reads, writes):
            if idx % 2 == 0:
                fw.op("dve", lambda e: e.tensor_copy(out_ap, in_ap), reads=reads, writes=writes)
            else:
                fw.op("act", lambda e: e.copy(out_ap, in_ap), reads=reads, writes=writes)

        def transposes_to(dst_buf, dst_ap_fn, srcs, src_bufs, tp_in, n_out):
            bi = next_tr()
            pv = PSb[bi]
            n = len(srcs)
            fw.group("pe", [lambda e, j=j, s=s: e.transpose(pv[0:n_out, j * 128:j * 128 + tp_in], s, idb[0:tp_in, 0:tp_in])
                            for j, s in enumerate(srcs)], reads=list(src_bufs) + [idb], writes=[PS[bi]])
            pview = pv[0:n_out, 0:n * 128].rearrange("p (a b) -> p a b", a=n)[:, :, 0:tp_in]
            dst_ap_fn(pview, PS[bi])

        def rms_rstd(ss_ap, ss_buf, n, tp):
            fw.op("act", lambda e: e.activation(ss_ap, ss_ap, AF.Sqrt, bias=EPS, scale=1.0 / n), reads=[ss_buf], writes=[ss_buf])
            fw.op("dve", lambda e: e.reciprocal(ss_ap, ss_ap), reads=[ss_buf], writes=[ss_buf])

        def process_block(cfg):
            tp, ntt = cfg["tp"], cfg["ntt"]
            NQ = tp * ntt
            xsrc = cfg["xsrc"]
            phases = cfg.get("phases", "ABCDEF")
            for tt in range(ntt):
                xl = xld[tt % 2]
                fw.dma("sp", xl[0:tp, :], xsrc[tt * tp:(tt + 1) * tp, :], writes=[xl])
                c = smallcol()
                ssap = small[0:tp, c:c + 1]
                fw.op("pool", lambda e: e.memset(ssap, 0.0), writes=[rr["smb"]])
                fw.op("act", lambda e: e.activation(xn[0:tp, :], xl[0:tp, :], AF.Square, accum_out=ssap),
                      reads=[xl], writes=[xn, rr["smb"]])
                rms_rstd(ssap, rr["smb"], D, tp)
                fw.op("act", lambda e: e.activation(xn[0:tp, :], xl[0:tp, :], AF.Copy, scale=ssap),
                      reads=[xl, rr["smb"]], writes=[xn])
                for half in range(2):
                    def cp(pview, pbuf, half=half, tt=tt):
                        evac_alt(half, xT[:, half * 8:(half + 1) * 8, tt * tp:(tt + 1) * tp], pview, [pbuf], [xT])
                    transposes_to(xT, cp, [xn[0:tp, (half * 8 + j) * 128:(half * 8 + j + 1) * 128] for j in range(8)],
                                  [xn], tp, 128)

            def inproj_tok(col0, ncols, consume):
                w = load_w(col0, ncols)
                for tt in range(ntt):
                    ps = next_ps()
                    fw.group("pe", [lambda e, kc=kc: e.matmul(ps[0:tp, 0:ncols], lhsT=xT[:, kc, tt * tp:(tt + 1) * tp],
                                                              rhs=w[:, kc, 0:ncols], start=(kc == 0), stop=(kc == 15))
                                    for kc in range(16)], reads=[xT, w], writes=[ps])
                    consume(tt, ps)

            def inproj_feat(w, j, consume):
                ps = next_ps()
                fw.group("pe", [lambda e, kc=kc: e.matmul(ps[:, 0:NQ], lhsT=w[:, kc, j * 128:(j + 1) * 128],
                                                          rhs=xT[:, kc, 0:NQ], start=(kc == 0), stop=(kc == 15))
                                for kc in range(16)], reads=[xT, w], writes=[ps])
                consume(ps)

            if stage <= 1:
                return
            if "B" in phases or "C" in phases:
                ktd, vd_, kout, vout = cfg.get("KT"), cfg.get("V"), cfg.get("kout"), cfg.get("vout")
                tok0 = cfg.get("tok0")
                kvb = cfg.get("kvbuf")
                tl = cfg.get("tails", tails)
                hT_ = cfg.get("hT", hT)
                hTb_ = cfg.get("hTb", hTb)
                do_y = cfg.get("do_y", True)
                if cfg.get("save_i") is not None:
                    si_ = cfg["save_i"]
                    if si_ == 0:
                        fw.alias(hT_own, [mixT])
                        fw.alias(tails_own, [mixT])
                        fw.op("dve", lambda e: e.tensor_scalar_mul(hT_own[:, :], hT[:, :], wselt[:, 0:1]), reads=[hT, wselt], writes=[hT_own])
                        fw.op("dve", lambda e: e.tensor_scalar_mul(tails_own[:, :, :], tails[:, :, :], wselt[:, 0:1]),
                              reads=[tails, wselt], writes=[tails_own])
                    else:
                        fw.op("dve", lambda e: e.scalar_tensor_tensor(hT_own[:, :], hT[:, :], wselt[:, si_:si_ + 1], hT_own[:, :],
                                                                      ALU.mult, ALU.add), reads=[hT, wselt, hT_own], writes=[hT_own])
                        fw.op("dve", lambda e: e.scalar_tensor_tensor(tails_own[:, :, :], tails[:, :, :], wselt[:, si_:si_ + 1],
                                                                      tails_own[:, :, :], ALU.mult, ALU.add),
                              reads=[tails, wselt, tails_own], writes=[tails_own])
                if cfg.get("init_hTb"):
                    fw.alias(hTb_own, [mixT])
                    fw.op("act", lambda e: e.copy(hTb_[:, :], hT_[:, :]), reads=[hT_], writes=[hTb_])
                for cc in (range(16) if "B" in phases else []):
                    isk = cc < 8
                    col0 = (OK_ if isk else OV) + (cc % 8) * 256
                    kts = KTst[cc % 2]

                    def cons(tt, ps, cc=cc, isk=isk, kts=kts):
                        f = kf[(cc * ntt + tt) % 2]
                        b = kb[(cc * ntt + tt) % 2]
                        if stage <= 1.2:
                            return
                        fw.op("act", lambda e: e.copy(f[0:tp, :], ps[0:tp, 0:256]), reads=[ps], writes=[f])
                        fw.op("dve", lambda e: e.tensor_copy(b[0:tp, :], f[0:tp, :]), reads=[f], writes=[b])
                        if stage <= 1.4:
                            return
                        dst = kout if isk else vout
                        fw.dma("pool", dst[tt * tp:(tt + 1) * tp, (cc % 8) * 256:(cc % 8 + 1) * 256], f[0:tp, :], reads=[f])
                        if stage <= 1.6:
                            return
                        if isk:
                            def cp(pview, pbuf):
                                fw.op("dve", lambda e: e.tensor_copy(kts[:, :, tt * tp:(tt + 1) * tp], pview), reads=[pbuf], writes=[kts])
                            transposes_to(kts, cp, [b[0:tp, j * 128:(j + 1) * 128] for j in range(2)], [b], tp, 128)
                        else:
                            fw.dma("pool", vd_[tok0 + tt * tp: tok0 + (tt + 1) * tp, (cc % 8) * 256:(cc % 8 + 1) * 256], b[0:tp, :],
                                   reads=[b], writes=[kvb])
                    inproj_tok(col0, 256, cons)
                    if isk and stage > 1.8:
                        hm0 = (cc % 8) * 2
                        fw.dma("pool", ktd[hm0:hm0 + 2, :, tok0:tok0 + NQ].rearrange("a d t -> d a t"), kts[:, :, 0:NQ],
                               reads=[kts], writes=[kvb])

                if stage <= 2:
                    return
                enter_ssd_phase()
                fw.alias(sz, [hres])
                fw.alias(xs_tok, [hres])
                for cc in (range(8) if cfg.get("do_z", True) else []):
                    def consz(tt, ps, cc=cc):
                        fw.op("act", lambda e: e.activation(sz[0:tp, tt, cc * 256:(cc + 1) * 256], ps[0:tp, 0:256], AF.Silu),
                              reads=[ps], writes=[sz])
                    inproj_tok(OZ + cc * 256, 256, consz)
                for cc in range(cfg.get("n_xbc", 12)):
                    w = load_w(OX + cc * 256, 256)
                    for j in range(2):
                        ct = cc * 2 + j
                        rw = raw[ct % 2]
                        ca = cva[ct % 2]

                        def consx(ps, ct=ct, rw=rw, ca=ca):
                            fw.op("pool", lambda e: e.tensor_copy(rw[:, 0:3], tl[:, ct, :]), reads=[tl], writes=[rw])
                            fw.op("act", lambda e: e.copy(rw[:, 3:3 + NQ], ps[:, 0:NQ]), reads=[ps], writes=[rw])
                            fw.op("pool", lambda e: e.tensor_copy(tl[:, ct, :], rw[:, NQ:NQ + 3]), reads=[rw], writes=[tl])
                            fw.op("dve", lambda e: e.tensor_scalar(ca[:, 0:NQ], rw[:, 3:3 + NQ], cw[:, ct, 3:4], cb[:, ct:ct + 1],
                                                                   ALU.mult, ALU.add), reads=[rw, cw, cb], writes=[ca])
                            for jj in range(3):
                                fw.op("dve", lambda e, jj=jj: e.scalar_tensor_tensor(ca[:, 0:NQ], rw[:, jj:jj + NQ], cw[:, ct, jj:jj + 1],
                                                                                   ca[:, 0:NQ], ALU.mult, ALU.add),
                                      reads=[rw, cw, ca], writes=[ca])
                            if ct < 16:
                                cs = csb[ct % 2]
                                fw.op("act", lambda e: e.activation(cs[:, 0:NQ], ca[:, 0:NQ], AF.Silu), reads=[ca], writes=[cs])

                                def cp(pview, pbuf):
                                    evac_alt(ct, xs_tok[0:tp, 0:ntt, ct * 128:(ct + 1) * 128], pview, [pbuf], [xs_tok])
                                transposes_to(xs_tok, cp, [cs[:, tt * tp:(tt + 1) * tp] for tt in range(ntt)], [cs], 128, tp)
                            elif ct < 20:
                                g = ct - 16
                                fw.op("act", lambda e: e.activation(BT[:, g, 0:NQ], ca[:, 0:NQ], AF.Silu), reads=[ca], writes=[BT])

                                def cp(pview, pbuf):
                                    evac_alt(ct, B_tok[0:tp, 0:ntt, g * 128:(g + 1) * 128], pview, [pbuf], [B_tok])
                                transposes_to(B_tok, cp, [BT[:, g, tt * tp:(tt + 1) * tp] for tt in range(ntt)], [BT], 128, tp)
                            else:
                                g = ct - 20
                                fw.op("act", lambda e: e.activation(CT[:, g, 0:NQ], ca[:, 0:NQ], AF.Silu), reads=[ca], writes=[CT])
                        inproj_feat(w, j, consx)

                def consdt(tt, ps):
                    d0, d1, d2 = dtmp
                    fw.op("dve", lambda e: e.tensor_tensor(d0[0:tp, :], ps[0:tp, 0:32], dtb_bc[0:tp, :], ALU.add),
                          reads=[ps, dtb_bc], writes=[d0])
                    fw.op("dve", lambda e: e.scalar_tensor_tensor(d1[0:tp, :], d0[0:tp, :], -1.0, d0[0:tp, :], ALU.mult, ALU.max),
                          reads=[d0], writes=[d1])
                    fw.op("act", lambda e: e.activation(d1[0:tp, :], d1[0:tp, :], AF.Exp, scale=-1.0), reads=[d1], writes=[d1])
                    fw.op("act", lambda e: e.activation(d1[0:tp, :], d1[0:tp, :], AF.Ln, bias=1.0), reads=[d1], writes=[d1])
                    fw.op("dve", lambda e: e.scalar_tensor_tensor(dts[0:tp, tt, :], d0[0:tp, :], 0.0, d1[0:tp, :], ALU.max, ALU.add),
                          reads=[d0, d1], writes=[dts])
                    fw.op("dve", lambda e: e.tensor_tensor(av[0:tp, tt, :], dts[0:tp, tt, :], A_bc[0:tp, :], ALU.mult),
                          reads=[dts, A_bc], writes=[av])
                inproj_tok(ODT, 32, consdt)

                if stage <= 3:
                    return
                def bc3(ap2, n):
                    return ap2.unsqueeze(2).to_broadcast([ap2.shape[0], ap2.shape[1], n])

                for c in range(ntt):
                    sl = slice(c * tp, (c + 1) * tp)
                    psA = next_ps()
                    fw.group("pe", [lambda e: e.matmul(psA[0:tp, 0:32], lhsT=triu[0:tp, 0:tp], rhs=av[0:tp, c, :], start=True, stop=True),
                                    lambda e: e.matmul(psA[:, 32:64], lhsT=onesf[0:tp, :], rhs=av[0:tp, c, :], start=True, stop=True)],
                             reads=[triu, onesf, av], writes=[psA])
                    fw.op("dve", lambda e: e.tensor_copy(acst[0:tp, :], psA[0:tp, 0:32]), reads=[psA], writes=[acst])
                    fw.op("dve", lambda e: e.tensor_copy(tot[:, :], psA[:, 32:64]), reads=[psA], writes=[tot])
                    if do_y:
                        fw.op("act", lambda e: e.activation(ea[0:tp, :], acst[0:tp, :], AF.Exp), reads=[acst], writes=[ea])
                    fw.op("act", lambda e: e.activation(dec[:, :], tot[:, :], AF.Exp), reads=[tot], writes=[dec])
                    fw.op("dve", lambda e: e.tensor_tensor(te[0:tp, :], tot[0:tp, :], acst[0:tp, :], ALU.subtract),
                          reads=[tot, acst], writes=[te])
                    fw.op("act", lambda e: e.activation(te[0:tp, :], te[0:tp, :], AF.Exp), reads=[te], writes=[te])
                    fw.op("dve", lambda e: e.tensor_tensor(te[0:tp, :], te[0:tp, :], dts[0:tp, c, :], ALU.mult),
                          reads=[te, dts], writes=[te])
                    xv = xs_tok[0:tp, c, :].rearrange("p (h q) -> p h q", h=32)
                    if do_y:
                        fw.op("dve", lambda e: e.tensor_tensor(xdt[0:tp, :].rearrange("p (h q) -> p h q", h=32), xv,
                                                               bc3(dts[0:tp, c, :], 64), ALU.mult), reads=[xs_tok, dts], writes=[xdt])
                    fw.op("pool", lambda e: e.tensor_tensor(xdte[0:tp, :].rearrange("p (h q) -> p h q", h=32), xv,
                                                            bc3(te[0:tp, :], 64), ALU.mult), reads=[xs_tok, te], writes=[xdte])
                    if do_y:
                        psC = next_ps()
                        fw.group("pe", [lambda e, g=g: e.matmul(psC[0:tp, g * 128:g * 128 + tp], lhsT=BT[:, g, sl], rhs=CT[:, g, sl],
                                                                start=True, stop=True) for g in range(4)],
                                 reads=[BT, CT], writes=[psC])
                        fw.op("dve", lambda e: e.tensor_tensor(cbm[0:tp, :, 0:tp],
                                                               psC[0:tp, :].rearrange("p (g t) -> p g t", g=4)[:, :, 0:tp],
                                                               triu[0:tp, 0:tp].unsqueeze(1).to_broadcast([tp, 4, tp]), ALU.mult),
                              reads=[psC, triu], writes=[cbm])
                        for g in range(4):
                            psY = next_ps()
                            for hg in (2 * g, 2 * g + 1):
                                h0 = hg * 4
                                R = Rb[hg % 2]
                                sgm = segs[hg % 2]
                                M = MT[hg % 2]
                                R3 = R[0:tp, 0:4 * tp].rearrange("p (h t) -> p h t", h=4)
                                fw.op("pool", lambda e: e.tensor_tensor(R3, triu[0:tp, 0:tp].unsqueeze(1).to_broadcast([tp, 4, tp]),
                                                                        bc3(av[0:tp, c, h0:h0 + 4], tp), ALU.mult),
                                      reads=[triu, av], writes=[R])
                                psB = next_ps()
                                fw.group("pe", [lambda e: e.matmul(psB[0:tp, 0:4 * tp], lhsT=onesf[0:tp, 0:tp], rhs=R[0:tp, 0:4 * tp],
                                                                   start=True, stop=True)], reads=[onesf, R], writes=[psB])
                                for hh in range(4):
                                    fw.op("dve", lambda e, hh=hh: e.tensor_scalar(sgm[0:tp, hh * tp:(hh + 1) * tp], psB[0:tp, hh * tp:(hh + 1) * tp],
                                                                                  acst[0:tp, h0 + hh:h0 + hh + 1], 0.0, ALU.subtract, ALU.min),
                                          reads=[psB, acst], writes=[sgm])
                                fw.op("act", lambda e: e.activation(sgm[0:tp, 0:4 * tp], sgm[0:tp, 0:4 * tp], AF.Exp), reads=[sgm], writes=[sgm])
                                fw.op("dve", lambda e: e.tensor_tensor(M[0:tp, 0:4 * tp].rearrange("p (h t) -> p h t", h=4),
                                                                       sgm[0:tp, 0:4 * tp].rearrange("p (h t) -> p h t", h=4),
                                                                       cbm[0:tp, g, 0:tp].unsqueeze(1).to_broadcast([tp, 4, tp]), ALU.mult),
                                      reads=[sgm, cbm], writes=[M])
                                fw.group("pe", [lambda e, hh=hh: e.matmul(psY[0:tp, ((h0 + hh) % 8) * 64:((h0 + hh) % 8) * 64 + 64],
                                                                          lhsT=M[0:tp, hh * tp:(hh + 1) * tp],
                                                                          rhs=xdt[0:tp, (h0 + hh) * 64:(h0 + hh + 1) * 64], start=True, stop=True)
                                                for hh in range(4)], reads=[M, xdt], writes=[psY])
                            psO = next_ps()
                            fw.group("pe", [lambda e: e.matmul(psO[0:tp, 0:512], lhsT=CT[:, g, sl], rhs=hTb_[:, g * 512:(g + 1) * 512],
                                                               start=True, stop=True)], reads=[CT, hTb_], writes=[psO])
                            a1, a2, yb = t1[g % 2], t2[g % 2], ym[g % 2]
                            gs = slice(g * 512, (g + 1) * 512)
                            v3 = lambda ap: ap.rearrange("p (h q) -> p h q", h=8)
                            fw.op("dve", lambda e: e.tensor_tensor(v3(a1[0:tp, :]), v3(psO[0:tp, 0:512]), bc3(ea[0:tp, 8 * g:8 * g + 8], 64), ALU.mult),
                                  reads=[psO, ea], writes=[a1])
                            fw.op("dve", lambda e: e.tensor_tensor(a1[0:tp, :], a1[0:tp, :], psY[0:tp, 0:512], ALU.add), reads=[a1, psY], writes=[a1])
                            fw.op("pool", lambda e: e.tensor_tensor(v3(a2[0:tp, :]), v3(xs_tok[0:tp, c, gs]), bc3(D_bc[0:tp, 8 * g:8 * g + 8], 64), ALU.mult),
                                  reads=[xs_tok, D_bc], writes=[a2])
                            fw.op("pool", lambda e: e.tensor_tensor(a1[0:tp, :], a1[0:tp, :], a2[0:tp, :], ALU.add), reads=[a1, a2], writes=[a1])
                            fw.op("pool", lambda e: e.tensor_tensor(a1[0:tp, :], a1[0:tp, :], sz[0:tp, c, gs], ALU.mult), reads=[a1, sz], writes=[a1])
                            cc_ = smallcol()
                            ssap = small[0:tp, cc_:cc_ + 1]
                            fw.op("pool", lambda e: e.memset(ssap, 0.0), writes=[rr["smb"]])
                            fw.op("act", lambda e: e.activation(a2[0:tp, :], a1[0:tp, :], AF.Square, accum_out=ssap), reads=[a1], writes=[a2, rr["smb"]])
                            rms_rstd(ssap, rr["smb"], 512, tp)
                            fw.op("act", lambda e: e.activation(yb[0:tp, :], a1[0:tp, :], AF.Copy, scale=ssap), reads=[a1, rr["smb"]], writes=[yb])

                            def cp(pview, pbuf, g=g):
                                evac_alt(g, mixT[:, 16 + 4 * g:16 + 4 * g + 4, sl], pview, [pbuf], [mixT])
                            transposes_to(mixT, cp, [yb[0:tp, j * 128:(j + 1) * 128] for j in range(4)], [yb], tp, 128)
                    for g in range(4):
                        psH = next_ps()
                        gs = slice(g * 512, (g + 1) * 512)
                        fw.group("pe", [lambda e: e.matmul(psH[:, 0:512], lhsT=B_tok[0:tp, c, g * 128:(g + 1) * 128], rhs=xdte[0:tp, gs],
                                                           start=True, stop=True)], reads=[B_tok, xdte], writes=[psH])
                        h3 = hT_[:, gs].rearrange("p (h q) -> p h q", h=8)
                        fw.op("dve", lambda e: e.tensor_tensor(h3, h3, bc3(dec[:, 8 * g:8 * g + 8], 64), ALU.mult), reads=[hT_, dec], writes=[hT_])
                        fw.op("dve", lambda e: e.tensor_tensor(hT_[:, gs], hT_[:, gs], psH[:, 0:512], ALU.add), reads=[hT_, psH], writes=[hT_])
                    if do_y:
                        fw.op("act", lambda e: e.copy(hTb_[:, :], hT_[:, :]), reads=[hT_], writes=[hTb_])

            if "E" not in phases:
                return
            fw.alias(mixT, [hT_own, hTb_own, tails_own])
            if stage <= 4:
                return
            enter_att_phase()
            for v in vst:
                fw.op("pool", lambda e, v=v: e.memset(v[:, :, 256:257], 1.0), writes=[v])
            ktiles = cfg["ktiles"]
            pieces = cfg["pieces"]
            for h in range(8):
                wq = load_w(OQ + h * 256, 256)
                for m in range(2):
                    def consq(ps, m=m):
                        fw.op("act", lambda e: e.copy(QT[:, m, 0:NQ], ps[:, 0:NQ]), reads=[ps], writes=[QT])
                    inproj_feat(wq, m, consq)

                def consg(tt, ps):
                    fw.op("act", lambda e: e.activation(sg[0:tp, tt, :], ps[0:tp, 0:256], AF.Silu), reads=[ps], writes=[sg])
                inproj_tok(OG + h * 256, 256, consg)
                if cfg.get("use_bmd", False):
                    fw.dma("sp", bmt[:], bmd[h].rearrange("p (d q) -> p d q", d=5), reads=[bmbuf], writes=[bmt])
                for m in range(2):
                    loaded = {}

                    def get_piece(pid, m=m, h=h, loaded=loaded):
                        if pid in loaded:
                            return loaded[pid]
                        ksrc, vsrc, nkeys, bufs = pieces[pid]
                        kbuf_ = kst[pid % 2]
                        vbuf_ = vst[pid % 2]
                        fw.dma("sp", kbuf_[:, 0:nkeys], ksrc(h * 2 + m), reads=bufs, writes=[kbuf_])
                        nfull = nkeys // 128
                        if nfull > 0:
                            fw.dma("sp", vbuf_[:, 0:nfull, 0:256], vsrc(h, 0, nfull * 128).rearrange("(kt p) e -> p kt e", p=128),
                                   reads=bufs, writes=[vbuf_])
                        rem = nkeys - nfull * 128
                        if rem > 0:
                            fw.dma("sp", vbuf_[0:rem, nfull, 0:256], vsrc(h, nfull * 128, rem), reads=bufs, writes=[vbuf_])
                        loaded[pid] = (kbuf_, vbuf_)
                        return loaded[pid]

                    nt = len(ktiles)
                    first_for_qs = {}
                    sinfo = {}

                    def emitS(i):
                        pid, idx, nk, diag, qsv = ktiles[i]
                        kbuf_, vbuf_ = get_piece(pid)
                        ps = PS[i % 4]
                        fw.group("pe", [lambda e: e.matmul(ps[0:nk, 0:NQ], lhsT=kbuf_[:, idx * 128:idx * 128 + nk], rhs=QT[:, m, 0:NQ],
                                                           start=True, stop=True)], reads=[kbuf_, QT], writes=[ps])
                        p = PT[i % 3]
                        if diag is None:
                            fw.op("act", lambda e: e.activation(p[0:nk, 0:NQ], ps[0:nk, 0:NQ], AF.Exp, bias=cb15[0:nk, h:h + 1], scale=SCALE),
                                  reads=[ps, cb15], writes=[p])
                        elif isinstance(diag, tuple):
                            dg = dgt[i % 2]
                            slot = bms[diag[1] % 5]
                            fw.dma("sp", slot[:], bm2[h, diag[1]], reads=[bm2buf], writes=[slot])
                            fw.op("dve", lambda e: e.scalar_tensor_tensor(dg[0:nk, 0:NQ], ps[0:nk, 0:NQ], SCALE, slot[0:nk, 0:NQ],
                                                                          ALU.mult, ALU.add), reads=[ps, slot], writes=[dg])
                            fw.op("act", lambda e: e.activation(p[0:nk, 0:NQ], dg[0:nk, 0:NQ], AF.Exp), reads=[dg], writes=[p])
                        else:
                            dg = dgt[i % 2]
                            fw.op("dve", lambda e: e.scalar_tensor_tensor(dg[0:nk, 0:NQ], ps[0:nk, 0:NQ], SCALE, bmt[0:nk, diag, 0:NQ],
                                                                          ALU.mult, ALU.add), reads=[ps, bmt], writes=[dg])
                            fw.op("act", lambda e: e.activation(p[0:nk, 0:NQ], dg[0:nk, 0:NQ], AF.Exp), reads=[dg], writes=[p])
                        sinfo[i] = (p, vbuf_)

                    last_for_qs = {}
                    for i, (pid, idx, nk, diag, qsv) in enumerate(ktiles):
                        for qs in qsv:
                            last_for_qs[qs] = i
                            if qs not in first_for_qs:
                                first_for_qs[qs] = i

                    def emitPV(i):
                        pid, idx, nk, diag, qsv = ktiles[i]
                        p, vbuf_ = sinfo.pop(i)
                        fw.group("pe", [lambda e, qs=qs: e.matmul(PS[4 + qs][0:tp, 0:257], lhsT=p[0:nk, qs * tp:(qs + 1) * tp],
                                                                  rhs=vbuf_[0:nk, idx, 0:257], start=(i == first_for_qs[qs]),
                                                                  stop=(i == last_for_qs[qs])) for qs in qsv],
                                 reads=[p, vbuf_], writes=[PS[4 + qs] for qs in qsv])

                    LA = 2
                    for i in range(nt + LA):
                        if i < nt:
                            emitS(i)
                        if i >= LA:
                            emitPV(i - LA)
                        if i < nt and ktiles[i][1] == LA and (ktiles[i][0] + 1) in pieces:
                            get_piece(ktiles[i][0] + 1)
                    for qs in range(ntt):
                        acc = PS[4 + qs]
                        c_ = smallcol()
                        rc = small[0:tp, c_:c_ + 1]
                        fw.op("dve", lambda e: e.reciprocal(rc, acc[0:tp, 256:257]), reads=[acc], writes=[rr["smb"]])
                        if m == 0:
                            fw.op("dve", lambda e: e.tensor_scalar_mul(att[0:tp, qs, :], acc[0:tp, 0:256], rc), reads=[acc, rr["smb"]], writes=[att])
                        else:
                            fw.op("dve", lambda e: e.tensor_tensor(rc, rc, nlam[0:tp, :], ALU.mult), reads=[rr["smb"], nlam], writes=[rr["smb"]])
                            fw.op("dve", lambda e: e.scalar_tensor_tensor(att[0:tp, qs, :], acc[0:tp, 0:256], rc, att[0:tp, qs, :],
                                                                          ALU.mult, ALU.add), reads=[acc, rr["smb"], att], writes=[att])
                    if m == 1:
                        for qs in range(ntt):
                            c2 = smallcol()
                            ssap = small[0:tp, c2:c2 + 1]
                            mab = ma[qs % 2]
                            fw.op("pool", lambda e: e.memset(ssap, 0.0), writes=[rr["smb"]])
                            fw.op("act", lambda e: e.activation(mab[0:tp, :], att[0:tp, qs, :], AF.Square, accum_out=ssap),
                                  reads=[att], writes=[mab, rr["smb"]])
                            rms_rstd(ssap, rr["smb"], 256, tp)
                            fw.op("dve", lambda e: e.scalar_tensor_tensor(mab[0:tp, :], att[0:tp, qs, :], ssap, sg[0:tp, qs, :],
                                                                          ALU.mult, ALU.mult), reads=[att, rr["smb"], sg], writes=[mab])

                            def cp(pview, pbuf, qs=qs):
                                evac_alt(qs, mixT[:, 2 * h:2 * h + 2, qs * tp:(qs + 1) * tp], pview, [pbuf], [mixT])
                            transposes_to(mixT, cp, [mab[0:tp, j * 128:(j + 1) * 128] for j in range(2)], [mab], tp, 128)

            if stage <= 5:
                return
            fw.alias(hres, [sz, xs_tok])
            for oc in range(16):
                col0 = oc * 128
                w = next_w()
                wv = w[:, :, :].rearrange("p a b -> p (a b)").rearrange("p (a b) -> p a b", a=32)
                fw.dma("sp", wv, Wob[:, col0:col0 + 128].rearrange("(kc p) c -> p kc c", p=128), reads=[Wobuf], writes=[w])
                for tt in range(ntt):
                    ps = next_ps()
                    fw.group("pe", [lambda e, kc=kc: e.matmul(ps[0:tp, 0:128], lhsT=mixT[:, kc, tt * tp:(tt + 1) * tp], rhs=wv[:, kc, :],
                                                              start=(kc == 0), stop=(kc == 31)) for kc in range(32)],
                             reads=[mixT, w], writes=[ps])
                    evac_alt(tt, hres[0:tp, tt, col0:col0 + 128], ps[0:tp, 0:128], [ps], [hres])
            yout = cfg["yout"]
            for tt in range(ntt):
                xl = xld[tt % 2]
                fw.dma("sp", xl[0:tp, :], xsrc[tt * tp:(tt + 1) * tp, :], writes=[xl])
                fw.op("dve", lambda e: e.tensor_tensor(xl[0:tp, :], xl[0:tp, :], hres[0:tp, tt, :], ALU.add), reads=[xl, hres], writes=[xl])
                c_ = smallcol()
                ssap = small[0:tp, c_:c_ + 1]
                fw.op("pool", lambda e: e.memset(ssap, 0.0), writes=[rr["smb"]])
                fw.op("act", lambda e: e.activation(hres[0:tp, tt, :], xl[0:tp, :], AF.Square, accum_out=ssap), reads=[xl], writes=[hres, rr["smb"]])
                rms_rstd(ssap, rr["smb"], D, tp)
                fw.op("dve", lambda e: e.scalar_tensor_tensor(xl[0:tp, :], xl[0:tp, :], ssap, fnw_bc[0:tp, :], ALU.mult, ALU.mult),
                      reads=[xl, rr["smb"], fnw_bc], writes=[xl])
                fw.dma("pool", yout[tt * tp:(tt + 1) * tp, :], xl[0:tp, :], reads=[xl])

        def write_state_outputs(conv_out, ssm_out):
            enter_ssd_phase()
            for ct in range(24):
                fw.dma("pool", conv_out[:, ct * 128:(ct + 1) * 128].rearrange("j p -> p j"), tails[:, ct, :], reads=[tails],
                       allow_slow_non_contiguous=True)
            for i in range(16):
                ps = next_ps()
                fw.group("pe", [lambda e: e.transpose(ps[:, 0:128], hT[:, i * 128:(i + 1) * 128], idf[:, :])], reads=[hT, idf], writes=[ps])
                st = t1[i % 2]
                evac_alt(i, st[:, 0:128], ps[:, 0:128], [ps], [st])
                fw.dma("pool", ssm_out[i * 128:(i + 1) * 128, :], st[:, 0:128], reads=[st])

        kvbufs = [Buf(None, "kv%d" % i) for i in range(NBLK)]
        for mstep in range(NBLK // 4 if stage > 0 else 0):
            for bi_ in range(4):
                blk = 4 * mstep + bi_
                tok0 = blk * 512
                cfg = dict(tp=128, ntt=4, xsrc=xp[tok0:tok0 + 512, :], KT=KTd, V=Vd, kout=k_p[tok0:tok0 + 512, :],
                           vout=v_p[tok0:tok0 + 512, :], tok0=tok0, kvbuf=kvbufs[blk], phases="ABCD", save_i=bi_, do_z=False, n_xbc=12, do_y=False)
                process_block(cfg)
            nkt = 16 * (mstep + 1)
            ktiles = []
            for kt in range(nkt):
                t_ = kt - 16 * mstep
                ktiles.append((kt // 16, kt % 16, 128, (None if t_ < -1 else ("dram", t_ + 1)), [0, 1, 2, 3]))
            pieces = {}
            for pid in range(mstep + 1):
                k0 = pid * 2048
                pieces[pid] = (lambda hm, k0=k0: KTd[hm, :, k0:k0 + 2048],
                               lambda h, o, n, k0=k0: Vd[k0 + o:k0 + o + n, h * 256:(h + 1) * 256],
                               2048, kvbufs[4 * pid:4 * pid + 4])
            cfg = dict(tp=128, ntt=4, xsrc=xown[mstep * 512:(mstep + 1) * 512, :], phases="ACDEF", tails=tails_own, hT=hT_own,
                       hTb=hTb_own, init_hTb=True,
                       ktiles=ktiles, pieces=pieces, yout=y_p[mstep * 512:(mstep + 1) * 512, :])
            process_block(cfg)
        if stage > 5:
            write_state_outputs(conv_p, ssm_p)

        if sample:
            fw.barrier()
            skv = Buf(None, "skv")
            for kt in range(16):
                xl = xld[kt % 2]
                fw.dma("sp", xl[:, :], ck[kt * 128:(kt + 1) * 128, :], writes=[xl])
                fw.op("act", lambda e: e.copy(xn[:, :], xl[:, :]), reads=[xl], writes=[xn])
                kts = mixT
                for half in range(2):
                    def cp(pview, pbuf, half=half):
                        evac_alt(half, mixT[:, half * 8:(half + 1) * 8, 0:128], pview, [pbuf], [mixT])
                    transposes_to(mixT, cp, [xn[:, (half * 8 + j) * 128:(half * 8 + j + 1) * 128] for j in range(8)], [xn], 128, 128)
                fw.dma("pool", KTsd[:, :, kt * 128:(kt + 1) * 128].rearrange("a d t -> d a t"), mixT[:, 0:16, 0:128],
                       reads=[mixT], writes=[skv])
                xl2 = xld[(kt + 1) % 2]
                fw.dma("sp", xl2[:, :], cv[kt * 128:(kt + 1) * 128, :], writes=[xl2])
                fw.op("dve", lambda e: e.tensor_copy(xs_tok[:, 0, :], xl2[:, :]), reads=[xl2], writes=[xs_tok])
                fw.dma("pool", Vsd[kt * 128:(kt + 1) * 128, :], xs_tok[:, 0, :], reads=[xs_tok], writes=[skv])
            fw.dma("sp", tails[:], cconv, writes=[tails])
            enter_ssd_phase()
            for i in range(16):
                st = t1[i % 2]
                fw.dma("sp", st[:, 0:128], sst[i * 128:(i + 1) * 128, :], writes=[st])
                ps = next_ps()
                fw.group("pe", [lambda e: e.transpose(ps[:, 0:128], st[:, 0:128], idf[:, :])], reads=[st, idf], writes=[ps])
                evac_alt(i, hT[:, i * 128:(i + 1) * 128], ps[:, 0:128], [ps], [hT])
            fw.op("act", lambda e: e.copy(hTb[:, :], hT[:, :]), reads=[hT], writes=[hTb])
            ktiles = []
            for kt in range(16):
                ktiles.append((kt // 16, kt % 16, 128, (0 if kt == 15 else None), [0]))
            ktiles.append((1, 0, 32, 1, [0]))
            pieces = {
                0: (lambda hm: KTsd[hm, :, 0:2048], lambda h, o, n: Vsd[o:o + n, h * 256:(h + 1) * 256], 2048, [skv]),
                1: (lambda hm: KTsd[hm, :, 2048:2080], lambda h, o, n: Vsd[2048 + o:2048 + o + n, h * 256:(h + 1) * 256], 32, [skv]),
            }
            cfg = dict(tp=32, ntt=1, xsrc=xsm, KT=KTsd, V=Vsd, kout=k_s, vout=v_s, tok0=PAST, kvbuf=skv, ktiles=ktiles,
                       pieces=pieces, yout=y_s, use_bmd=True)
            process_block(cfg)
            write_state_outputs(conv_s, ssm_s)

        fw.barrier()
        print("instructions:", fw.ninstr)
    return nc


def host_consts():
    ident = np.eye(128, dtype=np.float32)
    triu = np.triu(np.ones((128, 128), np.float32))
    anti = np.ascontiguousarray(ident[::-1])
    idx = np.arange(1152)
    rel = (511 - idx).astype(np.int32)
    b = rel_bucket_np(rel)
    oh = np.zeros((32, 1152), np.float32)
    oh[b, idx] = 1.0
    mask = np.zeros((128, 5, 512), np.float32)
    k = np.arange(128)[:, None]
    q = np.arange(512)[None, :]
    for d in range(5):
        vis = ((128 * (d - 1) + k) // 64) <= (q // 64)
        mask[:, d, :] = np.where(vis, 0.0, NEG)
    return dict(c_ident=ident, c_triu=triu, c_anti=anti, c_oh=oh, c_mask=mask)


def make_in_maps(inp, NBLK=32):
    f = lambda a: np.ascontiguousarray(np.asarray(a, dtype=np.float32))
    SEQ = NBLK * 512
    consts = host_consts()
    common = dict(
        relb=f(inp["rel_bias"]),
        nw=f(np.asarray(inp["norm_w"])[0].reshape(16, 128).T),
        w_in=f(inp["w_in"][0]), w_out=f(inp["w_out"][0]),
        lamv=f(np.stack([np.asarray(inp["lambda_q1"])[0], np.asarray(inp["lambda_k1"])[0],
                         np.asarray(inp["lambda_q2"])[0], np.asarray(inp["lambda_k2"])[0]])),
        sublnT=f(np.asarray(inp["subln_w"])[0].reshape(2, 128).T),
        convw=f(np.asarray(inp["conv_w"])[0].reshape(4, 24, 128).transpose(2, 1, 0)),
        convb=f(np.asarray(inp["conv_b"])[0].reshape(24, 128).T),
        dtb=f(inp["dt_bias"]), alog=f(inp["A_log"]), dskip=f(inp["D_skip"]),
        ssmnwT=f(np.asarray(inp["ssm_norm_w"])[0].reshape(16, 128).T),
        fnw=f(np.asarray(inp["final_norm_w"]).reshape(1, 2048)),
        **consts)
    maps = []
    NM = NBLK // 4
    xpa = np.asarray(inp["x_prompt"])
    for c in range(8):
        b, j = c // 4, c % 4
        m = dict(common)
        m["xp"] = f(xpa[b, :SEQ])
        m["xown"] = f(xpa[b, :SEQ].reshape(NM, 4, 512, 2048)[:, j].reshape(NM * 512, 2048))
        sel = np.zeros((17, 7), np.float32)
        for ti in range(17):
            delta = 128 * (ti - 1) - 512 * j
            d = 5 if delta <= -256 else (6 if delta >= 512 else delta // 128 + 1)
            sel[ti, d] = 1.0
        m["selw"] = f(np.broadcast_to(sel.reshape(1, 119), (128, 119)))
        ws = np.zeros((1, 4), np.float32)
        ws[0, j] = 1.0
        m["wsel"] = f(np.broadcast_to(ws, (128, 4)))
        m["xsm"] = f(np.asarray(inp["x_sample"])[c])
        m["ck"] = f(np.asarray(inp["cache_k"])[0, c].reshape(PAST, 2048))
        m["cv"] = f(np.asarray(inp["cache_v"])[0, c].reshape(PAST, 2048))
        m["cconv"] = f(np.asarray(inp["cache_conv"])[0, c].reshape(3, 24, 128).transpose(2, 1, 0))
        m["sst"] = f(np.asarray(inp["state_ssm"])[0, c].reshape(2048, 128))
        maps.append(m)
    return maps


_NC_CACHE = {}


def run(inp, NBLK=32, sample=True, stage=99):
    key = (NBLK, sample, stage)
    if key not in _NC_CACHE:
        _NC_CACHE[key] = build(NBLK, sample, stage)
    nc = _NC_CACHE[key]
    maps = make_in_maps(inp, NBLK)
    res = run_bass_kernel_spmd(nc, maps, core_ids=list(range(8)))
    return res.results


def assemble_y(r, NBLK):
    NM = NBLK // 4
    y = np.zeros((2, NBLK * 512, 2048), np.float32)
    yv = y.reshape(2, NM, 4, 512, 2048)
    for c in range(8):
        b, j = c // 4, c % 4
        yv[b, :, j] = r[c]["y_p"].reshape(NM, 512, 2048)
    return y


def kernel(**inp):
    r = run(inp, 32, True)
    S = 16384
    y_prompt = assemble_y(r, 32)
    y_sample = np.stack([r[c]["y_s"] for c in range(8)]).reshape(8, 32, 2048)
    k_prompt = np.stack([r[4 * b]["k_p"] for b in range(2)]).reshape(1, 2, S, 8, 2, 128)
    v_prompt = np.stack([r[4 * b]["v_p"] for b in range(2)]).reshape(1, 2, S, 8, 256)
    conv_prompt = np.stack([r[4 * b]["conv_p"] for b in range(2)]).reshape(1, 2, 3, 3072)
    ssm_prompt = np.stack([r[4 * b]["ssm_p"] for b in range(2)]).reshape(1, 2, 32, 64, 128)
    k_sample = np.stack([r[c]["k_s"] for c in range(8)]).reshape(1, 8, 32, 8, 2, 128)
    v_sample = np.stack([r[c]["v_s"] for c in range(8)]).reshape(1, 8, 32, 8, 256)
    conv_sample = np.stack([r[c]["conv_s"] for c in range(8)]).reshape(1, 8, 3, 3072)
    ssm_sample = np.stack([r[c]["ssm_s"] for c in range(8)]).reshape(1, 8, 32, 64, 128)
    return tuple(np.ascontiguousarray(a.astype(np.float32)) for a in
                 (y_prompt, y_sample, k_prompt, v_prompt, conv_prompt, ssm_prompt, k_sample, v_sample, conv_sample, ssm_sample))
```
